# Optimizing a Trainium2 kernel written in Bass

```python
import math
import jax, jax.numpy as jnp
from jax import lax
import numpy as np


D_MODEL = 1024
BATCH = 4
SEQ = 8192
DEPTH = 1

CHUNK = 64
Q_BLOCK = 128
RMS_EPS = 1e-6
ROPE_THETA = 10000.0
MAX_STREAM_OFFSET_CHUNKS = 512

DA_HEADS = 8
DA_HEAD_DIM = 64
DA_QK_WIDTH = DA_HEADS * 2 * DA_HEAD_DIM
DA_V_DIM = 2 * DA_HEAD_DIM
DA_V_WIDTH = DA_HEADS * DA_V_DIM

MLA_HEADS = 8
MLA_Q_RANK = 384
MLA_KV_RANK = 256
MLA_NOPE_DIM = 128
MLA_ROPE_DIM = 64
MLA_V_DIM = 128
MLA_QK_DIM = MLA_NOPE_DIM + MLA_ROPE_DIM
MLA_V_WIDTH = MLA_HEADS * MLA_V_DIM

IN_SPLITS = (DA_QK_WIDTH, DA_QK_WIDTH, DA_V_WIDTH, MLA_Q_RANK, MLA_KV_RANK, MLA_ROPE_DIM,
             D_MODEL, D_MODEL)
IN_COLS = sum(IN_SPLITS)

D_FF = ((8 * D_MODEL // 3 + 255) // 256) * 256

kernel_name = "hybrid_diffattn_mla_gated_swiglu"


def rms_norm(x, w):
    xf = x.astype(jnp.float32)
    y = xf * lax.rsqrt(jnp.mean(xf * xf, axis=-1, keepdims=True) + RMS_EPS)
    return (y * w.astype(jnp.float32)).astype(x.dtype)


def rope(x, positions):
    d = x.shape[-1]
    half = d // 2
    inv_freq = 1.0 / (ROPE_THETA ** (jnp.arange(half, dtype=jnp.float32) * (2.0 / d)))
    ang = positions.astype(jnp.float32)[:, None, :, None] * inv_freq
    cos, sin = jnp.cos(ang), jnp.sin(ang)
    x1 = x[..., :half].astype(jnp.float32)
    x2 = x[..., half:].astype(jnp.float32)
    out = jnp.concatenate([x1 * cos - x2 * sin, x2 * cos + x1 * sin], axis=-1)
    return out.astype(x.dtype)


def split_heads(t, n_heads):
    b, s, w = t.shape
    return t.reshape(b, s, n_heads, w // n_heads).transpose(0, 2, 1, 3)


def merge_heads(t):
    b, h, s, d = t.shape
    return t.transpose(0, 2, 1, 3).reshape(b, s, h * d)


def chunk_causal_attention(q, k, v, scale):
    b, h, s, dk = q.shape
    dv = v.shape[-1]
    nb = s // Q_BLOCK
    q_blocks = q.reshape(b, h, nb, Q_BLOCK, dk).transpose(2, 0, 1, 3, 4)
    key_chunk = jnp.arange(s) // CHUNK

    def one_block(args):
        qi, i = args
        q_chunk = (i * Q_BLOCK + jnp.arange(Q_BLOCK)) // CHUNK
        mask = key_chunk[None, :] <= q_chunk[:, None]
        sc = jnp.einsum('bhqd,bhkd->bhqk', qi, k,
                        preferred_element_type=jnp.float32) * scale
        sc = jnp.where(mask, sc, jnp.float32(-1e30))
        p = jax.nn.softmax(sc, axis=-1).astype(v.dtype)
        return jnp.einsum('bhqk,bhkd->bhqd', p, v)

    out = lax.map(one_block, (q_blocks, jnp.arange(nb)))
    return out.transpose(1, 2, 0, 3, 4).reshape(b, h, s, dv)


def setup_inputs(seed: int = 0) -> dict:
    key = jax.random.key(seed)
    ks = jax.random.split(key, 32)

    def dense(k, shape):
        return jax.random.normal(k, (DEPTH,) + shape, jnp.float32) * (shape[0] ** -0.5)

    def gain(k, n):
        return 1.0 + 0.05 * jax.random.normal(k, (DEPTH, n), jnp.float32)

    x = jax.random.normal(ks[0], (BATCH, SEQ, D_MODEL), jnp.float32)
    offsets = jax.random.randint(ks[1], (BATCH, 1), 0, MAX_STREAM_OFFSET_CHUNKS) * CHUNK
    positions = (offsets + jnp.arange(SEQ, dtype=jnp.int32)[None, :]).astype(jnp.int32)

    return {
        "x": x,
        "positions": positions,
        "ln_mix_pre": gain(ks[2], D_MODEL),
        "w_in": dense(ks[3], (D_MODEL, IN_COLS)),
        "lambda_q1": 0.1 * jax.random.normal(ks[4], (DEPTH, DA_HEAD_DIM), jnp.float32),
        "lambda_k1": 0.1 * jax.random.normal(ks[5], (DEPTH, DA_HEAD_DIM), jnp.float32),
        "lambda_q2": 0.1 * jax.random.normal(ks[6], (DEPTH, DA_HEAD_DIM), jnp.float32),
        "lambda_k2": 0.1 * jax.random.normal(ks[7], (DEPTH, DA_HEAD_DIM), jnp.float32),
        "da_subln": gain(ks[8], DA_V_DIM),
        "q_a_norm": gain(ks[9], MLA_Q_RANK),
        "w_uq": dense(ks[10], (MLA_Q_RANK, MLA_HEADS * MLA_QK_DIM)),
        "kv_a_norm": gain(ks[11], MLA_KV_RANK),
        "w_ukv": dense(ks[12], (MLA_KV_RANK, MLA_HEADS * (MLA_NOPE_DIM + MLA_V_DIM))),
        "w_proj_a": dense(ks[13], (DA_V_WIDTH, D_MODEL)),
        "w_proj_b": dense(ks[14], (MLA_V_WIDTH, D_MODEL)),
        "w_o": dense(ks[15], (D_MODEL, D_MODEL)),
        "ln_mix_post": gain(ks[16], D_MODEL),
        "ln_ffn_pre": gain(ks[17], D_MODEL),
        "w_ffn_gate": dense(ks[18], (D_MODEL, D_FF)),
        "w_ffn_up": dense(ks[19], (D_MODEL, D_FF)),
        "w_ffn_down": dense(ks[20], (D_FF, D_MODEL)),
        "ln_ffn_post": gain(ks[21], D_MODEL),
    }


def reference(x, positions, ln_mix_pre, w_in, lambda_q1, lambda_k1, lambda_q2, lambda_k2,
              da_subln, q_a_norm, w_uq, kv_a_norm, w_ukv, w_proj_a, w_proj_b, w_o,
              ln_mix_post, ln_ffn_pre, w_ffn_gate, w_ffn_up, w_ffn_down, ln_ffn_post):
    bsz, seq, _ = x.shape
    offs = np.cumsum(IN_SPLITS)[:-1].tolist()
    for l in range(DEPTH):
        h = rms_norm(x, ln_mix_pre[l])
        z = h @ w_in[l]
        qa, ka, va, cq, ckv, k_rope, g_a, g_b = jnp.split(z, offs, axis=-1)

        lambda_init = 0.8 - 0.6 * math.exp(-0.3 * l)
        qa = split_heads(qa, DA_HEADS)
        ka = split_heads(ka, DA_HEADS)
        va = split_heads(va, DA_HEADS)
        q1 = rope(qa[..., :DA_HEAD_DIM], positions)
        q2 = rope(qa[..., DA_HEAD_DIM:], positions)
        k1 = rope(ka[..., :DA_HEAD_DIM], positions)
        k2 = rope(ka[..., DA_HEAD_DIM:], positions)
        da_scale = DA_HEAD_DIM ** -0.5
        o1 = chunk_causal_attention(q1, k1, va, da_scale)
        o2 = chunk_causal_attention(q2, k2, va, da_scale)
        lam = (jnp.exp(jnp.sum(lambda_q1[l].astype(jnp.float32) * lambda_k1[l].astype(jnp.float32)))
               - jnp.exp(jnp.sum(lambda_q2[l].astype(jnp.float32) * lambda_k2[l].astype(jnp.float32)))
               + lambda_init)
        oa = o1 - lam.astype(o1.dtype) * o2
        oa = rms_norm(oa, da_subln[l]) * (1.0 - lambda_init)
        y_a = merge_heads(oa) @ w_proj_a[l]

        cq = rms_norm(cq, q_a_norm[l])
        qb = split_heads(cq @ w_uq[l], MLA_HEADS)
        qb_nope = qb[..., :MLA_NOPE_DIM]
        qb_rope = rope(qb[..., MLA_NOPE_DIM:], positions)
        ckv = rms_norm(ckv, kv_a_norm[l])
        kv = split_heads(ckv @ w_ukv[l], MLA_HEADS)
        kb_nope = kv[..., :MLA_NOPE_DIM]
        vb = kv[..., MLA_NOPE_DIM:]
        kb_rope = rope(k_rope[:, None, :, :], positions)
        kb_rope = jnp.broadcast_to(kb_rope, (bsz, MLA_HEADS, seq, MLA_ROPE_DIM))
        qb_full = jnp.concatenate([qb_nope, qb_rope], axis=-1)
        kb_full = jnp.concatenate([kb_nope, kb_rope], axis=-1)
        ob = chunk_causal_attention(qb_full, kb_full, vb, MLA_QK_DIM ** -0.5)
        y_b = merge_heads(ob) @ w_proj_b[l]

        merged = jax.nn.sigmoid(g_a) * y_a + jax.nn.sigmoid(g_b) * y_b
        x = x + rms_norm(merged @ w_o[l], ln_mix_post[l])

        h = rms_norm(x, ln_ffn_pre[l])
        f = (jax.nn.silu(h @ w_ffn_gate[l]) * (h @ w_ffn_up[l])) @ w_ffn_down[l]
        x = x + rms_norm(f, ln_ffn_post[l])
    return x
```

```python
import numpy as np
import ml_dtypes
from contextlib import ExitStack
import concourse.bass as bass
import concourse.mybir as mybir
from concourse.bass_utils import run_bass_kernel_spmd

F32, BF16, I32, U8 = mybir.dt.float32, mybir.dt.bfloat16, mybir.dt.int32, mybir.dt.uint8
AF = mybir.ActivationFunctionType
ALU = mybir.AluOpType
ENGS = ("pe", "act", "dve", "pool", "sp")

D = 1024
H = 8
DFF = 2816
NF = DFF // 128
EPS = 1e-6
LAMBDA_INIT = 0.8 - 0.6 * 1.0
NEG = -30000.0
ARENA_BYTES = 204 * 1024
PI_LO = 3.1415925
MAGIC = 12582912.0


def _cw():
    p = 2 * np.pi
    c1 = np.float32(6.28125)
    c2 = np.float32(p - float(c1))
    c3 = np.float32(p - float(c1) - float(c2))
    return float(c1), float(c2), float(c3)


CW1, CW2, CW3 = _cw()


class Buf:
    __slots__ = ("w", "r", "x")

    def __init__(self, excl=False):
        self.w = None
        self.r = []
        self.x = excl


def bufs(n):
    return [Buf() for _ in range(n)]


class Op:
    __slots__ = ("eng", "fn", "waits", "signal", "count", "ch", "n")


class Sched:
    def __init__(self):
        self.ops = {e: [] for e in ENGS}
        self.ch_last = {}
        self.ch_cnt = {}
        self.last_compute = {e: None for e in ENGS}
        self.pending_barrier = {e: [] for e in ENGS}

    def _add(self, eng, fn, reads, writes, ch=None):
        op = Op()
        op.eng = eng
        op.fn = fn
        op.signal = False
        op.count = 0
        op.ch = ch
        op.n = 0
        deps = []
        raw = set()
        for b in reads:
            if b.w is not None:
                deps.append(b.w)
                raw.add(id(b.w))
            if b.x:
                deps.extend(r for r in b.r if r.eng != eng)
        for b in writes:
            if b.w is not None:
                deps.append(b.w)
            deps.extend(b.r)
        bar = self.pending_barrier[eng]
        bar_ids = set(id(d) for d in bar)
        deps.extend(bar)
        self.pending_barrier[eng] = []
        if ch is not None:
            prev = self.ch_last.get(ch)
            if prev is not None:
                deps.append(prev)
            self.ch_cnt[ch] = self.ch_cnt.get(ch, 0) + 1
            op.n = self.ch_cnt[ch]
            self.ch_last[ch] = op
        waits = []
        seen = set()
        for d in deps:
            if d is op or id(d) in seen:
                continue
            seen.add(id(d))
            if d.ch is None and ch is None and d.eng == eng and id(d) not in bar_ids:
                if eng == "pe":
                    continue
            if d.ch is None and ch is None and d.eng == eng and id(d) in bar_ids:
                continue
            d.signal = True
            waits.append(d)
        op.waits = waits
        for b in reads:
            b.r.append(op)
        for b in writes:
            b.w = op
            b.r = []
        self.ops[eng].append(op)
        if ch is None:
            self.last_compute[eng] = op
        return op

    def op(self, eng, fn, reads=(), writes=()):
        return self._add(eng, fn, reads, writes)

    def dma(self, q, ch, out, in_, reads=(), writes=()):
        return self._add(q, lambda e: e.dma_start(out=out, in_=in_), reads, writes, ch=ch)

    def barrier(self):
        deps = [o for o in self.last_compute.values() if o is not None]
        deps += list(self.ch_last.values())
        for e in ENGS:
            self.pending_barrier[e] = list(deps)

    def emit(self, nc, stack):
        for e in ENGS:
            c = 0
            for op in self.ops[e]:
                if op.ch is None and op.signal:
                    c += 1
                    op.count = c
        esem = {e: stack.enter_context(nc.semaphore("s_" + e)) for e in ENGS}
        chsem = {ch: stack.enter_context(nc.semaphore("c_" + str(ch))) for ch in self.ch_cnt}
        block = stack.enter_context(nc.Block())
        ops = self.ops
        ch_last = self.ch_last

        def run(e, eng):
            waited = {}
            for op in ops[e]:
                for d in op.waits:
                    if d.ch is not None:
                        key, sem, val = ("c", d.ch), chsem[d.ch], 16 * d.n
                    else:
                        key, sem, val = ("e", d.eng), esem[d.eng], d.count
                    if waited.get(key, 0) >= val:
                        continue
                    waited[key] = val
                    eng.wait_ge(sem, val)
                ins = op.fn(eng)
                if op.ch is not None:
                    ins.then_inc(chsem[op.ch], 16)
                elif op.signal:
                    ins.then_inc(esem[e], 1)
            for ch, last in ch_last.items():
                if last.eng == e:
                    eng.wait_ge(chsem[ch], 16 * last.n)

        @block.tensor
        def _(eng):
            run("pe", eng)

        @block.scalar
        def _(eng):
            run("act", eng)

        @block.vector
        def _(eng):
            run("dve", eng)

        @block.gpsimd
        def _(eng):
            run("pool", eng)

        @block.sync
        def _(eng):
            run("sp", eng)


class Arena:
    def __init__(self, nc, nbytes):
        self.t = nc.alloc_sbuf_tensor("arena", [128, nbytes], U8)
        self.nbytes = nbytes
        self.off = 0
        self.floor = 0

    def alloc(self, free_elems, dtype, parts=128):
        sz = 2 if dtype == BF16 else (1 if dtype == U8 else 4)
        nb = free_elems * sz
        nb_al = (nb + 63) // 64 * 64
        assert self.off + nb_al <= self.nbytes, f"SBUF arena overflow {self.off}+{nb_al}>{self.nbytes}"
        ap = self.t[0:parts, self.off:self.off + nb]
        self.off += nb_al
        if dtype != U8:
            ap = ap.bitcast(dtype)
        return ap

    def persist(self):
        self.floor = self.off

    def reset(self):
        self.off = self.floor


class Ctx:
    pass


def v3(ap, inner):
    return ap.rearrange("p (a b) -> p a b", b=inner)


def copy_op(C, eng, out, in_, reads, writes):
    S = C.S
    if eng == "act":
        S.op("act", lambda e: e.activation(out=out, in_=in_, func=AF.Copy), reads, writes)
    else:
        S.op(eng, lambda e: e.tensor_copy(out=out, in_=in_), reads, writes)


class Wt:
    def __init__(self, C, src, kcn, ncols, splits=None):
        self.ap = v3(C.A.alloc(kcn * ncols, BF16), ncols)
        self.src, self.kcn, self.ncols = src, kcn, ncols
        self.splits = splits or [(0, ncols)]
        self.bufs = [bufs(kcn) for _ in self.splits]

    def load(self, C, groups=None):
        for g in (range(len(self.splits)) if groups is None else groups):
            c0, c1 = self.splits[g]
            for kc in range(self.kcn):
                ch = "w%d" % (C.wctr % 4)
                C.wctr += 1
                C.S.dma("pool", ch, self.ap[:, kc, c0:c1], self.src[kc * 128:(kc + 1) * 128, c0:c1], writes=[self.bufs[g][kc]])

    def rd(self, col=0):
        for g, (c0, c1) in enumerate(self.splits):
            if c0 <= col < c1:
                return self.bufs[g]
        raise ValueError(col)

    def rd_all(self):
        return [b for g in self.bufs for b in g]


def norm_part(C, xsrc, row0, P):
    S = C.S
    for tt in range(4):
        slot = C.xctr % 2
        C.xctr += 1
        xs, bxs = P.xs[slot], P.bxs[slot]
        S.dma("sp", "x%d" % slot, xs, xsrc[row0 + tt * 128: row0 + (tt + 1) * 128, :], writes=[bxs])
        ss = P.ss[:, tt:tt + 1]
        rs = P.rs[:, tt:tt + 1]
        bss, brs = P.bss[tt], P.brs[tt]
        S.op("act", lambda e, xs=xs, ss=ss: e.activation(out=P.junk, in_=xs, func=AF.Square, accum_out=ss), [bxs], [P.bjunk, bss])
        S.op("act", lambda e, ss=ss, rs=rs: e.activation(out=rs, in_=ss, func=AF.Sqrt, scale=1.0 / D, bias=EPS), [bss], [brs])
        S.op("dve", lambda e, rs=rs: e.reciprocal(out=rs, in_=rs), [brs], [brs])
        hb = P.hb[tt]
        S.op("dve", lambda e, hb=hb, xs=xs, rs=rs: e.tensor_scalar(out=hb, in0=xs, scalar1=rs, scalar2=None, op0=ALU.mult), [bxs, brs], [P.bhb[tt]])


def transpose_part(C, P):
    S, ps, pb = C.S, C.ps, C.pb
    hT3 = P.hTs[P.hctr % 2]
    bhT_ = P.bhTs[P.hctr % 2]
    P.hctr += 1
    for tt in range(4):
        hb = P.hb[tt]
        tb = C.tbank[C.tctr % 2]
        C.tctr += 1
        pst = ps[:, tb * 512:(tb + 1) * 512].bitcast(BF16)

        def tr(e, hb=hb, pst=pst):
            for kc in range(8):
                last = e.transpose(pst[:, kc * 128:(kc + 1) * 128], hb[:, kc * 128:(kc + 1) * 128], C.ident)
            return last
        S.op("pe", tr, [P.bhb[tt], C.bconst], [pb[tb]])
        dstT = hT3[:, :, tt * 128:(tt + 1) * 128]
        S.op("dve", lambda e, dstT=dstT, pst=pst: e.tensor_tensor(out=dstT, in0=v3(pst, 128), in1=C.g_pre.unsqueeze(2).to_broadcast([128, 8, 128]), op=ALU.mult), [pb[tb], C.bconst], [bhT_])
    return hT3, bhT_


def rope_tables(C, pos_src, col0, P):
    S = C.S
    S.dma("sp", "pos", P.posi, pos_src[col0:col0 + 512].partition_broadcast(128), writes=[P.bposi])
    a, k, r, r2 = P.ang, P.kk, P.rr, P.r2
    S.op("dve", lambda e: e.tensor_copy(out=a, in_=P.posi), [P.bposi], [P.bang])
    S.op("dve", lambda e: e.tensor_scalar(out=a, in0=a, scalar1=C.invf, scalar2=None, op0=ALU.mult), [P.bang, C.bconst], [P.bang])
    S.op("dve", lambda e: e.tensor_scalar(out=k, in0=a, scalar1=float(1 / (2 * np.pi)), scalar2=MAGIC, op0=ALU.mult, op1=ALU.add), [P.bang], [P.bkk])
    S.op("dve", lambda e: e.tensor_single_scalar(out=k, in_=k, scalar=-MAGIC, op=ALU.add), [P.bkk], [P.bkk])
    S.op("dve", lambda e: e.scalar_tensor_tensor(out=r, in0=k, scalar=-CW1, in1=a, op0=ALU.mult, op1=ALU.add), [P.bkk, P.bang], [P.brr])
    S.op("dve", lambda e: e.scalar_tensor_tensor(out=r, in0=k, scalar=-CW2, in1=r, op0=ALU.mult, op1=ALU.add), [P.bkk, P.brr], [P.brr])
    S.op("dve", lambda e: e.scalar_tensor_tensor(out=r, in0=k, scalar=-CW3, in1=r, op0=ALU.mult, op1=ALU.add), [P.bkk, P.brr], [P.brr])
    S.op("dve", lambda e: e.tensor_scalar(out=r, in0=r, scalar1=-PI_LO, scalar2=PI_LO, op0=ALU.max, op1=ALU.min), [P.brr], [P.brr])
    S.op("dve", lambda e: e.scalar_tensor_tensor(out=r2, in0=r, scalar=-1.0, in1=r, op0=ALU.mult, op1=ALU.max), [P.brr], [P.br2])
    S.op("act", lambda e: e.activation(out=P.cs, in_=r2, func=AF.Sin, scale=-1.0, bias=float(np.pi / 2)), [P.br2], [P.bcs])
    S.op("act", lambda e: e.activation(out=P.sn, in_=r, func=AF.Sin, scale=C.sinsc), [P.brr, C.bconst], [P.bsn])


def mm_acc(C, bank, lhs_list, rhs_list, reads):
    n = len(lhs_list)

    def fn(e):
        for i in range(n):
            last = e.matmul(bank, lhsT=lhs_list[i], rhs=rhs_list[i], start=(i == 0), stop=(i == n - 1))
        return last
    return fn


def nextbank(C):
    b = C.mbanks[C.mctr % len(C.mbanks)]
    C.mctr += 1
    return b


def rope_apply(C, P, bx, bxs, out, bout, parts=128):
    S, ps, pb = C.S, C.ps, C.pb
    i = C.rctr % 2
    C.rctr += 1
    t1, t2 = P.rt1[i][0:parts], P.rt2[i][0:parts]
    b1, b2 = P.brt1[i], P.brt2[i]
    X = ps[0:parts, bx * 512:(bx + 1) * 512]
    Xs = ps[0:parts, bxs * 512:(bxs + 1) * 512]
    S.op("dve", lambda e: e.tensor_tensor(out=t1, in0=X, in1=P.cs[0:parts], op=ALU.mult), [pb[bx], P.bcs], [b1])
    S.op("dve", lambda e: e.tensor_tensor(out=t2, in0=Xs, in1=P.sn[0:parts], op=ALU.mult), [pb[bxs], P.bsn], [b2])
    S.op("pool", lambda e: e.tensor_tensor(out=out, in0=t1, in1=t2, op=ALU.add), [b1, b2], [bout])


class RopePipe:
    def __init__(self, C, P):
        self.C, self.P, self.pend = C, P, None
        A = C.A
        self.xb = [A.alloc(512, BF16) for _ in range(2)]
        self.bxb = bufs(2)
        self.k = 0

    def push(self, bx, out, bout, parts=128):
        C, S, ps, pb = self.C, self.C.S, self.C.ps, self.C.pb
        i = self.k % 2
        self.k += 1
        xb = self.xb[i][0:parts]
        X = ps[0:parts, bx * 512:(bx + 1) * 512]
        S.op("act", lambda e: e.activation(out=xb, in_=X, func=AF.Copy), [pb[bx]], [self.bxb[i]])
        prev = self.pend
        self.pend = (bx, xb, self.bxb[i], out, bout, parts)
        if prev is not None:
            self._finish(prev)

    def _finish(self, item):
        C, S, ps, pb = self.C, self.C.S, self.C.ps, self.C.pb
        bx, xb, bxb, out, bout, parts = item
        bxs = nextbank(C)
        Xs = ps[0:parts, bxs * 512:(bxs + 1) * 512]
        S.op("pe", lambda e: e.matmul(Xs, lhsT=C.perm[0:parts, 0:parts], rhs=xb, start=True, stop=True), [bxb, C.bconst], [pb[bxs]])
        rope_apply(C, self.P, bx, bxs, out, bout, parts)

    def flush(self):
        if self.pend is not None:
            self._finish(self.pend)
            self.pend = None


def feat_rmsnorm(C, P, src_cols, nch, wsb3, bw, hT3, bhT, outn3, boutn, n_feat, gain):
    S, ps, pb = C.S, C.ps, C.pb
    for c in range(nch):
        b = nextbank(C)
        bank = ps[:, b * 512:(b + 1) * 512]
        col = src_cols + c * 128
        S.op("pe", mm_acc(C, bank, [wsb3[:, kc, col:col + 128] for kc in range(8)], [hT3[:, kc, :] for kc in range(8)], None), bw + [bhT], [pb[b]])
        zf = P.zf3[:, c, :]
        sq = P.zsq3[:, c, :]
        S.op("act", lambda e, sq=sq, bank=bank: e.activation(out=sq, in_=bank, func=AF.Square), [pb[b]], [P.bzsq[c]])
        S.op("dve", lambda e, zf=zf, bank=bank: e.tensor_copy(out=zf, in_=bank), [pb[b]], [P.bzf[c]])
    b = nextbank(C)
    bank = ps[:, b * 512:(b + 1) * 512]
    S.op("pe", mm_acc(C, bank, [C.ones] * nch, [P.zsq3[:, c, :] for c in range(nch)], None), [C.bconst] + P.bzsq[:nch], [pb[b]])
    S.op("act", lambda e: e.activation(out=P.zrs, in_=bank, func=AF.Sqrt, scale=1.0 / n_feat, bias=EPS), [pb[b]], [P.bzrs])
    S.op("dve", lambda e: e.reciprocal(out=P.zrs, in_=P.zrs), [P.bzrs], [P.bzrs])
    for c in range(nch):
        zf = P.zf3[:, c, :]
        o = outn3[:, c, :]
        gc_ = gain[:, c:c + 1]
        S.op("dve", lambda e, zf=zf, o=o, gc_=gc_: e.scalar_tensor_tensor(out=o, in0=zf, scalar=gc_, in1=P.zrs, op0=ALU.mult, op1=ALU.mult), [P.bzf[c], P.bzrs, C.bconst], [boutn])


def alloc_norm_bufs(C, P):
    A = C.A
    P.xs = [A.alloc(D, F32) for _ in range(2)]
    P.bxs = bufs(2)
    P.hb = [A.alloc(D, BF16) for _ in range(4)]
    P.bhb = bufs(4)
    P.junk = A.alloc(D, BF16)
    P.bjunk = Buf()
    P.ss = A.alloc(4, F32)
    P.rs = A.alloc(4, F32)
    P.bss = bufs(4)
    P.brs = bufs(4)
    P.hTs = [v3(A.alloc(8 * 512, BF16), 512) for _ in range(2)]
    P.bhTs = bufs(2)
    P.hctr = 0


def alloc_rope_bufs(C, P):
    A = C.A
    P.posi = C.posi
    P.bposi = Buf()
    for n in ("ang", "kk", "rr", "r2", "cs", "sn"):
        setattr(P, n, A.alloc(512, F32))
        setattr(P, "b" + n, Buf())
    P.rt1 = [A.alloc(512, F32) for _ in range(2)]
    P.rt2 = [A.alloc(512, F32) for _ in range(2)]
    P.brt1 = bufs(2)
    P.brt2 = bufs(2)


def phase_A1(C, T):
    S, A, ps, pb = C.S, C.A, C.ps, C.pb
    NB = C.NB
    P = Ctx()
    NC1 = 2368
    W1 = Wt(C, T.w1, 8, NC1, [(0, 1024), (1024, 2048), (2048, 2368)])
    WK = Wt(C, T.wukv, 2, 2048)
    W1.load(C)
    WK.load(C)
    w13, wk3 = W1.ap, WK.ap
    bwk = WK.rd_all()
    alloc_norm_bufs(C, P)
    alloc_rope_bufs(C, P)
    Kst = A.alloc(8 * 512, BF16); bKst = Buf()
    KNst = A.alloc(8 * 512, BF16); bKNst = Buf()
    Vst = A.alloc(8 * 512, BF16); bVst = Buf()
    VBst = A.alloc(8 * 512, BF16); bVBst = Buf()
    KRst = A.alloc(512, BF16); bKRst = Buf()
    Kst3, KNst3 = v3(Kst, 512), v3(KNst, 512)
    Vst4 = Vst.rearrange("p (h t d) -> p h t d", h=8, t=4)
    VBst4 = VBst.rearrange("p (h t d) -> p h t d", h=8, t=4)
    P.zf3 = v3(A.alloc(2 * 512, F32), 512); P.bzf = bufs(2)
    P.zsq3 = v3(A.alloc(2 * 512, BF16), 512); P.bzsq = bufs(2)
    P.zrs = A.alloc(512, F32); P.bzrs = Buf()
    ckvn = A.alloc(2 * 512, BF16); bckvn = Buf()
    ckvn3 = v3(ckvn, 512)
    C.mbanks = [2, 3, 4, 5, 6, 7]
    C.tbank = [0, 1]
    RP = RopePipe(C, P)
    ev = 0
    import os
    stop = int(os.environ.get("A1STOP", "99"))
    if stop == 0:
        return
    norm_part(C, T.x_all, 0, P)
    nxt = transpose_part(C, P)
    rope_tables(C, T.pos_all, 0, P)
    for blk in range(NB):
        hT3, bhT = nxt
        hk = [hT3[:, kc, :] for kc in range(8)]
        for h in range(H):
            if h == 4 and blk + 1 < NB:
                norm_part(C, T.x_all, (blk + 1) * 512, P)
            bx = nextbank(C)
            S.op("pe", mm_acc(C, ps[:, bx * 512:(bx + 1) * 512], [w13[:, kc, h * 128:(h + 1) * 128] for kc in range(8)], hk, None), W1.rd(0) + [bhT], [pb[bx]])
            RP.push(bx, Kst3[:, h, :], bKst)
        bx = nextbank(C)
        S.op("pe", mm_acc(C, ps[0:64, bx * 512:(bx + 1) * 512], [w13[:, kc, 2304:2368] for kc in range(8)], hk, None), W1.rd(2048) + [bhT], [pb[bx]])
        RP.push(bx, KRst[0:64], bKRst, parts=64)
        RP.flush()
        S.dma("pool", "stK", T.KD.rearrange("h p s -> p h s")[:, :, blk * 512:(blk + 1) * 512], Kst3, reads=[bKst])
        S.dma("pool", "stKR", T.KR[:, blk * 512:(blk + 1) * 512], KRst[0:64], reads=[bKRst])
        if blk + 1 < NB:
            rope_tables(C, T.pos_all, (blk + 1) * 512, P)

        for tt in range(4):
            for half in range(2):
                b = nextbank(C)
                bank = ps[:, b * 512:(b + 1) * 512]
                S.op("pe", mm_acc(C, bank, [hT3[:, kc, tt * 128:(tt + 1) * 128] for kc in range(8)], [w13[:, kc, 1024 + half * 512:1024 + (half + 1) * 512] for kc in range(8)], None), W1.rd(1024) + [bhT], [pb[b]])
                ev += 1
                copy_op(C, "act" if ev % 2 else "dve", Vst4[:, half * 4:(half + 1) * 4, tt, :], v3(bank, 128), [pb[b]], [bVst])
        S.dma("pool", "stV", T.VD.rearrange("h p s -> p h s")[:, :, blk * 512:(blk + 1) * 512], v3(Vst, 512), reads=[bVst])
        if blk + 1 < NB:
            nxt = transpose_part(C, P)
        if stop == 4:
            continue
        feat_rmsnorm(C, P, 2048, 2, w13, W1.rd(2048), hT3, bhT, ckvn3, bckvn, 256, C.g_kva)
        if stop == 41:
            continue
        for h in range(H):
            b = nextbank(C)
            bank = ps[:, b * 512:(b + 1) * 512]
            S.op("pe", mm_acc(C, bank, [wk3[:, c, h * 128:(h + 1) * 128] for c in range(2)], [ckvn3[:, c, :] for c in range(2)], None), bwk + [bckvn], [pb[b]])
            ev += 1
            copy_op(C, "act" if ev % 2 else "dve", KNst3[:, h, :], bank, [pb[b]], [bKNst])
        S.dma("pool", "stKN", T.KN.rearrange("h p s -> p h s")[:, :, blk * 512:(blk + 1) * 512], KNst3, reads=[bKNst])
        if stop == 42:
            continue
        for tt in range(4):
            for half in range(2):
                b = nextbank(C)
                bank = ps[:, b * 512:(b + 1) * 512]
                S.op("pe", mm_acc(C, bank, [ckvn3[:, c, tt * 128:(tt + 1) * 128] for c in range(2)], [wk3[:, c, 1024 + half * 512:1024 + (half + 1) * 512] for c in range(2)], None), bwk + [bckvn], [pb[b]])
                ev += 1
                copy_op(C, "act" if ev % 2 else "dve", VBst4[:, half * 4:(half + 1) * 4, tt, :], v3(bank, 128), [pb[b]], [bVBst])
        S.dma("pool", "stVB", T.VB.rearrange("h p s -> p h s")[:, :, blk * 512:(blk + 1) * 512], v3(VBst, 512), reads=[bVBst])
        if stop == 5:
            continue


def phase_A2(C, T):
    S, A, ps, pb = C.S, C.A, C.ps, C.pb
    NQ = C.NQ
    P = Ctx()
    NC2 = 3456
    W2 = Wt(C, T.w2, 8, NC2, [(0, 1024), (1024, 1408), (1408, 3456)])
    WQ = Wt(C, T.wuq, 3, 1536)
    W2.load(C)
    WQ.load(C)
    w23, wq3 = W2.ap, WQ.ap
    bwq = WQ.rd_all()
    alloc_norm_bufs(C, P)
    alloc_rope_bufs(C, P)
    Qst = A.alloc(8 * 512, BF16); bQst = Buf()
    QNst = A.alloc(8 * 512, BF16); bQNst = Buf()
    QRst = A.alloc(4 * 512, BF16); bQRst = Buf()
    Gst = [A.alloc(8 * 512, BF16) for _ in range(2)]; bGst = bufs(2)
    Qst3, QNst3, QRst3 = v3(Qst, 512), v3(QNst, 512), v3(QRst, 512)
    P.zf3 = v3(A.alloc(3 * 512, F32), 512); P.bzf = bufs(3)
    P.zsq3 = v3(A.alloc(3 * 512, BF16), 512); P.bzsq = bufs(3)
    P.zrs = A.alloc(512, F32); P.bzrs = Buf()
    cqn = A.alloc(3 * 512, BF16); bcqn = Buf()
    cqn3 = v3(cqn, 512)
    C.mbanks = [2, 3, 4, 5, 6, 7]
    C.tbank = [0, 1]
    RP = RopePipe(C, P)
    ev = 0
    norm_part(C, T.x_own, 0, P)
    nxt = transpose_part(C, P)
    rope_tables(C, T.pos_own, 0, P)
    for j in range(NQ):
        hT3, bhT = nxt
        hk = [hT3[:, kc, :] for kc in range(8)]
        sl = slice(j * 512, (j + 1) * 512)
        for h in range(H):
            if h == 4 and j + 1 < NQ:
                norm_part(C, T.x_own, (j + 1) * 512, P)
            bx = nextbank(C)
            S.op("pe", mm_acc(C, ps[:, bx * 512:(bx + 1) * 512], [w23[:, kc, h * 128:(h + 1) * 128] for kc in range(8)], hk, None), W2.rd(0) + [bhT], [pb[bx]])
            RP.push(bx, Qst3[:, h, :], bQst)
        RP.flush()
        S.dma("pool", "stQ", T.QD.rearrange("h p s -> p h s")[:, :, sl], Qst3, reads=[bQst])
        feat_rmsnorm(C, P, 1024, 3, w23, W2.rd(1024), hT3, bhT, cqn3, bcqn, 384, C.g_qa)
        cq = [cqn3[:, c, :] for c in range(3)]
        for hp in range(4):
            bx = nextbank(C)
            S.op("pe", mm_acc(C, ps[:, bx * 512:(bx + 1) * 512], [wq3[:, c, 1024 + hp * 128:1024 + (hp + 1) * 128] for c in range(3)], cq, None), bwq + [bcqn], [pb[bx]])
            RP.push(bx, QRst3[:, hp, :], bQRst)
        RP.flush()
        S.dma("pool", "stQR", T.QR.rearrange("h r s -> (h r) s").rearrange("(a p) s -> p a s", p=128)[:, :, sl], QRst3, reads=[bQRst])
        if j + 1 < NQ:
            rope_tables(C, T.pos_own, (j + 1) * 512, P)
        for h in range(H):
            b = nextbank(C)
            bank = ps[:, b * 512:(b + 1) * 512]
            S.op("pe", mm_acc(C, bank, [wq3[:, c, h * 128:(h + 1) * 128] for c in range(3)], cq, None), bwq + [bcqn], [pb[b]])
            ev += 1
            copy_op(C, "act" if ev % 2 else "dve", QNst3[:, h, :], bank, [pb[b]], [bQNst])
        S.dma("pool", "stQN", T.QN.rearrange("h p s -> p h s")[:, :, sl], QNst3, reads=[bQNst])
        if j + 1 < NQ:
            nxt = transpose_part(C, P)
        for gi in range(2):
            g3 = v3(Gst[gi], 512)
            for c in range(8):
                b = nextbank(C)
                bank = ps[:, b * 512:(b + 1) * 512]
                col = 1408 + gi * 1024 + c * 128
                S.op("pe", mm_acc(C, bank, [w23[:, kc, col:col + 128] for kc in range(8)], hk, None), W2.rd(1408) + [bhT], [pb[b]])
                o = g3[:, c, :]
                S.op("act", lambda e, o=o, bank=bank: e.activation(out=o, in_=bank, func=AF.Sigmoid), [pb[b]], [bGst[gi]])
            S.dma("pool", "stG%d" % gi, (T.GA if gi == 0 else T.GB).rearrange("h p s -> p h s")[:, :, sl], g3, reads=[bGst[gi]])


def attention(C, T, mla):
    S, A, ps, pb = C.S, C.A, C.ps, C.pb
    NB, NQ, Sq, SQ = C.NB, C.NQ, C.S_, C.SQ
    scale = (192.0 if mla else 64.0) ** -0.5
    KT = A.alloc(Sq, BF16); bK = bufs(NB)
    V = A.alloc(Sq, BF16); bV = bufs(NB)
    QT = [A.alloc(SQ, BF16) for _ in range(2)]; bQ = bufs(2)
    if mla:
        KR = A.alloc(Sq, BF16); bKR = Buf()
        QR = [A.alloc(SQ, BF16) for _ in range(2)]; bQR = bufs(2)
        S.dma("sp", "ldKR", KR[0:64], T.KR, writes=[bKR])
        S.dma("sp", "ldKR", KR[64:128], T.KR, writes=[bKR])
    NP = 6
    Pb = [A.alloc(1024, BF16) for _ in range(NP)]; bP = bufs(NP)
    mA = A.alloc(4 * 512, BF16); mB = A.alloc(4 * 512, BF16); bM = Buf()
    S.dma("sp", "ldM", v3(mA, 512), T.maskA.rearrange("t p q -> p t q"), writes=[bM])
    S.dma("sp", "ldM", v3(mB, 512), T.maskB.rearrange("t p q -> p t q"), writes=[bM])
    mA3, mB3 = v3(mA, 512), v3(mB, 512)
    accs = [[A.alloc(1024, F32) for _ in range(2)] for _ in range(2)]; baccs = [bufs(2), bufs(2)]
    hi = [A.alloc(512, BF16) for _ in range(4)]; lo = [A.alloc(512, BF16) for _ in range(4)]
    bhl = bufs(4)
    r1 = A.alloc(512, F32); r2 = A.alloc(512, F32)
    br1, br2 = bufs(2)
    oe = [A.alloc(512, F32) for _ in range(3)]; boe = bufs(3)
    Ost = [A.alloc(512, BF16) for _ in range(2)]; bOst = bufs(2)
    Kdram = T.KN if mla else T.KD
    Vdram = T.VB if mla else T.VD
    Qdram = T.QN if mla else T.QD
    Odram = T.OB if mla else T.OA
    octr = 0
    pctr = 0
    sctr = 0
    bctr = 0
    SB = [(0, 1), (2, 3)]
    if mla:
        OB_ = [4, 5]
        L1 = 6
    else:
        O1, O2, L1, L2 = 4, 5, 6, 7

    def bk(i):
        return ps[:, i * 512:(i + 1) * 512]

    def load_head(h):
        for kb in range(NB):
            sl = slice(kb * 512, (kb + 1) * 512)
            S.dma("sp", "ldK%d" % (kb % 4), KT[:, sl], Kdram[h][:, sl], writes=[bK[kb]])
            S.dma("sp", "ldV%d" % (kb % 4), V[:, sl], Vdram[h][:, sl], writes=[bV[kb]])

    def load_q(h):
        s = h % 2
        S.dma("sp", "ldQ%d" % s, QT[s], Qdram[h], writes=[bQ[s]])
        if mla:
            S.dma("sp", "ldQR%d" % s, QR[s][0:64], T.QR[h], writes=[bQR[s]])
            S.dma("sp", "ldQR%d" % s, QR[s][64:128], T.QR[h], writes=[bQR[s]])

    load_q(0)
    pendq = []
    for h in range(H):
        if h + 1 < H:
            load_q(h + 1)
        load_head(h)
        qs = h % 2
        Q = QT[qs]
        units = []
        for j in range(NQ):
            nkb = 2 * j + 2
            tiles = [(kb, kt) for kb in range(nkb) for kt in range(4)]
            if mla:
                grp = [tiles[i:i + 2] for i in range(0, len(tiles), 2)]
            else:
                grp = [[t] for t in tiles]
            for gi, g in enumerate(grp):
                units.append((j, g, gi == 0, gi == len(grp) - 1))
        for (j, g, first, last) in units:
            if first:
                bctr += 1
            bsl = bctr % 2
            sb = SB[sctr % 2]
            sctr += 1
            pslot = pctr % NP
            pctr += 1
            qsl = slice(j * 512, (j + 1) * 512)
            rd = [bQ[qs], C.bconst, bM]
            if mla:
                rd += [bKR, bQR[qs]]
            for (kb, kt) in g:
                rd.append(bK[kb])

            def qk(e, g=g, sb=sb, j=j, qsl=qsl, Q=Q, qs=qs):
                last_i = None
                kss = [slice(kb * 512 + kt * 128, kb * 512 + (kt + 1) * 128) for (kb, kt) in g]
                masked = g[0][0] >= 2 * j
                m3 = mA3 if g[0][0] == 2 * j else mB3
                if mla:
                    e.matmul(bk(sb[0]), lhsT=KT[:, kss[0]], rhs=Q[:, qsl], start=True, stop=False)
                    e.matmul(bk(sb[1]), lhsT=KT[:, kss[1]], rhs=Q[:, qsl], start=True, stop=False)
                    e.matmul(bk(sb[0]), lhsT=KR[0:64, kss[0]], rhs=QR[qs][0:64, qsl], start=False, stop=not masked)
                    last_i = e.matmul(bk(sb[1]), lhsT=KR[64:128, kss[1]], rhs=QR[qs][64:128, qsl], start=False, stop=not masked)
                    if masked:
                        e.matmul(bk(sb[0]), lhsT=C.ident, rhs=m3[:, g[0][1], :], start=False, stop=True)
                        last_i = e.matmul(bk(sb[1]), lhsT=C.ident, rhs=m3[:, g[1][1], :], start=False, stop=True)
                else:
                    e.matmul(bk(sb[0]), lhsT=KT[0:64, kss[0]], rhs=Q[0:64, qsl], start=True, stop=not masked)
                    last_i = e.matmul(bk(sb[1]), lhsT=KT[64:128, kss[0]], rhs=Q[64:128, qsl], start=True, stop=not masked)
                    if masked:
                        e.matmul(bk(sb[0]), lhsT=C.ident, rhs=m3[:, g[0][1], :], start=False, stop=True)
                        last_i = e.matmul(bk(sb[1]), lhsT=C.ident, rhs=m3[:, g[0][1], :], start=False, stop=True)
                return last_i
            S.op("pe", qk, rd, [pb[sb[0]], pb[sb[1]]])
            Pt = Pb[pslot]
            src = ps[:, sb[0] * 512:(sb[0] + 2) * 512]
            S.op("act", lambda e, Pt=Pt, src=src: e.activation(out=Pt, in_=src, func=AF.Exp, scale=scale), [pb[sb[0]], pb[sb[1]]], [bP[pslot]])
            if first:
                uidx = 0
            else:
                uidx += 1
            if mla:
                jobs = [(uidx % 2, Pt, accs[bsl][uidx % 2], uidx < 2)]
            else:
                jobs = [(0, Pt[:, 0:512], accs[bsl][0][:, 0:512], first)]
                if uidx % 3 == 2:
                    jobs.append((1, Pt[:, 512:1024], accs[bsl][1][:, 0:512], uidx == 2))
            for (ch_, ph, ac, init) in jobs:
                bb_ = baccs[bsl][ch_]
                if init:
                    S.op("dve", lambda e, ac=ac, ph=ph: e.tensor_copy(out=ac, in_=ph), [bP[pslot]], [bb_])
                else:
                    S.op("dve", lambda e, ac=ac, ph=ph: e.tensor_tensor(out=ac, in0=ac, in1=ph, op=ALU.add), [bP[pslot], bb_], [bb_])
            if len(pendq) >= 2:
                pendq.pop(0)()

            def pv_rec(g=g, first=first, last=last, Pt=Pt, pslot=pslot, j=j, bsl=bsl, h=h, uidx=uidx):
                nonlocal octr
                rdv = [bP[pslot], C.bconst] + [bV[kb] for (kb, kt) in g]
                kss = [slice(kb * 512 + kt * 128, kb * 512 + (kt + 1) * 128) for (kb, kt) in g]
                if mla:
                    Ob = OB_[bsl]

                    def pv(e):
                        e.matmul(bk(Ob), lhsT=V[:, kss[0]], rhs=Pt[:, 0:512], start=first, stop=False)
                        return e.matmul(bk(Ob), lhsT=V[:, kss[1]], rhs=Pt[:, 512:1024], start=False, stop=last)
                    wr = [pb[Ob]]
                else:
                    def pv(e):
                        e.matmul(bk(O1), lhsT=V[:, kss[0]], rhs=Pt[:, 0:512], start=first, stop=last)
                        li = e.matmul(bk(O2), lhsT=V[:, kss[0]], rhs=Pt[:, 512:1024], start=first, stop=last)
                        if uidx % 3 != 2:
                            li = e.matmul(bk(L2), lhsT=C.ones, rhs=Pt[:, 512:1024], start=first, stop=False)
                        return li
                    wr = [pb[O1], pb[O2], pb[L2]]
                S.op("pe", pv, rdv, wr)
                if last:
                    os_ = octr % 2
                    octr += 1
                    ot = Ost[os_]
                    qsl2 = slice(j * 512, (j + 1) * 512)
                    acw = accs[bsl]
                    if not mla:
                        for k_, bnk in enumerate((O1, O2)):
                            S.op("act", lambda e, k_=k_, bnk=bnk: e.activation(out=oe[k_], in_=bk(bnk), func=AF.Copy), [pb[bnk]], [boe[k_]])
                    if mla:
                        parts_ = [(0, accs[bsl][0][:, 0:512]), (0, accs[bsl][0][:, 512:1024]), (1, accs[bsl][1][:, 0:512]), (1, accs[bsl][1][:, 512:1024])]
                    else:
                        parts_ = [(0, accs[bsl][0][:, 0:512]), (1, accs[bsl][1][:, 0:512])]
                    for k_, (ch_, ac) in enumerate(parts_):
                        hh, ll = hi[k_], lo[k_]
                        S.op("pool", lambda e, ac=ac, hh=hh: e.tensor_copy(out=hh, in_=ac), [baccs[bsl][ch_]], [bhl[k_]])
                        S.op("pool", lambda e, ac=ac, hh=hh, ll=ll: e.tensor_tensor(out=ll, in0=ac, in1=hh, op=ALU.subtract), [baccs[bsl][ch_], bhl[k_]], [bhl[k_]])
                    if mla:
                        def lsum(e):
                            for k_ in range(4):
                                e.matmul(bk(L1), lhsT=C.ones, rhs=hi[k_], start=(k_ == 0), stop=False)
                                li = e.matmul(bk(L1), lhsT=C.ones, rhs=lo[k_], start=False, stop=(k_ == 3))
                            return li
                        S.op("pe", lsum, bhl + [C.bconst], [pb[L1]])
                        S.op("act", lambda e: e.activation(out=r1, in_=bk(L1), func=AF.Ln), [pb[L1]], [br1])
                        S.op("act", lambda e: e.activation(out=r1, in_=r1, func=AF.Exp, scale=-1.0), [br1], [br1])
                        S.op("dve", lambda e: e.tensor_tensor(out=ot, in0=bk(Ob), in1=r1, op=ALU.mult), [pb[Ob], br1], [bOst[os_]])
                    else:
                        def lsum(e):
                            e.matmul(bk(L1), lhsT=C.ones, rhs=hi[0], start=True, stop=False)
                            e.matmul(bk(L1), lhsT=C.ones, rhs=lo[0], start=False, stop=True)
                            e.matmul(bk(L2), lhsT=C.ones, rhs=hi[1], start=False, stop=False)
                            return e.matmul(bk(L2), lhsT=C.ones, rhs=lo[1], start=False, stop=True)
                        S.op("pe", lsum, [bhl[0], bhl[1], C.bconst], [pb[L1], pb[L2]])
                        S.op("act", lambda e: e.activation(out=oe[2], in_=bk(L2), func=AF.Copy), [pb[L2]], [boe[2]])
                        S.op("dve", lambda e: e.reciprocal(out=r1, in_=bk(L1)), [pb[L1]], [br1])
                        S.op("dve", lambda e: e.reciprocal(out=r2, in_=oe[2]), [boe[2]], [br2])
                        S.op("pool", lambda e: e.tensor_tensor(out=oe[0], in0=oe[0], in1=r1, op=ALU.mult), [boe[0], br1], [boe[0]])
                        S.op("pool", lambda e: e.tensor_tensor(out=oe[1], in0=oe[1], in1=r2, op=ALU.mult), [boe[1], br2], [boe[1]])
                        S.op("dve", lambda e: e.scalar_tensor_tensor(out=ot, in0=oe[1], scalar=C.neglam, in1=oe[0], op0=ALU.mult, op1=ALU.add), [boe[0], boe[1], C.bconst], [bOst[os_]])
                    S.dma("pool", "stO%d" % os_, Odram[h][:, qsl2], ot, reads=[bOst[os_]])
            pendq.append(pv_rec)
        while pendq:
            pendq.pop(0)()


def phase_C1(C, T):
    S, A, ps, pb = C.S, C.A, C.ps, C.pb
    NQ = C.NQ
    WPA = Wt(C, T.wpa, 8, 1024)
    WPB = Wt(C, T.wpb, 8, 1024)
    WO = Wt(C, T.wo, 8, 1024)
    WPA.load(C)
    WPB.load(C)
    WO.load(C)
    wpa, wpb, wo = WPA.ap, WPB.ap, WO.ap
    bwpa, bwpb, bwo = WPA.rd_all(), WPB.rd_all(), WO.rd_all()
    gpost = A.alloc(D, F32); bgpost = Buf()
    S.dma("sp", "ldg", gpost, T.g_post.partition_broadcast(128), writes=[bgpost])
    oa = [v3(A.alloc(8 * 512, BF16), 512) for _ in range(2)]; boa = bufs(2)
    ob = [v3(A.alloc(8 * 512, BF16), 512) for _ in range(2)]; bob = bufs(2)
    ga = [v3(A.alloc(8 * 512, BF16), 512) for _ in range(2)]; bga = bufs(2)
    gb = [v3(A.alloc(8 * 512, BF16), 512) for _ in range(2)]; bgb = bufs(2)
    oans = [v3(A.alloc(8 * 512, BF16), 512) for _ in range(2)]; boans = [bufs(8), bufs(8)]
    sq = [A.alloc(512, BF16) for _ in range(2)]; bsq = bufs(2)
    rsn = [A.alloc(512, F32) for _ in range(2)]; brsn = bufs(2)
    m1 = [A.alloc(512, F32) for _ in range(2)]; bm1 = bufs(2)
    m2 = [A.alloc(512, F32) for _ in range(2)]; bm2 = bufs(2)
    mT = v3(A.alloc(8 * 512, BF16), 512); bmT = bufs(8)
    xs = [A.alloc(D, F32) for _ in range(2)]; bxs = bufs(2)
    tt_ = [A.alloc(D, F32) for _ in range(2)]; btt = bufs(2)
    junk = A.alloc(D, BF16); bjunk = Buf()
    ss = A.alloc(8, F32); bss = bufs(8)
    C.mbanks = [0, 1, 2, 3]
    upairs = [(4, 5), (6, 7)]
    uc = 0
    xc = 0

    def loads(j):
        s = j % 2
        sl = slice(j * 512, (j + 1) * 512)
        S.dma("sp", "ldoa%d" % s, oa[s], T.OA.rearrange("h p s -> p h s")[:, :, sl], writes=[boa[s]])
        S.dma("sp", "ldob%d" % s, ob[s], T.OB.rearrange("h p s -> p h s")[:, :, sl], writes=[bob[s]])
        S.dma("sp", "ldga%d" % s, ga[s], T.GA.rearrange("h p s -> p h s")[:, :, sl], writes=[bga[s]])
        S.dma("sp", "ldgb%d" % s, gb[s], T.GB.rearrange("h p s -> p h s")[:, :, sl], writes=[bgb[s]])

    def subln(j):
        s = j % 2
        oan, boan = oans[s], boans[s]
        for h in range(H):
            i = h % 2
            src = oa[s][:, h, :]
            S.op("pool", lambda e, i=i, src=src: e.tensor_tensor(out=sq[i], in0=src, in1=src, op=ALU.mult), [boa[s]], [bsq[i]])
            b = nextbank(C)
            bank = ps[:, b * 512:(b + 1) * 512]
            S.op("pe", mm_acc(C, bank, [C.ones], [sq[i]], None), [C.bconst, bsq[i]], [pb[b]])
            S.op("act", lambda e, i=i, bank=bank: e.activation(out=rsn[i], in_=bank, func=AF.Ln, scale=1.0 / 128, bias=EPS), [pb[b]], [brsn[i]])
            S.op("act", lambda e, i=i: e.activation(out=rsn[i], in_=rsn[i], func=AF.Exp, scale=-0.5), [brsn[i]], [brsn[i]])
            dst = oan[:, h, :]
            S.op("dve", lambda e, i=i, src=src, dst=dst: e.scalar_tensor_tensor(out=dst, in0=src, scalar=C.gsub, in1=rsn[i], op0=ALU.mult, op1=ALU.mult), [boa[s], brsn[i], C.bconst], [boan[h]])

    loads(0)
    subln(0)
    for j in range(NQ):
        if j + 1 < NQ:
            loads(j + 1)
        s = j % 2
        oan, boan = oans[s], boans[s]
        for c in range(8):
            i = c % 2
            ba = nextbank(C)
            bka = ps[:, ba * 512:(ba + 1) * 512]
            S.op("pe", mm_acc(C, bka, [wpa[:, h, c * 128:(c + 1) * 128] for h in range(H)], [oan[:, h, :] for h in range(H)], None), bwpa + boan, [pb[ba]])
            bb = nextbank(C)
            bkb = ps[:, bb * 512:(bb + 1) * 512]
            S.op("pe", mm_acc(C, bkb, [wpb[:, h, c * 128:(c + 1) * 128] for h in range(H)], [ob[s][:, h, :] for h in range(H)], None), bwpb + [bob[s]], [pb[bb]])
            gac = ga[s][:, c, :]
            gbc = gb[s][:, c, :]
            S.op("dve", lambda e, i=i, bka=bka, gac=gac: e.tensor_tensor(out=m1[i], in0=bka, in1=gac, op=ALU.mult), [pb[ba], bga[s]], [bm1[i]])
            S.op("dve", lambda e, i=i, bkb=bkb, gbc=gbc: e.tensor_tensor(out=m2[i], in0=bkb, in1=gbc, op=ALU.mult), [pb[bb], bgb[s]], [bm2[i]])
            dst = mT[:, c, :]
            S.op("pool", lambda e, i=i, dst=dst: e.tensor_tensor(out=dst, in0=m1[i], in1=m2[i], op=ALU.add), [bm1[i], bm2[i]], [bmT[c]])
        if j + 1 < NQ:
            subln(j + 1)
        for tt in range(4):
            up = upairs[uc % 2]
            uc += 1
            xi = xc % 2
            xc += 1
            row0 = j * 512 + tt * 128
            S.dma("sp", "ldx%d" % xi, xs[xi], T.x_own[row0:row0 + 128, :], writes=[bxs[xi]])

            def mmu(e, up=up, tt=tt):
                for half in range(2):
                    bank = ps[:, up[half] * 512:(up[half] + 1) * 512]
                    for c in range(8):
                        last = e.matmul(bank, lhsT=mT[:, c, tt * 128:(tt + 1) * 128], rhs=wo[:, c, half * 512:(half + 1) * 512], start=(c == 0), stop=(c == 7))
                return last
            S.op("pe", mmu, bwo + bmT, [pb[up[0]], pb[up[1]]])
            u = ps[:, up[0] * 512:(up[0] + 2) * 512]
            si = (j * 4 + tt) % 8
            ssc = ss[:, si:si + 1]
            S.op("act", lambda e, u=u, ssc=ssc: e.activation(out=junk, in_=u, func=AF.Square, accum_out=ssc), [pb[up[0]], pb[up[1]]], [bjunk, bss[si]])
            S.op("act", lambda e, ssc=ssc: e.activation(out=ssc, in_=ssc, func=AF.Sqrt, scale=1.0 / D, bias=EPS), [bss[si]], [bss[si]])
            S.op("dve", lambda e, ssc=ssc: e.reciprocal(out=ssc, in_=ssc), [bss[si]], [bss[si]])
            t = tt_[xi]
            S.op("dve", lambda e, t=t, u=u, ssc=ssc: e.scalar_tensor_tensor(out=t, in0=u, scalar=ssc, in1=gpost, op0=ALU.mult, op1=ALU.mult), [pb[up[0]], pb[up[1]], bss[si], bgpost], [btt[xi]])
            x_ = xs[xi]
            S.op("pool", lambda e, t=t, x_=x_: e.tensor_tensor(out=t, in0=t, in1=x_, op=ALU.add), [btt[xi], bxs[xi]], [btt[xi]])
            S.dma("pool", "stx%d" % xi, T.out[row0:row0 + 128, :], t, reads=[btt[xi]])


def phase_C2(C, T):
    S, A, ps, pb = C.S, C.A, C.ps, C.pb
    NQ = C.NQ
    fsp = [(0, 768), (768, 1536), (1536, 2176), (2176, 2816)]
    WG = Wt(C, T.wg, 8, DFF, fsp)
    WU = Wt(C, T.wu, 8, DFF, fsp)
    WD = Wt(C, T.wd, NF, 1024)
    for g in range(4):
        WG.load(C, [g])
        WU.load(C, [g])
    WD.load(C)
    wg, wu, wd = WG.ap, WU.ap, WD.ap
    bwd = WD.rd_all()
    aT = v3(A.alloc(NF * 512, BF16), 512); baT = bufs(NF)
    gpost = A.alloc(D, F32); bgpost = Buf()
    S.dma("sp", "ldg", gpost, T.g_fpost.partition_broadcast(128), writes=[bgpost])
    x1 = [A.alloc(D, F32) for _ in range(4)]; bx1 = bufs(4)
    hb = [A.alloc(D, BF16) for _ in range(2)]; bhb = bufs(2)
    hT = A.alloc(8 * 512, BF16); bhT = Buf()
    hT3 = v3(hT, 512)
    junk = A.alloc(D, BF16); bjunk = Buf()
    ss = A.alloc(8, F32); bss = bufs(8)
    sg = [A.alloc(512, F32) for _ in range(2)]; bsg = bufs(2)
    ot = [A.alloc(D, F32) for _ in range(2)]; bot = bufs(2)
    tb = [0, 1]
    tc = 0
    gub = [(2, 3), (4, 5)]
    gc = 0
    dpairs = [(6, 7), (0, 1)]
    dc = 0
    oc = 0
    for j in range(NQ):
        for tt in range(4):
            row0 = j * 512 + tt * 128
            S.dma("sp", "ldx1_%d" % tt, x1[tt], T.out[row0:row0 + 128, :], writes=[bx1[tt]])
            si = tt
            ssc = ss[:, si:si + 1]
            xx = x1[tt]
            S.op("act", lambda e, xx=xx, ssc=ssc: e.activation(out=junk, in_=xx, func=AF.Square, accum_out=ssc), [bx1[tt]], [bjunk, bss[si]])
            S.op("act", lambda e, ssc=ssc: e.activation(out=ssc, in_=ssc, func=AF.Sqrt, scale=1.0 / D, bias=EPS), [bss[si]], [bss[si]])
            S.op("dve", lambda e, ssc=ssc: e.reciprocal(out=ssc, in_=ssc), [bss[si]], [bss[si]])
            hi = tt % 2
            hbi = hb[hi]
            S.op("dve", lambda e, hbi=hbi, xx=xx, ssc=ssc: e.tensor_scalar(out=hbi, in0=xx, scalar1=ssc, scalar2=None, op0=ALU.mult), [bx1[tt], bss[si]], [bhb[hi]])
            t_b = tb[tc % 2]
            tc += 1
            pst = ps[:, t_b * 512:(t_b + 1) * 512].bitcast(BF16)

            def tr(e, hbi=hbi, pst=pst):
                for kc in range(8):
                    last = e.transpose(pst[:, kc * 128:(kc + 1) * 128], hbi[:, kc * 128:(kc + 1) * 128], C.ident)
                return last
            S.op("pe", tr, [bhb[hi], C.bconst], [pb[t_b]])
            dstT = hT3[:, :, tt * 128:(tt + 1) * 128]
            S.op("dve", lambda e, dstT=dstT, pst=pst: e.tensor_tensor(out=dstT, in0=v3(pst, 128), in1=C.g_fpre.unsqueeze(2).to_broadcast([128, 8, 128]), op=ALU.mult), [pb[t_b], C.bconst], [bhT])
        hk = [hT3[:, kc, :] for kc in range(8)]
        for f in range(NF):
            gu = gub[gc % 2]
            gc += 1
            G = ps[:, gu[0] * 512:(gu[0] + 1) * 512]
            U = ps[:, gu[1] * 512:(gu[1] + 1) * 512]
            S.op("pe", mm_acc(C, G, [wg[:, kc, f * 128:(f + 1) * 128] for kc in range(8)], hk, None), WG.rd(f * 128) + [bhT], [pb[gu[0]]])
            S.op("pe", mm_acc(C, U, [wu[:, kc, f * 128:(f + 1) * 128] for kc in range(8)], hk, None), WU.rd(f * 128) + [bhT], [pb[gu[1]]])
            i = f % 2
            S.op("act", lambda e, i=i, G=G: e.activation(out=sg[i], in_=G, func=AF.Silu), [pb[gu[0]]], [bsg[i]])
            dst = aT[:, f, :]
            S.op("dve", lambda e, i=i, U=U, dst=dst: e.tensor_tensor(out=dst, in0=U, in1=sg[i], op=ALU.mult), [pb[gu[1]], bsg[i]], [baT[f]])
        for tt in range(4):
            dp = dpairs[dc % 2]
            dc += 1

            def mmd(e, dp=dp, tt=tt):
                for half in range(2):
                    bank = ps[:, dp[half] * 512:(dp[half] + 1) * 512]
                    for f in range(NF):
                        last = e.matmul(bank, lhsT=aT[:, f, tt * 128:(tt + 1) * 128], rhs=wd[:, f, half * 512:(half + 1) * 512], start=(f == 0), stop=(f == NF - 1))
                return last
            S.op("pe", mmd, bwd + baT, [pb[dp[0]], pb[dp[1]]])
            u = ps[:, dp[0] * 512:(dp[0] + 2) * 512] if dp[1] == dp[0] + 1 else None
            si = 4 + tt
            ssc = ss[:, si:si + 1]
            S.op("act", lambda e, u=u, ssc=ssc: e.activation(out=junk, in_=u, func=AF.Square, accum_out=ssc), [pb[dp[0]], pb[dp[1]]], [bjunk, bss[si]])
            S.op("act", lambda e, ssc=ssc: e.activation(out=ssc, in_=ssc, func=AF.Sqrt, scale=1.0 / D, bias=EPS), [bss[si]], [bss[si]])
            S.op("dve", lambda e, ssc=ssc: e.reciprocal(out=ssc, in_=ssc), [bss[si]], [bss[si]])
            oi = oc % 2
            oc += 1
            o_ = ot[oi]
            S.op("dve", lambda e, o_=o_, u=u, ssc=ssc: e.scalar_tensor_tensor(out=o_, in0=u, scalar=ssc, in1=gpost, op0=ALU.mult, op1=ALU.mult), [pb[dp[0]], pb[dp[1]], bss[si], bgpost], [bot[oi]])
            xx = x1[tt]
            S.op("pool", lambda e, o_=o_, xx=xx: e.tensor_tensor(out=o_, in0=o_, in1=xx, op=ALU.add), [bot[oi], bx1[tt]], [bot[oi]])
            row0 = j * 512 + tt * 128
            S.dma("pool", "sto%d" % oi, T.out[row0:row0 + 128, :], o_, reads=[bot[oi]])


def build(Sq, phases=("A1", "A2", "BD", "BM", "C1", "C2"), dbg=False):
    nc = bass.Bass("TRN2", target_bir_lowering=False)
    NB = Sq // 512
    NQ = NB // 2
    SQ = Sq // 2
    T = Ctx()

    def din(name, shape, dt=F32):
        return nc.dram_tensor(name, shape, dt, kind="ExternalInput").ap()

    def scr(name, shape, dt=BF16):
        return nc.dram_tensor(name, shape, dt, kind="ExternalOutput" if dbg else "Internal").ap()

    T.x_all = din("x_all", [Sq, D]); T.x_own = din("x_own", [SQ, D])
    T.pos_all = din("pos_all", [Sq], I32); T.pos_own = din("pos_own", [SQ], I32)
    T.w1 = din("w1", [D, 2368]); T.w2 = din("w2", [D, 3456])
    T.wuq = din("wuq", [384, 1536]); T.wukv = din("wukv", [256, 2048])
    T.wpa = din("wpa", [D, D]); T.wpb = din("wpb", [D, D]); T.wo = din("wo", [D, D])
    T.wg = din("wg", [D, DFF]); T.wu = din("wu", [D, DFF]); T.wd = din("wd", [DFF, D])
    T.cst = din("cst", [128, 32])
    T.permm = din("permm", [128, 128], BF16)
    T.lam4 = din("lam4", [4, 64])
    T.g_post = din("g_post", [D]); T.g_fpost = din("g_fpost", [D])
    T.maskA = din("maskA", [4, 128, 512], BF16); T.maskB = din("maskB", [4, 128, 512], BF16)
    T.out = nc.dram_tensor("out", [SQ, D], F32, kind="ExternalOutput").ap()
    T.KD = scr("KD", [H, 128, Sq]); T.VD = scr("VD", [H, 128, Sq])
    T.KN = scr("KN", [H, 128, Sq]); T.VB = scr("VB", [H, 128, Sq]); T.KR = scr("KR", [64, Sq])
    T.QD = scr("QD", [H, 128, SQ]); T.QN = scr("QN", [H, 128, SQ]); T.QR = scr("QR", [H, 64, SQ])
    T.GA = scr("GA", [H, 128, SQ]); T.GB = scr("GB", [H, 128, SQ])
    T.OA = scr("OA", [H, 128, SQ]); T.OB = scr("OB", [H, 128, SQ])

    C = Ctx()
    C.nc = nc
    C.S = Sched()
    C.A = Arena(nc, ARENA_BYTES)
    C.ps = nc.alloc_psum_tensor("ps", [128, 4096], F32)
    C.pb = [Buf(excl=True) for _ in range(8)]
    C.NB, C.NQ, C.S_, C.SQ = NB, NQ, Sq, SQ
    C.xctr = C.tctr = C.mctr = C.rctr = C.wctr = 0
    C.bout = Buf()
    C.posi = nc.alloc_sbuf_tensor("posi", [128, 512], I32)[:, :]
    S, A = C.S, C.A
    C.bconst = Buf()
    cst = A.alloc(32, F32)
    S.dma("sp", "cst", cst, T.cst, writes=[C.bconst])
    C.invf = cst[:, 0:1]
    C.sinsc = cst[:, 1:2]
    C.g_pre = cst[:, 4:12]
    C.g_qa = cst[:, 12:15]
    C.g_kva = cst[:, 15:17]
    C.g_fpre = cst[:, 17:25]
    small = A.alloc(8, F32)
    C.gsub = small[:, 0:1]
    C.neglam = small[:, 1:2]
    S.op("dve", lambda e: e.tensor_scalar(out=C.gsub, in0=cst[:, 2:3], scalar1=float(1.0 - LAMBDA_INIT), scalar2=None, op0=ALU.mult), [C.bconst], [C.bconst])
    lam = A.alloc(4 * 64, F32)
    S.dma("sp", "cst", lam, T.lam4.rearrange("a b -> (a b)").partition_broadcast(128), writes=[C.bconst])
    lp = A.alloc(128, F32)
    S.op("dve", lambda e: e.tensor_tensor(out=lp[:, 0:64], in0=lam[:, 0:64], in1=lam[:, 64:128], op=ALU.mult), [C.bconst], [C.bconst])
    S.op("dve", lambda e: e.tensor_tensor(out=lp[:, 64:128], in0=lam[:, 128:192], in1=lam[:, 192:256], op=ALU.mult), [C.bconst], [C.bconst])
    S.op("dve", lambda e: e.tensor_reduce(out=small[:, 2:3], in_=lp[:, 0:64], axis=mybir.AxisListType.X, op=ALU.add), [C.bconst], [C.bconst])
    S.op("dve", lambda e: e.tensor_reduce(out=small[:, 3:4], in_=lp[:, 64:128], axis=mybir.AxisListType.X, op=ALU.add), [C.bconst], [C.bconst])
    S.op("act", lambda e: e.activation(out=small[:, 4:6], in_=small[:, 2:4], func=AF.Exp), [C.bconst], [C.bconst])
    S.op("dve", lambda e: e.tensor_tensor(out=small[:, 6:7], in0=small[:, 5:6], in1=small[:, 4:5], op=ALU.subtract), [C.bconst], [C.bconst])
    S.op("dve", lambda e: e.tensor_single_scalar(out=C.neglam, in_=small[:, 6:7], scalar=-float(LAMBDA_INIT), op=ALU.add), [C.bconst], [C.bconst])
    C.perm = A.alloc(128, BF16)
    S.dma("sp", "cst", C.perm, T.permm, writes=[C.bconst])
    idf = A.alloc(128, F32)
    C.ident = A.alloc(128, BF16)
    C.ones = A.alloc(128, BF16)
    S.op("pool", lambda e: e.memset(idf, 1.0), [], [C.bconst])
    S.op("pool", lambda e: e.memset(C.ones, 1.0), [], [C.bconst])
    S.op("pool", lambda e: e.affine_select(out=idf, in_=idf, pattern=[[-1, 128]], compare_op=ALU.is_equal, fill=0.0, base=0, channel_multiplier=1), [C.bconst], [C.bconst])
    S.op("dve", lambda e: e.tensor_copy(out=C.ident, in_=idf), [C.bconst], [C.bconst])
    A.persist()
    S.barrier()
    for ph in phases:
        A.reset()
        if ph == "A1":
            phase_A1(C, T)
        elif ph == "A2":
            phase_A2(C, T)
        elif ph == "BD":
            attention(C, T, mla=False)
        elif ph == "BM":
            attention(C, T, mla=True)
        elif ph == "C1":
            phase_C1(C, T)
        elif ph == "C2":
            phase_C2(C, T)
        S.barrier()
    with ExitStack() as st:
        S.emit(nc, st)
    return nc


def _swap_idx(n_groups64):
    idx = []
    for g in range(n_groups64):
        b = g * 64
        idx += list(range(b + 32, b + 64)) + list(range(b, b + 32))
    return np.array(idx)


def host_prep(inputs, Sq):
    f32 = np.float32
    x = np.asarray(inputs["x"], f32)
    pos = np.asarray(inputs["positions"], np.int32)
    w_in = np.asarray(inputs["w_in"], f32)[0]
    qa, ka, va = w_in[:, 0:1024], w_in[:, 1024:2048], w_in[:, 2048:3072]
    cq, ckv, kr = w_in[:, 3072:3456], w_in[:, 3456:3712], w_in[:, 3712:3776]
    gA, gB = w_in[:, 3776:4800], w_in[:, 4800:5824]
    w1 = np.ascontiguousarray(np.concatenate([ka, va, ckv, kr], axis=1))
    w2 = np.ascontiguousarray(np.concatenate([qa, cq, gA, gB], axis=1))
    w_uq = np.asarray(inputs["w_uq"], f32)[0].reshape(384, 8, 192)
    uq_n = w_uq[:, :, 0:128].reshape(384, 1024)
    uq_r = w_uq[:, :, 128:192].reshape(384, 512)
    wuq = np.ascontiguousarray(np.concatenate([uq_n, uq_r], axis=1))
    w_ukv = np.asarray(inputs["w_ukv"], f32)[0].reshape(256, 8, 256)
    wukv = np.ascontiguousarray(np.concatenate([w_ukv[:, :, 0:128].reshape(256, 1024), w_ukv[:, :, 128:256].reshape(256, 1024)], axis=1))
    cst = np.zeros((128, 32), f32)
    invf = (1.0 / (f32(10000.0) ** (np.arange(32, dtype=f32) * f32(2.0 / 64)))).astype(f32)
    pidx = np.arange(128)
    cst[:, 0] = invf[pidx % 32]
    cst[:, 1] = np.where((pidx % 64) < 32, -1.0, 1.0)
    cst[:, 2] = np.asarray(inputs["da_subln"], f32)[0]
    cst[:, 4:12] = np.asarray(inputs["ln_mix_pre"], f32)[0].reshape(8, 128).T
    cst[:, 12:15] = np.asarray(inputs["q_a_norm"], f32)[0].reshape(3, 128).T
    cst[:, 15:17] = np.asarray(inputs["kv_a_norm"], f32)[0].reshape(2, 128).T
    cst[:, 17:25] = np.asarray(inputs["ln_ffn_pre"], f32)[0].reshape(8, 128).T
    lam4 = np.ascontiguousarray(np.stack([np.asarray(inputs[k], f32)[0] for k in ("lambda_q1", "lambda_k1", "lambda_q2", "lambda_k2")]))
    kk = np.arange(512)[:, None] // 64
    qq = np.arange(512)[None, :] // 64
    diag = np.where(kk <= qq, 0.0, NEG).astype(f32).reshape(4, 128, 512)
    zeros = np.zeros((4, 128, 512), f32)
    full = np.full((4, 128, 512), NEG, f32)
    bf = ml_dtypes.bfloat16
    bf = ml_dtypes.bfloat16
    permm = np.zeros((128, 128), f32)
    permm[_swap_idx(2), np.arange(128)] = 1.0
    shared = dict(
        permm=permm.astype(bf),
        w1=w1, w2=w2, wuq=wuq, wukv=wukv,
        wpa=np.ascontiguousarray(np.asarray(inputs["w_proj_a"], f32)[0]),
        wpb=np.ascontiguousarray(np.asarray(inputs["w_proj_b"], f32)[0]),
        wo=np.ascontiguousarray(np.asarray(inputs["w_o"], f32)[0]),
        wg=np.ascontiguousarray(np.asarray(inputs["w_ffn_gate"], f32)[0]),
        wu=np.ascontiguousarray(np.asarray(inputs["w_ffn_up"], f32)[0]),
        wd=np.ascontiguousarray(np.asarray(inputs["w_ffn_down"], f32)[0]),
        cst=cst, lam4=lam4,
        g_post=np.ascontiguousarray(np.asarray(inputs["ln_mix_post"], f32)[0]),
        g_fpost=np.ascontiguousarray(np.asarray(inputs["ln_ffn_post"], f32)[0]),
    )
    B = x.shape[0]
    NB = Sq // 512
    maps = []
    for c in range(2 * B):
        b, p = c // 2, c % 2
        m = dict(shared)
        m["x_all"] = np.ascontiguousarray(x[b])
        m["x_own"] = np.ascontiguousarray(x[b].reshape(NB, 512, D)[p::2].reshape(Sq // 2, D))
        m["pos_all"] = np.ascontiguousarray(pos[b])
        m["pos_own"] = np.ascontiguousarray(pos[b].reshape(NB, 512)[p::2].reshape(Sq // 2))
        m["maskA"] = (diag if p == 0 else zeros).astype(bf)
        m["maskB"] = (full if p == 0 else diag).astype(bf)
        maps.append(m)
    return maps


_NC_CACHE = {}


def kernel(**inputs):
    x = np.asarray(inputs["x"])
    B, Sq, _ = x.shape
    maps = host_prep(inputs, Sq)
    if Sq not in _NC_CACHE:
        _NC_CACHE[Sq] = build(Sq)
    nc = _NC_CACHE[Sq]
    res = run_bass_kernel_spmd(nc, maps, core_ids=list(range(2 * B)))
    NB = Sq // 512
    out = np.empty((B, NB, 512, D), np.float32)
    for c in range(2 * B):
        b, p = c // 2, c % 2
        out[b, p::2] = np.asarray(res.results[c]["out"], np.float32).reshape(NB // 2, 512, D)
    return out.reshape(B, Sq, D)
```

```python
import numpy as np
import ml_dtypes
from contextlib import ExitStack
import concourse.bass as bass
import concourse.mybir as mybir
from concourse.bass_utils import run_bass_kernel_spmd

F32, BF16, I32, U8 = mybir.dt.float32, mybir.dt.bfloat16, mybir.dt.int32, mybir.dt.uint8
AF = mybir.ActivationFunctionType
ALU = mybir.AluOpType
ENGS = ("pe", "act", "dve", "pool", "sp")

D = 1024
H = 8
DFF = 2816
NF = DFF // 128
EPS = 1e-6
LAMBDA_INIT = 0.8 - 0.6 * 1.0
NEG = -30000.0
ARENA_BYTES = 204 * 1024
PI_LO = 3.1415925
MAGIC = 12582912.0


def _cw():
    p = 2 * np.pi
    c1 = np.float32(6.28125)
    c2 = np.float32(p - float(c1))
    c3 = np.float32(p - float(c1) - float(c2))
    return float(c1), float(c2), float(c3)


CW1, CW2, CW3 = _cw()


class Buf:
    __slots__ = ("w", "r", "x")

    def __init__(self, excl=False):
        self.w = None
        self.r = []
        self.x = excl


def bufs(n):
    return [Buf() for _ in range(n)]


class Op:
    __slots__ = ("eng", "fn", "waits", "signal", "count", "ch", "n")


class Sched:
    def __init__(self):
        self.ops = {e: [] for e in ENGS}
        self.ch_last = {}
        self.ch_cnt = {}
        self.last_compute = {e: None for e in ENGS}
        self.pending_barrier = {e: [] for e in ENGS}

    def _add(self, eng, fn, reads, writes, ch=None):
        op = Op()
        op.eng = eng
        op.fn = fn
        op.signal = False
        op.count = 0
        op.ch = ch
        op.n = 0
        deps = []
        raw = set()
        for b in reads:
            if b.w is not None:
                deps.append(b.w)
                raw.add(id(b.w))
            if b.x:
                deps.extend(r for r in b.r if r.eng != eng)
        for b in writes:
            if b.w is not None:
                deps.append(b.w)
            deps.extend(b.r)
        bar = self.pending_barrier[eng]
        bar_ids = set(id(d) for d in bar)
        deps.extend(bar)
        self.pending_barrier[eng] = []
        if ch is not None:
            prev = self.ch_last.get(ch)
            if prev is not None:
                deps.append(prev)
            self.ch_cnt[ch] = self.ch_cnt.get(ch, 0) + 1
            op.n = self.ch_cnt[ch]
            self.ch_last[ch] = op
        waits = []
        seen = set()
        for d in deps:
            if d is op or id(d) in seen:
                continue
            seen.add(id(d))
            if d.ch is None and ch is None and d.eng == eng and id(d) not in bar_ids:
                if eng == "pe":
                    continue
            if d.ch is None and ch is None and d.eng == eng and id(d) in bar_ids:
                continue
            d.signal = True
            waits.append(d)
        op.waits = waits
        for b in reads:
            b.r.append(op)
        for b in writes:
            b.w = op
            b.r = []
        self.ops[eng].append(op)
        if ch is None:
            self.last_compute[eng] = op
        return op

    def op(self, eng, fn, reads=(), writes=()):
        return self._add(eng, fn, reads, writes)

    def dma(self, q, ch, out, in_, reads=(), writes=()):
        return self._add(q, lambda e: e.dma_start(out=out, in_=in_), reads, writes, ch=ch)

    def barrier(self):
        deps = [o for o in self.last_compute.values() if o is not None]
        deps += list(self.ch_last.values())
        for e in ENGS:
            self.pending_barrier[e] = list(deps)

    def emit(self, nc, stack):
        for e in ENGS:
            c = 0
            for op in self.ops[e]:
                if op.ch is None and op.signal:
                    c += 1
                    op.count = c
        esem = {e: stack.enter_context(nc.semaphore("s_" + e)) for e in ENGS}
        chsem = {ch: stack.enter_context(nc.semaphore("c_" + str(ch))) for ch in self.ch_cnt}
        block = stack.enter_context(nc.Block())
        ops = self.ops
        ch_last = self.ch_last

        def run(e, eng):
            waited = {}
            for op in ops[e]:
                for d in op.waits:
                    if d.ch is not None:
                        key, sem, val = ("c", d.ch), chsem[d.ch], 16 * d.n
                    else:
                        key, sem, val = ("e", d.eng), esem[d.eng], d.count
                    if waited.get(key, 0) >= val:
                        continue
                    waited[key] = val
                    eng.wait_ge(sem, val)
                ins = op.fn(eng)
                if op.ch is not None:
                    ins.then_inc(chsem[op.ch], 16)
                elif op.signal:
                    ins.then_inc(esem[e], 1)
            for ch, last in ch_last.items():
                if last.eng == e:
                    eng.wait_ge(chsem[ch], 16 * last.n)

        @block.tensor
        def _(eng):
            run("pe", eng)

        @block.scalar
        def _(eng):
            run("act", eng)

        @block.vector
        def _(eng):
            run("dve", eng)

        @block.gpsimd
        def _(eng):
            run("pool", eng)

        @block.sync
        def _(eng):
            run("sp", eng)


class Arena:
    def __init__(self, nc, nbytes):
        self.t = nc.alloc_sbuf_tensor("arena", [128, nbytes], U8)
        self.nbytes = nbytes
        self.off = 0
        self.floor = 0

    def alloc(self, free_elems, dtype, parts=128):
        sz = 2 if dtype == BF16 else (1 if dtype == U8 else 4)
        nb = free_elems * sz
        nb_al = (nb + 63) // 64 * 64
        assert self.off + nb_al <= self.nbytes, f"SBUF arena overflow {self.off}+{nb_al}>{self.nbytes}"
        ap = self.t[0:parts, self.off:self.off + nb]
        self.off += nb_al
        if dtype != U8:
            ap = ap.bitcast(dtype)
        return ap

    def persist(self):
        self.floor = self.off

    def reset(self):
        self.off = self.floor


class Ctx:
    pass


def v3(ap, inner):
    return ap.rearrange("p (a b) -> p a b", b=inner)


def copy_op(C, eng, out, in_, reads, writes):
    S = C.S
    if eng == "act":
        S.op("act", lambda e: e.activation(out=out, in_=in_, func=AF.Copy), reads, writes)
    else:
        S.op(eng, lambda e: e.tensor_copy(out=out, in_=in_), reads, writes)


class Wt:
    def __init__(self, C, src, kcn, ncols, splits=None):
        self.ap = v3(C.A.alloc(kcn * ncols, BF16), ncols)
        self.src, self.kcn, self.ncols = src, kcn, ncols
        self.splits = splits or [(0, ncols)]
        self.bufs = [bufs(kcn) for _ in self.splits]

    def load(self, C, groups=None):
        for g in (range(len(self.splits)) if groups is None else groups):
            c0, c1 = self.splits[g]
            for kc in range(self.kcn):
                ch = "w%d" % (C.wctr % 4)
                C.wctr += 1
                C.S.dma("pool", ch, self.ap[:, kc, c0:c1], self.src[kc * 128:(kc + 1) * 128, c0:c1], writes=[self.bufs[g][kc]])

    def rd(self, col=0):
        for g, (c0, c1) in enumerate(self.splits):
            if c0 <= col < c1:
                return self.bufs[g]
        raise ValueError(col)

    def rd_all(self):
        return [b for g in self.bufs for b in g]


def norm_part(C, xsrc, row0, P):
    S = C.S
    for tt in range(4):
        slot = C.xctr % 2
        C.xctr += 1
        xs, bxs = P.xs[slot], P.bxs[slot]
        S.dma("sp", "x%d" % slot, xs, xsrc[row0 + tt * 128: row0 + (tt + 1) * 128, :], writes=[bxs])
        ss = P.ss[:, tt:tt + 1]
        rs = P.rs[:, tt:tt + 1]
        bss, brs = P.bss[tt], P.brs[tt]
        S.op("act", lambda e, xs=xs, ss=ss: e.activation(out=P.junk, in_=xs, func=AF.Square, accum_out=ss), [bxs], [P.bjunk, bss])
        S.op("act", lambda e, ss=ss, rs=rs: e.activation(out=rs, in_=ss, func=AF.Sqrt, scale=1.0 / D, bias=EPS), [bss], [brs])
        S.op("dve", lambda e, rs=rs: e.reciprocal(out=rs, in_=rs), [brs], [brs])
        hb = P.hb[tt]
        S.op("dve", lambda e, hb=hb, xs=xs, rs=rs: e.tensor_scalar(out=hb, in0=xs, scalar1=rs, scalar2=None, op0=ALU.mult), [bxs, brs], [P.bhb[tt]])


def transpose_part(C, P):
    S, ps, pb = C.S, C.ps, C.pb
    hT3 = P.hTs[P.hctr % 2]
    bhT_ = P.bhTs[P.hctr % 2]
    P.hctr += 1
    for tt in range(4):
        hb = P.hb[tt]
        tb = C.tbank[C.tctr % 2]
        C.tctr += 1
        pst = ps[:, tb * 512:(tb + 1) * 512].bitcast(BF16)

        def tr(e, hb=hb, pst=pst):
            for kc in range(8):
                last = e.transpose(pst[:, kc * 128:(kc + 1) * 128], hb[:, kc * 128:(kc + 1) * 128], C.ident)
            return last
        S.op("pe", tr, [P.bhb[tt], C.bconst], [pb[tb]])
        dstT = hT3[:, :, tt * 128:(tt + 1) * 128]
        S.op("dve", lambda e, dstT=dstT, pst=pst: e.tensor_tensor(out=dstT, in0=v3(pst, 128), in1=C.g_pre.unsqueeze(2).to_broadcast([128, 8, 128]), op=ALU.mult), [pb[tb], C.bconst], [bhT_])
    return hT3, bhT_


def rope_tables(C, pos_src, col0, P):
    S = C.S
    S.dma("sp", "pos", P.posi, pos_src[col0:col0 + 512].partition_broadcast(128), writes=[P.bposi])
    a, k, r, r2 = P.ang, P.kk, P.rr, P.r2
    S.op("dve", lambda e: e.tensor_copy(out=a, in_=P.posi), [P.bposi], [P.bang])
    S.op("dve", lambda e: e.tensor_scalar(out=a, in0=a, scalar1=C.invf, scalar2=None, op0=ALU.mult), [P.bang, C.bconst], [P.bang])
    S.op("dve", lambda e: e.tensor_scalar(out=k, in0=a, scalar1=float(1 / (2 * np.pi)), scalar2=MAGIC, op0=ALU.mult, op1=ALU.add), [P.bang], [P.bkk])
    S.op("dve", lambda e: e.tensor_single_scalar(out=k, in_=k, scalar=-MAGIC, op=ALU.add), [P.bkk], [P.bkk])
    S.op("dve", lambda e: e.scalar_tensor_tensor(out=r, in0=k, scalar=-CW1, in1=a, op0=ALU.mult, op1=ALU.add), [P.bkk, P.bang], [P.brr])
    S.op("dve", lambda e: e.scalar_tensor_tensor(out=r, in0=k, scalar=-CW2, in1=r, op0=ALU.mult, op1=ALU.add), [P.bkk, P.brr], [P.brr])
    S.op("dve", lambda e: e.scalar_tensor_tensor(out=r, in0=k, scalar=-CW3, in1=r, op0=ALU.mult, op1=ALU.add), [P.bkk, P.brr], [P.brr])
    S.op("dve", lambda e: e.tensor_scalar(out=r, in0=r, scalar1=-PI_LO, scalar2=PI_LO, op0=ALU.max, op1=ALU.min), [P.brr], [P.brr])
    S.op("dve", lambda e: e.scalar_tensor_tensor(out=r2, in0=r, scalar=-1.0, in1=r, op0=ALU.mult, op1=ALU.max), [P.brr], [P.br2])
    S.op("act", lambda e: e.activation(out=P.cs, in_=r2, func=AF.Sin, scale=-1.0, bias=float(np.pi / 2)), [P.br2], [P.bcs])
    S.op("act", lambda e: e.activation(out=P.sn, in_=r, func=AF.Sin, scale=C.sinsc), [P.brr, C.bconst], [P.bsn])


def mm_acc(C, bank, lhs_list, rhs_list, reads):
    n = len(lhs_list)

    def fn(e):
        for i in range(n):
            last = e.matmul(bank, lhsT=lhs_list[i], rhs=rhs_list[i], start=(i == 0), stop=(i == n - 1))
        return last
    return fn


def nextbank(C):
    b = C.mbanks[C.mctr % len(C.mbanks)]
    C.mctr += 1
    return b


def rope_apply(C, P, bx, bxs, out, bout, parts=128):
    S, ps, pb = C.S, C.ps, C.pb
    i = C.rctr % 2
    C.rctr += 1
    t1, t2 = P.rt1[i][0:parts], P.rt2[i][0:parts]
    b1, b2 = P.brt1[i], P.brt2[i]
    X = ps[0:parts, bx * 512:(bx + 1) * 512]
    Xs = ps[0:parts, bxs * 512:(bxs + 1) * 512]
    S.op("dve", lambda e: e.tensor_tensor(out=t1, in0=X, in1=P.cs[0:parts], op=ALU.mult), [pb[bx], P.bcs], [b1])
    S.op("dve", lambda e: e.tensor_tensor(out=t2, in0=Xs, in1=P.sn[0:parts], op=ALU.mult), [pb[bxs], P.bsn], [b2])
    S.op("pool", lambda e: e.tensor_tensor(out=out, in0=t1, in1=t2, op=ALU.add), [b1, b2], [bout])


class RopePipe:
    def __init__(self, C, P):
        self.C, self.P, self.pend = C, P, None
        A = C.A
        self.xb = [A.alloc(512, BF16) for _ in range(3)]
        self.bxb = bufs(3)
        self.xf = [A.alloc(512, F32) for _ in range(3)]
        self.bxf = bufs(3)
        self.k = 0

    def push(self, bx, out, bout, parts=128):
        C, S, ps, pb = self.C, self.C.S, self.C.ps, self.C.pb
        i = self.k % 3
        self.k += 1
        xb = self.xb[i][0:parts]
        xf = self.xf[i][0:parts]
        X = ps[0:parts, bx * 512:(bx + 1) * 512]
        S.op("act", lambda e: e.activation(out=xb, in_=X, func=AF.Copy), [pb[bx]], [self.bxb[i]])
        S.op("act", lambda e: e.activation(out=xf, in_=X, func=AF.Copy), [pb[bx]], [self.bxf[i]])
        prev = self.pend
        self.pend = (xf, self.bxf[i], xb, self.bxb[i], out, bout, parts)
        if prev is not None:
            self._finish(prev)

    def _finish(self, item):
        C, S, ps, pb = self.C, self.C.S, self.C.ps, self.C.pb
        xf, bxf, xb, bxb, out, bout, parts = item
        P = self.P
        bxs = nextbank(C)
        Xs = ps[0:parts, bxs * 512:(bxs + 1) * 512]
        S.op("pe", lambda e: e.matmul(Xs, lhsT=C.perm[0:parts, 0:parts], rhs=xb, start=True, stop=True), [bxb, C.bconst], [pb[bxs]])
        i = C.rctr % 2
        C.rctr += 1
        t1, t2 = P.rt1[i][0:parts], P.rt2[i][0:parts]
        b1, b2 = P.brt1[i], P.brt2[i]
        S.op("dve", lambda e: e.tensor_tensor(out=t1, in0=xf, in1=P.cs[0:parts], op=ALU.mult), [bxf, P.bcs], [b1])
        S.op("dve", lambda e: e.tensor_tensor(out=t2, in0=Xs, in1=P.sn[0:parts], op=ALU.mult), [pb[bxs], P.bsn], [b2])
        S.op("pool", lambda e: e.tensor_tensor(out=out, in0=t1, in1=t2, op=ALU.add), [b1, b2], [bout])

    def flush(self):
        if self.pend is not None:
            self._finish(self.pend)
            self.pend = None


def feat_rmsnorm(C, P, src_cols, nch, wsb3, bw, hT3, bhT, outn3, boutn, n_feat, gain):
    S, ps, pb = C.S, C.ps, C.pb
    for c in range(nch):
        b = nextbank(C)
        bank = ps[:, b * 512:(b + 1) * 512]
        col = src_cols + c * 128
        S.op("pe", mm_acc(C, bank, [wsb3[:, kc, col:col + 128] for kc in range(8)], [hT3[:, kc, :] for kc in range(8)], None), bw + [bhT], [pb[b]])
        zf = P.zf3[:, c, :]
        sq = P.zsq3[:, c, :]
        S.op("act", lambda e, sq=sq, bank=bank: e.activation(out=sq, in_=bank, func=AF.Square), [pb[b]], [P.bzsq[c]])
        S.op("dve", lambda e, zf=zf, bank=bank: e.tensor_copy(out=zf, in_=bank), [pb[b]], [P.bzf[c]])
    b = nextbank(C)
    bank = ps[:, b * 512:(b + 1) * 512]
    S.op("pe", mm_acc(C, bank, [C.ones] * nch, [P.zsq3[:, c, :] for c in range(nch)], None), [C.bconst] + P.bzsq[:nch], [pb[b]])
    S.op("act", lambda e: e.activation(out=P.zrs, in_=bank, func=AF.Sqrt, scale=1.0 / n_feat, bias=EPS), [pb[b]], [P.bzrs])
    S.op("dve", lambda e: e.reciprocal(out=P.zrs, in_=P.zrs), [P.bzrs], [P.bzrs])
    for c in range(nch):
        zf = P.zf3[:, c, :]
        o = outn3[:, c, :]
        gc_ = gain[:, c:c + 1]
        S.op("dve", lambda e, zf=zf, o=o, gc_=gc_: e.scalar_tensor_tensor(out=o, in0=zf, scalar=gc_, in1=P.zrs, op0=ALU.mult, op1=ALU.mult), [P.bzf[c], P.bzrs, C.bconst], [boutn])


def alloc_norm_bufs(C, P):
    A = C.A
    P.xs = [A.alloc(D, F32) for _ in range(2)]
    P.bxs = bufs(2)
    P.hb = [A.alloc(D, BF16) for _ in range(4)]
    P.bhb = bufs(4)
    P.junk = A.alloc(D, BF16)
    P.bjunk = Buf()
    P.ss = A.alloc(4, F32)
    P.rs = A.alloc(4, F32)
    P.bss = bufs(4)
    P.brs = bufs(4)
    P.hTs = [v3(A.alloc(8 * 512, BF16), 512) for _ in range(2)]
    P.bhTs = bufs(2)
    P.hctr = 0


def alloc_rope_bufs(C, P):
    A = C.A
    P.posi = C.posi
    P.bposi = Buf()
    for n in ("ang", "kk", "rr", "r2", "cs", "sn"):
        setattr(P, n, A.alloc(512, F32))
        setattr(P, "b" + n, Buf())
    P.rt1 = [A.alloc(512, F32) for _ in range(2)]
    P.rt2 = [A.alloc(512, F32) for _ in range(2)]
    P.brt1 = bufs(2)
    P.brt2 = bufs(2)


def phase_A1(C, T):
    S, A, ps, pb = C.S, C.A, C.ps, C.pb
    NB = C.NB
    P = Ctx()
    NC1 = 2368
    W1 = Wt(C, T.w1, 8, NC1, [(0, 1024), (1024, 2048), (2048, 2368)])
    WK = Wt(C, T.wukv, 2, 2048)
    W1.load(C)
    WK.load(C)
    w13, wk3 = W1.ap, WK.ap
    bwk = WK.rd_all()
    alloc_norm_bufs(C, P)
    alloc_rope_bufs(C, P)
    Kst = A.alloc(8 * 512, BF16); bKst = Buf()
    KNst = A.alloc(8 * 512, BF16); bKNst = Buf()
    Vst = A.alloc(8 * 512, BF16); bVst = Buf()
    VBst = A.alloc(8 * 512, BF16); bVBst = Buf()
    KRst = A.alloc(512, BF16); bKRst = Buf()
    Kst3, KNst3 = v3(Kst, 512), v3(KNst, 512)
    Vst4 = Vst.rearrange("p (h t d) -> p h t d", h=8, t=4)
    VBst4 = VBst.rearrange("p (h t d) -> p h t d", h=8, t=4)
    P.zf3 = v3(A.alloc(2 * 512, F32), 512); P.bzf = bufs(2)
    P.zsq3 = v3(A.alloc(2 * 512, BF16), 512); P.bzsq = bufs(2)
    P.zrs = A.alloc(512, F32); P.bzrs = Buf()
    ckvn = A.alloc(2 * 512, BF16); bckvn = Buf()
    ckvn3 = v3(ckvn, 512)
    C.mbanks = [2, 3, 4, 5, 6, 7]
    C.tbank = [0, 1]
    RP = RopePipe(C, P)
    ev = 0
    import os
    stop = int(os.environ.get("A1STOP", "99"))
    if stop == 0:
        return
    norm_part(C, T.x_all, 0, P)
    nxt = transpose_part(C, P)
    rope_tables(C, T.pos_all, 0, P)
    for blk in range(NB):
        hT3, bhT = nxt
        hk = [hT3[:, kc, :] for kc in range(8)]
        for h in range(H):
            if h == 4 and blk + 1 < NB:
                norm_part(C, T.x_all, (blk + 1) * 512, P)
            bx = nextbank(C)
            S.op("pe", mm_acc(C, ps[:, bx * 512:(bx + 1) * 512], [w13[:, kc, h * 128:(h + 1) * 128] for kc in range(8)], hk, None), W1.rd(0) + [bhT], [pb[bx]])
            RP.push(bx, Kst3[:, h, :], bKst)
        bx = nextbank(C)
        S.op("pe", mm_acc(C, ps[0:64, bx * 512:(bx + 1) * 512], [w13[:, kc, 2304:2368] for kc in range(8)], hk, None), W1.rd(2048) + [bhT], [pb[bx]])
        RP.push(bx, KRst[0:64], bKRst, parts=64)
        RP.flush()
        S.dma("pool", "stK", T.KD.rearrange("h p s -> p h s")[:, :, blk * 512:(blk + 1) * 512], Kst3, reads=[bKst])
        S.dma("pool", "stKR", T.KR[:, blk * 512:(blk + 1) * 512], KRst[0:64], reads=[bKRst])
        if blk + 1 < NB:
            rope_tables(C, T.pos_all, (blk + 1) * 512, P)

        for tt in range(4):
            for half in range(2):
                b = nextbank(C)
                bank = ps[:, b * 512:(b + 1) * 512]
                S.op("pe", mm_acc(C, bank, [hT3[:, kc, tt * 128:(tt + 1) * 128] for kc in range(8)], [w13[:, kc, 1024 + half * 512:1024 + (half + 1) * 512] for kc in range(8)], None), W1.rd(1024) + [bhT], [pb[b]])
                ev += 1
                copy_op(C, "act" if ev % 2 else "dve", Vst4[:, half * 4:(half + 1) * 4, tt, :], v3(bank, 128), [pb[b]], [bVst])
        S.dma("pool", "stV", T.VD.rearrange("h p s -> p h s")[:, :, blk * 512:(blk + 1) * 512], v3(Vst, 512), reads=[bVst])
        if blk + 1 < NB:
            nxt = transpose_part(C, P)
        if stop == 4:
            continue
        feat_rmsnorm(C, P, 2048, 2, w13, W1.rd(2048), hT3, bhT, ckvn3, bckvn, 256, C.g_kva)
        if stop == 41:
            continue
        for h in range(H):
            b = nextbank(C)
            bank = ps[:, b * 512:(b + 1) * 512]
            S.op("pe", mm_acc(C, bank, [wk3[:, c, h * 128:(h + 1) * 128] for c in range(2)], [ckvn3[:, c, :] for c in range(2)], None), bwk + [bckvn], [pb[b]])
            ev += 1
            copy_op(C, "act" if ev % 2 else "dve", KNst3[:, h, :], bank, [pb[b]], [bKNst])
        S.dma("pool", "stKN", T.KN.rearrange("h p s -> p h s")[:, :, blk * 512:(blk + 1) * 512], KNst3, reads=[bKNst])
        if stop == 42:
            continue
        for tt in range(4):
            for half in range(2):
                b = nextbank(C)
                bank = ps[:, b * 512:(b + 1) * 512]
                S.op("pe", mm_acc(C, bank, [ckvn3[:, c, tt * 128:(tt + 1) * 128] for c in range(2)], [wk3[:, c, 1024 + half * 512:1024 + (half + 1) * 512] for c in range(2)], None), bwk + [bckvn], [pb[b]])
                ev += 1
                copy_op(C, "act" if ev % 2 else "dve", VBst4[:, half * 4:(half + 1) * 4, tt, :], v3(bank, 128), [pb[b]], [bVBst])
        S.dma("pool", "stVB", T.VB.rearrange("h p s -> p h s")[:, :, blk * 512:(blk + 1) * 512], v3(VBst, 512), reads=[bVBst])
        if stop == 5:
            continue


def phase_A2(C, T):
    S, A, ps, pb = C.S, C.A, C.ps, C.pb
    NQ = C.NQ
    P = Ctx()
    NC2 = 3456
    W2 = Wt(C, T.w2, 8, NC2, [(0, 1024), (1024, 1408), (1408, 3456)])
    WQ = Wt(C, T.wuq, 3, 1536)
    W2.load(C)
    WQ.load(C)
    w23, wq3 = W2.ap, WQ.ap
    bwq = WQ.rd_all()
    alloc_norm_bufs(C, P)
    alloc_rope_bufs(C, P)
    Qst = A.alloc(8 * 512, BF16); bQst = Buf()
    QNst = A.alloc(8 * 512, BF16); bQNst = Buf()
    QRst = A.alloc(4 * 512, BF16); bQRst = Buf()
    Gst = [A.alloc(8 * 512, BF16) for _ in range(2)]; bGst = bufs(2)
    Qst3, QNst3, QRst3 = v3(Qst, 512), v3(QNst, 512), v3(QRst, 512)
    P.zf3 = v3(A.alloc(3 * 512, F32), 512); P.bzf = bufs(3)
    P.zsq3 = v3(A.alloc(3 * 512, BF16), 512); P.bzsq = bufs(3)
    P.zrs = A.alloc(512, F32); P.bzrs = Buf()
    cqn = A.alloc(3 * 512, BF16); bcqn = Buf()
    cqn3 = v3(cqn, 512)
    C.mbanks = [2, 3, 4, 5, 6, 7]
    C.tbank = [0, 1]
    RP = RopePipe(C, P)
    ev = 0
    norm_part(C, T.x_own, 0, P)
    nxt = transpose_part(C, P)
    rope_tables(C, T.pos_own, 0, P)
    for j in range(NQ):
        hT3, bhT = nxt
        hk = [hT3[:, kc, :] for kc in range(8)]
        sl = slice(j * 512, (j + 1) * 512)
        for h in range(H):
            if h == 4 and j + 1 < NQ:
                norm_part(C, T.x_own, (j + 1) * 512, P)
            bx = nextbank(C)
            S.op("pe", mm_acc(C, ps[:, bx * 512:(bx + 1) * 512], [w23[:, kc, h * 128:(h + 1) * 128] for kc in range(8)], hk, None), W2.rd(0) + [bhT], [pb[bx]])
            RP.push(bx, Qst3[:, h, :], bQst)
        RP.flush()
        S.dma("pool", "stQ", T.QD.rearrange("h p s -> p h s")[:, :, sl], Qst3, reads=[bQst])
        feat_rmsnorm(C, P, 1024, 3, w23, W2.rd(1024), hT3, bhT, cqn3, bcqn, 384, C.g_qa)
        cq = [cqn3[:, c, :] for c in range(3)]
        for hp in range(4):
            bx = nextbank(C)
            S.op("pe", mm_acc(C, ps[:, bx * 512:(bx + 1) * 512], [wq3[:, c, 1024 + hp * 128:1024 + (hp + 1) * 128] for c in range(3)], cq, None), bwq + [bcqn], [pb[bx]])
            RP.push(bx, QRst3[:, hp, :], bQRst)
        RP.flush()
        S.dma("pool", "stQR", T.QR.rearrange("h r s -> (h r) s").rearrange("(a p) s -> p a s", p=128)[:, :, sl], QRst3, reads=[bQRst])
        if j + 1 < NQ:
            rope_tables(C, T.pos_own, (j + 1) * 512, P)
        for h in range(H):
            b = nextbank(C)
            bank = ps[:, b * 512:(b + 1) * 512]
            S.op("pe", mm_acc(C, bank, [wq3[:, c, h * 128:(h + 1) * 128] for c in range(3)], cq, None), bwq + [bcqn], [pb[b]])
            ev += 1
            copy_op(C, "act" if ev % 2 else "dve", QNst3[:, h, :], bank, [pb[b]], [bQNst])
        S.dma("pool", "stQN", T.QN.rearrange("h p s -> p h s")[:, :, sl], QNst3, reads=[bQNst])
        if j + 1 < NQ:
            nxt = transpose_part(C, P)
        for gi in range(2):
            g3 = v3(Gst[gi], 512)
            for c in range(8):
                b = nextbank(C)
                bank = ps[:, b * 512:(b + 1) * 512]
                col = 1408 + gi * 1024 + c * 128
                S.op("pe", mm_acc(C, bank, [w23[:, kc, col:col + 128] for kc in range(8)], hk, None), W2.rd(1408) + [bhT], [pb[b]])
                o = g3[:, c, :]
                S.op("act", lambda e, o=o, bank=bank: e.activation(out=o, in_=bank, func=AF.Sigmoid), [pb[b]], [bGst[gi]])
            S.dma("pool", "stG%d" % gi, (T.GA if gi == 0 else T.GB).rearrange("h p s -> p h s")[:, :, sl], g3, reads=[bGst[gi]])


def attention(C, T, mla):
    S, A, ps, pb = C.S, C.A, C.ps, C.pb
    NB, NQ, Sq, SQ = C.NB, C.NQ, C.S_, C.SQ
    scale = (192.0 if mla else 64.0) ** -0.5
    KT = A.alloc(Sq, BF16); bK = bufs(NB)
    V = A.alloc(Sq, BF16); bV = bufs(NB)
    QT = [A.alloc(SQ, BF16) for _ in range(2)]; bQ = bufs(2)
    if mla:
        KR = A.alloc(Sq, BF16); bKR = Buf()
        QR = [A.alloc(SQ, BF16) for _ in range(2)]; bQR = bufs(2)
        S.dma("sp", "ldKR", KR[0:64], T.KR, writes=[bKR])
        S.dma("sp", "ldKR", KR[64:128], T.KR, writes=[bKR])
    NP = 6
    Pb = [A.alloc(1024, BF16) for _ in range(NP)]; bP = bufs(NP)
    mA = A.alloc(4 * 512, BF16); mB = A.alloc(4 * 512, BF16); bM = Buf()
    S.dma("sp", "ldM", v3(mA, 512), T.maskA.rearrange("t p q -> p t q"), writes=[bM])
    S.dma("sp", "ldM", v3(mB, 512), T.maskB.rearrange("t p q -> p t q"), writes=[bM])
    mA3, mB3 = v3(mA, 512), v3(mB, 512)
    accs = [A.alloc(1024, F32) for _ in range(2)]; baccs = [bufs(2), bufs(2)]
    hi = [A.alloc(512, BF16) for _ in range(2)]; lo = [A.alloc(512, BF16) for _ in range(2)]
    bhl = bufs(2)
    r1 = A.alloc(512, F32); r2 = A.alloc(512, F32)
    br1, br2 = bufs(2)
    oe = [A.alloc(512, F32) for _ in range(3)]; boe = bufs(3)
    Ost = [A.alloc(512, BF16) for _ in range(2)]; bOst = bufs(2)
    Kdram = T.KN if mla else T.KD
    Vdram = T.VB if mla else T.VD
    Qdram = T.QN if mla else T.QD
    Odram = T.OB if mla else T.OA
    octr = 0
    pctr = 0
    sctr = 0
    bctr = 0
    SB = [(0, 1), (2, 3)]
    if mla:
        OB_ = [4, 5]
        L1 = 6
    else:
        O1, O2, L1, L2 = 4, 5, 6, 7

    def bk(i):
        return ps[:, i * 512:(i + 1) * 512]

    def load_head(h):
        for kb in range(NB):
            sl = slice(kb * 512, (kb + 1) * 512)
            S.dma("sp", "ldK%d" % (kb % 4), KT[:, sl], Kdram[h][:, sl], writes=[bK[kb]])
            S.dma("sp", "ldV%d" % (kb % 4), V[:, sl], Vdram[h][:, sl], writes=[bV[kb]])

    def load_q(h):
        s = h % 2
        S.dma("sp", "ldQ%d" % s, QT[s], Qdram[h], writes=[bQ[s]])
        if mla:
            S.dma("sp", "ldQR%d" % s, QR[s][0:64], T.QR[h], writes=[bQR[s]])
            S.dma("sp", "ldQR%d" % s, QR[s][64:128], T.QR[h], writes=[bQR[s]])

    load_q(0)
    pendq = []
    for h in range(H):
        if h + 1 < H:
            load_q(h + 1)
        load_head(h)
        qs = h % 2
        Q = QT[qs]
        units = []
        for j in range(NQ):
            nkb = 2 * j + 2
            tiles = [(kb, kt) for kb in range(nkb) for kt in range(4)]
            if mla:
                grp = [tiles[i:i + 2] for i in range(0, len(tiles), 2)]
            else:
                grp = [[t] for t in tiles]
            for gi, g in enumerate(grp):
                units.append((j, g, gi == 0, gi == len(grp) - 1))
        for (j, g, first, last) in units:
            if first:
                bctr += 1
            bsl = bctr % 2
            sb = SB[sctr % 2]
            sctr += 1
            pslot = pctr % NP
            pctr += 1
            qsl = slice(j * 512, (j + 1) * 512)
            rd = [bQ[qs], C.bconst, bM]
            if mla:
                rd += [bKR, bQR[qs]]
            for (kb, kt) in g:
                rd.append(bK[kb])

            def qk(e, g=g, sb=sb, j=j, qsl=qsl, Q=Q, qs=qs):
                last_i = None
                kss = [slice(kb * 512 + kt * 128, kb * 512 + (kt + 1) * 128) for (kb, kt) in g]
                masked = g[0][0] >= 2 * j
                m3 = mA3 if g[0][0] == 2 * j else mB3
                if mla:
                    e.matmul(bk(sb[0]), lhsT=KT[:, kss[0]], rhs=Q[:, qsl], start=True, stop=False)
                    e.matmul(bk(sb[1]), lhsT=KT[:, kss[1]], rhs=Q[:, qsl], start=True, stop=False)
                    e.matmul(bk(sb[0]), lhsT=KR[0:64, kss[0]], rhs=QR[qs][0:64, qsl], start=False, stop=not masked)
                    last_i = e.matmul(bk(sb[1]), lhsT=KR[64:128, kss[1]], rhs=QR[qs][64:128, qsl], start=False, stop=not masked)
                    if masked:
                        e.matmul(bk(sb[0]), lhsT=C.ident, rhs=m3[:, g[0][1], :], start=False, stop=True)
                        last_i = e.matmul(bk(sb[1]), lhsT=C.ident, rhs=m3[:, g[1][1], :], start=False, stop=True)
                else:
                    e.matmul(bk(sb[0]), lhsT=KT[0:64, kss[0]], rhs=Q[0:64, qsl], start=True, stop=not masked)
                    last_i = e.matmul(bk(sb[1]), lhsT=KT[64:128, kss[0]], rhs=Q[64:128, qsl], start=True, stop=not masked)
                    if masked:
                        e.matmul(bk(sb[0]), lhsT=C.ident, rhs=m3[:, g[0][1], :], start=False, stop=True)
                        last_i = e.matmul(bk(sb[1]), lhsT=C.ident, rhs=m3[:, g[0][1], :], start=False, stop=True)
                return last_i
            S.op("pe", qk, rd, [pb[sb[0]], pb[sb[1]]])
            Pt = Pb[pslot]
            src = ps[:, sb[0] * 512:(sb[0] + 2) * 512]
            S.op("act", lambda e, Pt=Pt, src=src: e.activation(out=Pt, in_=src, func=AF.Exp, scale=scale), [pb[sb[0]], pb[sb[1]]], [bP[pslot]])
            for half in ((0, 1) if mla else (0,)):
                ph = Pt[:, half * 512:(half + 1) * 512]
                ac = accs[bsl][:, half * 512:(half + 1) * 512]
                bb_ = baccs[bsl][half]
                if first:
                    S.op("dve", lambda e, ac=ac, ph=ph: e.tensor_copy(out=ac, in_=ph), [bP[pslot]], [bb_])
                else:
                    S.op("dve", lambda e, ac=ac, ph=ph: e.tensor_tensor(out=ac, in0=ac, in1=ph, op=ALU.add), [bP[pslot], bb_], [bb_])
            if len(pendq) >= 2:
                pendq.pop(0)()

            def pv_rec(g=g, first=first, last=last, Pt=Pt, pslot=pslot, j=j, bsl=bsl, h=h):
                nonlocal octr
                rdv = [bP[pslot], C.bconst] + [bV[kb] for (kb, kt) in g]
                kss = [slice(kb * 512 + kt * 128, kb * 512 + (kt + 1) * 128) for (kb, kt) in g]
                if mla:
                    Ob = OB_[bsl]

                    def pv(e):
                        e.matmul(bk(Ob), lhsT=V[:, kss[0]], rhs=Pt[:, 0:512], start=first, stop=False)
                        return e.matmul(bk(Ob), lhsT=V[:, kss[1]], rhs=Pt[:, 512:1024], start=False, stop=last)
                    wr = [pb[Ob]]
                else:
                    def pv(e):
                        e.matmul(bk(O1), lhsT=V[:, kss[0]], rhs=Pt[:, 0:512], start=first, stop=last)
                        e.matmul(bk(O2), lhsT=V[:, kss[0]], rhs=Pt[:, 512:1024], start=first, stop=last)
                        return e.matmul(bk(L2), lhsT=C.ones, rhs=Pt[:, 512:1024], start=first, stop=last)
                    wr = [pb[O1], pb[O2], pb[L2]]
                S.op("pe", pv, rdv, wr)
                if last:
                    os_ = octr % 2
                    octr += 1
                    ot = Ost[os_]
                    qsl2 = slice(j * 512, (j + 1) * 512)
                    acw = accs[bsl]
                    if not mla:
                        for k_, bnk in enumerate((O1, O2, L2)):
                            S.op("act", lambda e, k_=k_, bnk=bnk: e.activation(out=oe[k_], in_=bk(bnk), func=AF.Copy), [pb[bnk]], [boe[k_]])
                    for half in ((0, 1) if mla else (0,)):
                        ac, hh, ll = acw[:, half * 512:(half + 1) * 512], hi[half], lo[half]
                        S.op("pool", lambda e, ac=ac, hh=hh: e.tensor_copy(out=hh, in_=ac), [baccs[bsl][half]], [bhl[half]])
                        S.op("pool", lambda e, ac=ac, hh=hh, ll=ll: e.tensor_tensor(out=ll, in0=ac, in1=hh, op=ALU.subtract), [baccs[bsl][half], bhl[half]], [bhl[half]])
                    if mla:
                        def lsum(e):
                            e.matmul(bk(L1), lhsT=C.ones, rhs=hi[0], start=True, stop=False)
                            e.matmul(bk(L1), lhsT=C.ones, rhs=lo[0], start=False, stop=False)
                            e.matmul(bk(L1), lhsT=C.ones, rhs=hi[1], start=False, stop=False)
                            return e.matmul(bk(L1), lhsT=C.ones, rhs=lo[1], start=False, stop=True)
                        S.op("pe", lsum, [bhl[0], bhl[1], C.bconst], [pb[L1]])
                        S.op("act", lambda e: e.activation(out=r1, in_=bk(L1), func=AF.Ln), [pb[L1]], [br1])
                        S.op("act", lambda e: e.activation(out=r1, in_=r1, func=AF.Exp, scale=-1.0), [br1], [br1])
                        S.op("dve", lambda e: e.tensor_tensor(out=ot, in0=bk(Ob), in1=r1, op=ALU.mult), [pb[Ob], br1], [bOst[os_]])
                    else:
                        def lsum(e):
                            e.matmul(bk(L1), lhsT=C.ones, rhs=hi[0], start=True, stop=False)
                            return e.matmul(bk(L1), lhsT=C.ones, rhs=lo[0], start=False, stop=True)
                        S.op("pe", lsum, [bhl[0], C.bconst], [pb[L1]])
                        S.op("dve", lambda e: e.reciprocal(out=r1, in_=bk(L1)), [pb[L1]], [br1])
                        S.op("dve", lambda e: e.reciprocal(out=r2, in_=oe[2]), [boe[2]], [br2])
                        S.op("pool", lambda e: e.tensor_tensor(out=oe[0], in0=oe[0], in1=r1, op=ALU.mult), [boe[0], br1], [boe[0]])
                        S.op("pool", lambda e: e.tensor_tensor(out=oe[1], in0=oe[1], in1=r2, op=ALU.mult), [boe[1], br2], [boe[1]])
                        S.op("dve", lambda e: e.scalar_tensor_tensor(out=ot, in0=oe[1], scalar=C.neglam, in1=oe[0], op0=ALU.mult, op1=ALU.add), [boe[0], boe[1], C.bconst], [bOst[os_]])
                    S.dma("pool", "stO%d" % os_, Odram[h][:, qsl2], ot, reads=[bOst[os_]])
            pendq.append(pv_rec)
        while pendq:
            pendq.pop(0)()


def phase_C1(C, T):
    S, A, ps, pb = C.S, C.A, C.ps, C.pb
    NQ = C.NQ
    WPA = Wt(C, T.wpa, 8, 1024)
    WPB = Wt(C, T.wpb, 8, 1024)
    WO = Wt(C, T.wo, 8, 1024)
    WPA.load(C)
    WPB.load(C)
    WO.load(C)
    wpa, wpb, wo = WPA.ap, WPB.ap, WO.ap
    bwpa, bwpb, bwo = WPA.rd_all(), WPB.rd_all(), WO.rd_all()
    gpost = A.alloc(D, F32); bgpost = Buf()
    S.dma("sp", "ldg", gpost, T.g_post.partition_broadcast(128), writes=[bgpost])
    oa = [v3(A.alloc(8 * 512, BF16), 512) for _ in range(2)]; boa = bufs(2)
    ob = [v3(A.alloc(8 * 512, BF16), 512) for _ in range(2)]; bob = bufs(2)
    ga = [v3(A.alloc(8 * 512, BF16), 512) for _ in range(2)]; bga = bufs(2)
    gb = [v3(A.alloc(8 * 512, BF16), 512) for _ in range(2)]; bgb = bufs(2)
    oans = [v3(A.alloc(8 * 512, BF16), 512) for _ in range(2)]; boans = [bufs(8), bufs(8)]
    sq = [A.alloc(512, BF16) for _ in range(2)]; bsq = bufs(2)
    rsn = [A.alloc(512, F32) for _ in range(2)]; brsn = bufs(2)
    m1 = [A.alloc(512, F32) for _ in range(2)]; bm1 = bufs(2)
    m2 = [A.alloc(512, F32) for _ in range(2)]; bm2 = bufs(2)
    mT = v3(A.alloc(8 * 512, BF16), 512); bmT = bufs(8)
    xs = [A.alloc(D, F32) for _ in range(2)]; bxs = bufs(2)
    tt_ = [A.alloc(D, F32) for _ in range(2)]; btt = bufs(2)
    junk = A.alloc(D, BF16); bjunk = Buf()
    ss = A.alloc(8, F32); bss = bufs(8)
    C.mbanks = [0, 1, 2, 3]
    upairs = [(4, 5), (6, 7)]
    uc = 0
    xc = 0

    def loads(j):
        s = j % 2
        sl = slice(j * 512, (j + 1) * 512)
        S.dma("sp", "ldoa%d" % s, oa[s], T.OA.rearrange("h p s -> p h s")[:, :, sl], writes=[boa[s]])
        S.dma("sp", "ldob%d" % s, ob[s], T.OB.rearrange("h p s -> p h s")[:, :, sl], writes=[bob[s]])
        S.dma("sp", "ldga%d" % s, ga[s], T.GA.rearrange("h p s -> p h s")[:, :, sl], writes=[bga[s]])
        S.dma("sp", "ldgb%d" % s, gb[s], T.GB.rearrange("h p s -> p h s")[:, :, sl], writes=[bgb[s]])

    def subln(j):
        s = j % 2
        oan, boan = oans[s], boans[s]
        for h in range(H):
            i = h % 2
            src = oa[s][:, h, :]
            S.op("pool", lambda e, i=i, src=src: e.tensor_tensor(out=sq[i], in0=src, in1=src, op=ALU.mult), [boa[s]], [bsq[i]])
            b = nextbank(C)
            bank = ps[:, b * 512:(b + 1) * 512]
            S.op("pe", mm_acc(C, bank, [C.ones], [sq[i]], None), [C.bconst, bsq[i]], [pb[b]])
            S.op("act", lambda e, i=i, bank=bank: e.activation(out=rsn[i], in_=bank, func=AF.Ln, scale=1.0 / 128, bias=EPS), [pb[b]], [brsn[i]])
            S.op("act", lambda e, i=i: e.activation(out=rsn[i], in_=rsn[i], func=AF.Exp, scale=-0.5), [brsn[i]], [brsn[i]])
            dst = oan[:, h, :]
            S.op("dve", lambda e, i=i, src=src, dst=dst: e.scalar_tensor_tensor(out=dst, in0=src, scalar=C.gsub, in1=rsn[i], op0=ALU.mult, op1=ALU.mult), [boa[s], brsn[i], C.bconst], [boan[h]])

    loads(0)
    subln(0)
    for j in range(NQ):
        if j + 1 < NQ:
            loads(j + 1)
        s = j % 2
        oan, boan = oans[s], boans[s]
        for c in range(8):
            i = c % 2
            ba = nextbank(C)
            bka = ps[:, ba * 512:(ba + 1) * 512]
            S.op("pe", mm_acc(C, bka, [wpa[:, h, c * 128:(c + 1) * 128] for h in range(H)], [oan[:, h, :] for h in range(H)], None), bwpa + boan, [pb[ba]])
            bb = nextbank(C)
            bkb = ps[:, bb * 512:(bb + 1) * 512]
            S.op("pe", mm_acc(C, bkb, [wpb[:, h, c * 128:(c + 1) * 128] for h in range(H)], [ob[s][:, h, :] for h in range(H)], None), bwpb + [bob[s]], [pb[bb]])
            gac = ga[s][:, c, :]
            gbc = gb[s][:, c, :]
            S.op("dve", lambda e, i=i, bka=bka, gac=gac: e.tensor_tensor(out=m1[i], in0=bka, in1=gac, op=ALU.mult), [pb[ba], bga[s]], [bm1[i]])
            S.op("dve", lambda e, i=i, bkb=bkb, gbc=gbc: e.tensor_tensor(out=m2[i], in0=bkb, in1=gbc, op=ALU.mult), [pb[bb], bgb[s]], [bm2[i]])
            dst = mT[:, c, :]
            S.op("pool", lambda e, i=i, dst=dst: e.tensor_tensor(out=dst, in0=m1[i], in1=m2[i], op=ALU.add), [bm1[i], bm2[i]], [bmT[c]])
        if j + 1 < NQ:
            subln(j + 1)
        for tt in range(4):
            up = upairs[uc % 2]
            uc += 1
            xi = xc % 2
            xc += 1
            row0 = j * 512 + tt * 128
            S.dma("sp", "ldx%d" % xi, xs[xi], T.x_own[row0:row0 + 128, :], writes=[bxs[xi]])

            def mmu(e, up=up, tt=tt):
                for half in range(2):
                    bank = ps[:, up[half] * 512:(up[half] + 1) * 512]
                    for c in range(8):
                        last = e.matmul(bank, lhsT=mT[:, c, tt * 128:(tt + 1) * 128], rhs=wo[:, c, half * 512:(half + 1) * 512], start=(c == 0), stop=(c == 7))
                return last
            S.op("pe", mmu, bwo + bmT, [pb[up[0]], pb[up[1]]])
            u = ps[:, up[0] * 512:(up[0] + 2) * 512]
            si = (j * 4 + tt) % 8
            ssc = ss[:, si:si + 1]
            S.op("act", lambda e, u=u, ssc=ssc: e.activation(out=junk, in_=u, func=AF.Square, accum_out=ssc), [pb[up[0]], pb[up[1]]], [bjunk, bss[si]])
            S.op("act", lambda e, ssc=ssc: e.activation(out=ssc, in_=ssc, func=AF.Sqrt, scale=1.0 / D, bias=EPS), [bss[si]], [bss[si]])
            S.op("dve", lambda e, ssc=ssc: e.reciprocal(out=ssc, in_=ssc), [bss[si]], [bss[si]])
            t = tt_[xi]
            S.op("dve", lambda e, t=t, u=u, ssc=ssc: e.scalar_tensor_tensor(out=t, in0=u, scalar=ssc, in1=gpost, op0=ALU.mult, op1=ALU.mult), [pb[up[0]], pb[up[1]], bss[si], bgpost], [btt[xi]])
            x_ = xs[xi]
            S.op("pool", lambda e, t=t, x_=x_: e.tensor_tensor(out=t, in0=t, in1=x_, op=ALU.add), [btt[xi], bxs[xi]], [btt[xi]])
            S.dma("pool", "stx%d" % xi, T.out[row0:row0 + 128, :], t, reads=[btt[xi]])


def phase_C2(C, T):
    S, A, ps, pb = C.S, C.A, C.ps, C.pb
    NQ = C.NQ
    fsp = [(0, 768), (768, 1536), (1536, 2176), (2176, 2816)]
    WG = Wt(C, T.wg, 8, DFF, fsp)
    WU = Wt(C, T.wu, 8, DFF, fsp)
    WD = Wt(C, T.wd, NF, 1024)
    for g in range(4):
        WG.load(C, [g])
        WU.load(C, [g])
    WD.load(C)
    wg, wu, wd = WG.ap, WU.ap, WD.ap
    bwd = WD.rd_all()
    aT = v3(A.alloc(NF * 512, BF16), 512); baT = bufs(NF)
    gpost = A.alloc(D, F32); bgpost = Buf()
    S.dma("sp", "ldg", gpost, T.g_fpost.partition_broadcast(128), writes=[bgpost])
    x1 = [A.alloc(D, F32) for _ in range(4)]; bx1 = bufs(4)
    hb = [A.alloc(D, BF16) for _ in range(2)]; bhb = bufs(2)
    hT = A.alloc(8 * 512, BF16); bhT = Buf()
    hT3 = v3(hT, 512)
    junk = A.alloc(D, BF16); bjunk = Buf()
    ss = A.alloc(8, F32); bss = bufs(8)
    sg = [A.alloc(512, F32) for _ in range(2)]; bsg = bufs(2)
    ot = [A.alloc(D, F32) for _ in range(2)]; bot = bufs(2)
    tb = [0, 1]
    tc = 0
    gub = [(2, 3), (4, 5)]
    gc = 0
    dpairs = [(6, 7), (0, 1)]
    dc = 0
    oc = 0
    for j in range(NQ):
        for tt in range(4):
            row0 = j * 512 + tt * 128
            S.dma("sp", "ldx1_%d" % tt, x1[tt], T.out[row0:row0 + 128, :], writes=[bx1[tt]])
            si = tt
            ssc = ss[:, si:si + 1]
            xx = x1[tt]
            S.op("act", lambda e, xx=xx, ssc=ssc: e.activation(out=junk, in_=xx, func=AF.Square, accum_out=ssc), [bx1[tt]], [bjunk, bss[si]])
            S.op("act", lambda e, ssc=ssc: e.activation(out=ssc, in_=ssc, func=AF.Sqrt, scale=1.0 / D, bias=EPS), [bss[si]], [bss[si]])
            S.op("dve", lambda e, ssc=ssc: e.reciprocal(out=ssc, in_=ssc), [bss[si]], [bss[si]])
            hi = tt % 2
            hbi = hb[hi]
            S.op("dve", lambda e, hbi=hbi, xx=xx, ssc=ssc: e.tensor_scalar(out=hbi, in0=xx, scalar1=ssc, scalar2=None, op0=ALU.mult), [bx1[tt], bss[si]], [bhb[hi]])
            t_b = tb[tc % 2]
            tc += 1
            pst = ps[:, t_b * 512:(t_b + 1) * 512].bitcast(BF16)

            def tr(e, hbi=hbi, pst=pst):
                for kc in range(8):
                    last = e.transpose(pst[:, kc * 128:(kc + 1) * 128], hbi[:, kc * 128:(kc + 1) * 128], C.ident)
                return last
            S.op("pe", tr, [bhb[hi], C.bconst], [pb[t_b]])
            dstT = hT3[:, :, tt * 128:(tt + 1) * 128]
            S.op("dve", lambda e, dstT=dstT, pst=pst: e.tensor_tensor(out=dstT, in0=v3(pst, 128), in1=C.g_fpre.unsqueeze(2).to_broadcast([128, 8, 128]), op=ALU.mult), [pb[t_b], C.bconst], [bhT])
        hk = [hT3[:, kc, :] for kc in range(8)]
        for f in range(NF):
            gu = gub[gc % 2]
            gc += 1
            G = ps[:, gu[0] * 512:(gu[0] + 1) * 512]
            U = ps[:, gu[1] * 512:(gu[1] + 1) * 512]
            S.op("pe", mm_acc(C, G, [wg[:, kc, f * 128:(f + 1) * 128] for kc in range(8)], hk, None), WG.rd(f * 128) + [bhT], [pb[gu[0]]])
            S.op("pe", mm_acc(C, U, [wu[:, kc, f * 128:(f + 1) * 128] for kc in range(8)], hk, None), WU.rd(f * 128) + [bhT], [pb[gu[1]]])
            i = f % 2
            S.op("act", lambda e, i=i, G=G: e.activation(out=sg[i], in_=G, func=AF.Silu), [pb[gu[0]]], [bsg[i]])
            dst = aT[:, f, :]
            S.op("dve", lambda e, i=i, U=U, dst=dst: e.tensor_tensor(out=dst, in0=U, in1=sg[i], op=ALU.mult), [pb[gu[1]], bsg[i]], [baT[f]])
        for tt in range(4):
            dp = dpairs[dc % 2]
            dc += 1

            def mmd(e, dp=dp, tt=tt):
                for half in range(2):
                    bank = ps[:, dp[half] * 512:(dp[half] + 1) * 512]
                    for f in range(NF):
                        last = e.matmul(bank, lhsT=aT[:, f, tt * 128:(tt + 1) * 128], rhs=wd[:, f, half * 512:(half + 1) * 512], start=(f == 0), stop=(f == NF - 1))
                return last
            S.op("pe", mmd, bwd + baT, [pb[dp[0]], pb[dp[1]]])
            u = ps[:, dp[0] * 512:(dp[0] + 2) * 512] if dp[1] == dp[0] + 1 else None
            si = 4 + tt
            ssc = ss[:, si:si + 1]
            S.op("act", lambda e, u=u, ssc=ssc: e.activation(out=junk, in_=u, func=AF.Square, accum_out=ssc), [pb[dp[0]], pb[dp[1]]], [bjunk, bss[si]])
            S.op("act", lambda e, ssc=ssc: e.activation(out=ssc, in_=ssc, func=AF.Sqrt, scale=1.0 / D, bias=EPS), [bss[si]], [bss[si]])
            S.op("dve", lambda e, ssc=ssc: e.reciprocal(out=ssc, in_=ssc), [bss[si]], [bss[si]])
            oi = oc % 2
            oc += 1
            o_ = ot[oi]
            S.op("dve", lambda e, o_=o_, u=u, ssc=ssc: e.scalar_tensor_tensor(out=o_, in0=u, scalar=ssc, in1=gpost, op0=ALU.mult, op1=ALU.mult), [pb[dp[0]], pb[dp[1]], bss[si], bgpost], [bot[oi]])
            xx = x1[tt]
            S.op("pool", lambda e, o_=o_, xx=xx: e.tensor_tensor(out=o_, in0=o_, in1=xx, op=ALU.add), [bot[oi], bx1[tt]], [bot[oi]])
            row0 = j * 512 + tt * 128
            S.dma("pool", "sto%d" % oi, T.out[row0:row0 + 128, :], o_, reads=[bot[oi]])


def build(Sq, phases=("A1", "A2", "BD", "BM", "C1", "C2"), dbg=False):
    nc = bass.Bass("TRN2", target_bir_lowering=False)
    NB = Sq // 512
    NQ = NB // 2
    SQ = Sq // 2
    T = Ctx()

    def din(name, shape, dt=F32):
        return nc.dram_tensor(name, shape, dt, kind="ExternalInput").ap()

    def scr(name, shape, dt=BF16):
        return nc.dram_tensor(name, shape, dt, kind="ExternalOutput" if dbg else "Internal").ap()

    T.x_all = din("x_all", [Sq, D]); T.x_own = din("x_own", [SQ, D])
    T.pos_all = din("pos_all", [Sq], I32); T.pos_own = din("pos_own", [SQ], I32)
    T.w1 = din("w1", [D, 2368]); T.w2 = din("w2", [D, 3456])
    T.wuq = din("wuq", [384, 1536]); T.wukv = din("wukv", [256, 2048])
    T.wpa = din("wpa", [D, D]); T.wpb = din("wpb", [D, D]); T.wo = din("wo", [D, D])
    T.wg = din("wg", [D, DFF]); T.wu = din("wu", [D, DFF]); T.wd = din("wd", [DFF, D])
    T.cst = din("cst", [128, 32])
    T.permm = din("permm", [128, 128], BF16)
    T.lam4 = din("lam4", [4, 64])
    T.g_post = din("g_post", [D]); T.g_fpost = din("g_fpost", [D])
    T.maskA = din("maskA", [4, 128, 512], BF16); T.maskB = din("maskB", [4, 128, 512], BF16)
    T.out = nc.dram_tensor("out", [SQ, D], F32, kind="ExternalOutput").ap()
    T.KD = scr("KD", [H, 128, Sq]); T.VD = scr("VD", [H, 128, Sq])
    T.KN = scr("KN", [H, 128, Sq]); T.VB = scr("VB", [H, 128, Sq]); T.KR = scr("KR", [64, Sq])
    T.QD = scr("QD", [H, 128, SQ]); T.QN = scr("QN", [H, 128, SQ]); T.QR = scr("QR", [H, 64, SQ])
    T.GA = scr("GA", [H, 128, SQ]); T.GB = scr("GB", [H, 128, SQ])
    T.OA = scr("OA", [H, 128, SQ]); T.OB = scr("OB", [H, 128, SQ])

    C = Ctx()
    C.nc = nc
    C.S = Sched()
    C.A = Arena(nc, ARENA_BYTES)
    C.ps = nc.alloc_psum_tensor("ps", [128, 4096], F32)
    C.pb = [Buf(excl=True) for _ in range(8)]
    C.NB, C.NQ, C.S_, C.SQ = NB, NQ, Sq, SQ
    C.xctr = C.tctr = C.mctr = C.rctr = C.wctr = 0
    C.bout = Buf()
    C.posi = nc.alloc_sbuf_tensor("posi", [128, 512], I32)[:, :]
    S, A = C.S, C.A
    C.bconst = Buf()
    cst = A.alloc(32, F32)
    S.dma("sp", "cst", cst, T.cst, writes=[C.bconst])
    C.invf = cst[:, 0:1]
    C.sinsc = cst[:, 1:2]
    C.g_pre = cst[:, 4:12]
    C.g_qa = cst[:, 12:15]
    C.g_kva = cst[:, 15:17]
    C.g_fpre = cst[:, 17:25]
    small = A.alloc(8, F32)
    C.gsub = small[:, 0:1]
    C.neglam = small[:, 1:2]
    S.op("dve", lambda e: e.tensor_scalar(out=C.gsub, in0=cst[:, 2:3], scalar1=float(1.0 - LAMBDA_INIT), scalar2=None, op0=ALU.mult), [C.bconst], [C.bconst])
    lam = A.alloc(4 * 64, F32)
    S.dma("sp", "cst", lam, T.lam4.rearrange("a b -> (a b)").partition_broadcast(128), writes=[C.bconst])
    lp = A.alloc(128, F32)
    S.op("dve", lambda e: e.tensor_tensor(out=lp[:, 0:64], in0=lam[:, 0:64], in1=lam[:, 64:128], op=ALU.mult), [C.bconst], [C.bconst])
    S.op("dve", lambda e: e.tensor_tensor(out=lp[:, 64:128], in0=lam[:, 128:192], in1=lam[:, 192:256], op=ALU.mult), [C.bconst], [C.bconst])
    S.op("dve", lambda e: e.tensor_reduce(out=small[:, 2:3], in_=lp[:, 0:64], axis=mybir.AxisListType.X, op=ALU.add), [C.bconst], [C.bconst])
    S.op("dve", lambda e: e.tensor_reduce(out=small[:, 3:4], in_=lp[:, 64:128], axis=mybir.AxisListType.X, op=ALU.add), [C.bconst], [C.bconst])
    S.op("act", lambda e: e.activation(out=small[:, 4:6], in_=small[:, 2:4], func=AF.Exp), [C.bconst], [C.bconst])
    S.op("dve", lambda e: e.tensor_tensor(out=small[:, 6:7], in0=small[:, 5:6], in1=small[:, 4:5], op=ALU.subtract), [C.bconst], [C.bconst])
    S.op("dve", lambda e: e.tensor_single_scalar(out=C.neglam, in_=small[:, 6:7], scalar=-float(LAMBDA_INIT), op=ALU.add), [C.bconst], [C.bconst])
    C.perm = A.alloc(128, BF16)
    S.dma("sp", "cst", C.perm, T.permm, writes=[C.bconst])
    idf = A.alloc(128, F32)
    C.ident = A.alloc(128, BF16)
    C.ones = A.alloc(128, BF16)
    S.op("pool", lambda e: e.memset(idf, 1.0), [], [C.bconst])
    S.op("pool", lambda e: e.memset(C.ones, 1.0), [], [C.bconst])
    S.op("pool", lambda e: e.affine_select(out=idf, in_=idf, pattern=[[-1, 128]], compare_op=ALU.is_equal, fill=0.0, base=0, channel_multiplier=1), [C.bconst], [C.bconst])
    S.op("dve", lambda e: e.tensor_copy(out=C.ident, in_=idf), [C.bconst], [C.bconst])
    A.persist()
    S.barrier()
    for ph in phases:
        A.reset()
        if ph == "A1":
            phase_A1(C, T)
        elif ph == "A2":
            phase_A2(C, T)
        elif ph == "BD":
            attention(C, T, mla=False)
        elif ph == "BM":
            attention(C, T, mla=True)
        elif ph == "C1":
            phase_C1(C, T)
        elif ph == "C2":
            phase_C2(C, T)
        S.barrier()
    with ExitStack() as st:
        S.emit(nc, st)
    return nc


def _swap_idx(n_groups64):
    idx = []
    for g in range(n_groups64):
        b = g * 64
        idx += list(range(b + 32, b + 64)) + list(range(b, b + 32))
    return np.array(idx)


def host_prep(inputs, Sq):
    f32 = np.float32
    x = np.asarray(inputs["x"], f32)
    pos = np.asarray(inputs["positions"], np.int32)
    w_in = np.asarray(inputs["w_in"], f32)[0]
    qa, ka, va = w_in[:, 0:1024], w_in[:, 1024:2048], w_in[:, 2048:3072]
    cq, ckv, kr = w_in[:, 3072:3456], w_in[:, 3456:3712], w_in[:, 3712:3776]
    gA, gB = w_in[:, 3776:4800], w_in[:, 4800:5824]
    w1 = np.ascontiguousarray(np.concatenate([ka, va, ckv, kr], axis=1))
    w2 = np.ascontiguousarray(np.concatenate([qa, cq, gA, gB], axis=1))
    w_uq = np.asarray(inputs["w_uq"], f32)[0].reshape(384, 8, 192)
    uq_n = w_uq[:, :, 0:128].reshape(384, 1024)
    uq_r = w_uq[:, :, 128:192].reshape(384, 512)
    wuq = np.ascontiguousarray(np.concatenate([uq_n, uq_r], axis=1))
    w_ukv = np.asarray(inputs["w_ukv"], f32)[0].reshape(256, 8, 256)
    wukv = np.ascontiguousarray(np.concatenate([w_ukv[:, :, 0:128].reshape(256, 1024), w_ukv[:, :, 128:256].reshape(256, 1024)], axis=1))
    cst = np.zeros((128, 32), f32)
    invf = (1.0 / (f32(10000.0) ** (np.arange(32, dtype=f32) * f32(2.0 / 64)))).astype(f32)
    pidx = np.arange(128)
    cst[:, 0] = invf[pidx % 32]
    cst[:, 1] = np.where((pidx % 64) < 32, -1.0, 1.0)
    cst[:, 2] = np.asarray(inputs["da_subln"], f32)[0]
    cst[:, 4:12] = np.asarray(inputs["ln_mix_pre"], f32)[0].reshape(8, 128).T
    cst[:, 12:15] = np.asarray(inputs["q_a_norm"], f32)[0].reshape(3, 128).T
    cst[:, 15:17] = np.asarray(inputs["kv_a_norm"], f32)[0].reshape(2, 128).T
    cst[:, 17:25] = np.asarray(inputs["ln_ffn_pre"], f32)[0].reshape(8, 128).T
    lam4 = np.ascontiguousarray(np.stack([np.asarray(inputs[k], f32)[0] for k in ("lambda_q1", "lambda_k1", "lambda_q2", "lambda_k2")]))
    kk = np.arange(512)[:, None] // 64
    qq = np.arange(512)[None, :] // 64
    diag = np.where(kk <= qq, 0.0, NEG).astype(f32).reshape(4, 128, 512)
    zeros = np.zeros((4, 128, 512), f32)
    full = np.full((4, 128, 512), NEG, f32)
    bf = ml_dtypes.bfloat16
    bf = ml_dtypes.bfloat16
    permm = np.zeros((128, 128), f32)
    permm[_swap_idx(2), np.arange(128)] = 1.0
    shared = dict(
        permm=permm.astype(bf),
        w1=w1, w2=w2, wuq=wuq, wukv=wukv,
        wpa=np.ascontiguousarray(np.asarray(inputs["w_proj_a"], f32)[0]),
        wpb=np.ascontiguousarray(np.asarray(inputs["w_proj_b"], f32)[0]),
        wo=np.ascontiguousarray(np.asarray(inputs["w_o"], f32)[0]),
        wg=np.ascontiguousarray(np.asarray(inputs["w_ffn_gate"], f32)[0]),
        wu=np.ascontiguousarray(np.asarray(inputs["w_ffn_up"], f32)[0]),
        wd=np.ascontiguousarray(np.asarray(inputs["w_ffn_down"], f32)[0]),
        cst=cst, lam4=lam4,
        g_post=np.ascontiguousarray(np.asarray(inputs["ln_mix_post"], f32)[0]),
        g_fpost=np.ascontiguousarray(np.asarray(inputs["ln_ffn_post"], f32)[0]),
    )
    B = x.shape[0]
    NB = Sq // 512
    maps = []
    for c in range(2 * B):
        b, p = c // 2, c % 2
        m = dict(shared)
        m["x_all"] = np.ascontiguousarray(x[b])
        m["x_own"] = np.ascontiguousarray(x[b].reshape(NB, 512, D)[p::2].reshape(Sq // 2, D))
        m["pos_all"] = np.ascontiguousarray(pos[b])
        m["pos_own"] = np.ascontiguousarray(pos[b].reshape(NB, 512)[p::2].reshape(Sq // 2))
        m["maskA"] = (diag if p == 0 else zeros).astype(bf)
        m["maskB"] = (full if p == 0 else diag).astype(bf)
        maps.append(m)
    return maps


_NC_CACHE = {}


def kernel(**inputs):
    x = np.asarray(inputs["x"])
    B, Sq, _ = x.shape
    maps = host_prep(inputs, Sq)
    if Sq not in _NC_CACHE:
        _NC_CACHE[Sq] = build(Sq)
    nc = _NC_CACHE[Sq]
    res = run_bass_kernel_spmd(nc, maps, core_ids=list(range(2 * B)))
    NB = Sq // 512
    out = np.empty((B, NB, 512, D), np.float32)
    for c in range(2 * B):
        b, p = c // 2, c % 2
        out[b, p::2] = np.asarray(res.results[c]["out"], np.float32).reshape(NB // 2, 512, D)
    return out.reshape(B, Sq, D)
```

```python
import numpy as np
import ml_dtypes
from contextlib import ExitStack
import concourse.bass as bass
import concourse.mybir as mybir
from concourse.bass_utils import run_bass_kernel_spmd

F32, BF16, I32, U8 = mybir.dt.float32, mybir.dt.bfloat16, mybir.dt.int32, mybir.dt.uint8
AF = mybir.ActivationFunctionType
ALU = mybir.AluOpType
ENGS = ("pe", "act", "dve", "pool", "sp")

D = 1024
H = 8
DFF = 2816
NF = DFF // 128
EPS = 1e-6
LAMBDA_INIT = 0.8 - 0.6 * 1.0
NEG = -30000.0
ARENA_BYTES = 204 * 1024
PI_LO = 3.1415925
MAGIC = 12582912.0


def _cw():
    p = 2 * np.pi
    c1 = np.float32(6.28125)
    c2 = np.float32(p - float(c1))
    c3 = np.float32(p - float(c1) - float(c2))
    return float(c1), float(c2), float(c3)


CW1, CW2, CW3 = _cw()


class Buf:
    __slots__ = ("w", "r", "x")

    def __init__(self, excl=False):
        self.w = None
        self.r = []
        self.x = excl


def bufs(n):
    return [Buf() for _ in range(n)]


class Op:
    __slots__ = ("eng", "fn", "waits", "signal", "count", "ch", "n")


class Sched:
    def __init__(self):
        self.ops = {e: [] for e in ENGS}
        self.ch_last = {}
        self.ch_cnt = {}
        self.last_compute = {e: None for e in ENGS}
        self.pending_barrier = {e: [] for e in ENGS}

    def _add(self, eng, fn, reads, writes, ch=None):
        op = Op()
        op.eng = eng
        op.fn = fn
        op.signal = False
        op.count = 0
        op.ch = ch
        op.n = 0
        deps = []
        raw = set()
        for b in reads:
            if b.w is not None:
                deps.append(b.w)
                raw.add(id(b.w))
            if b.x:
                deps.extend(r for r in b.r if r.eng != eng)
        for b in writes:
            if b.w is not None:
                deps.append(b.w)
            deps.extend(b.r)
        bar = self.pending_barrier[eng]
        bar_ids = set(id(d) for d in bar)
        deps.extend(bar)
        self.pending_barrier[eng] = []
        if ch is not None:
            prev = self.ch_last.get(ch)
            if prev is not None:
                deps.append(prev)
            self.ch_cnt[ch] = self.ch_cnt.get(ch, 0) + 1
            op.n = self.ch_cnt[ch]
            self.ch_last[ch] = op
        waits = []
        seen = set()
        for d in deps:
            if d is op or id(d) in seen:
                continue
            seen.add(id(d))
            if d.ch is None and ch is None and d.eng == eng and id(d) not in bar_ids:
                if eng == "pe":
                    continue
            if d.ch is None and ch is None and d.eng == eng and id(d) in bar_ids:
                continue
            d.signal = True
            waits.append(d)
        op.waits = waits
        for b in reads:
            b.r.append(op)
        for b in writes:
            b.w = op
            b.r = []
        self.ops[eng].append(op)
        if ch is None:
            self.last_compute[eng] = op
        return op

    def op(self, eng, fn, reads=(), writes=()):
        return self._add(eng, fn, reads, writes)

    def dma(self, q, ch, out, in_, reads=(), writes=()):
        return self._add(q, lambda e: e.dma_start(out=out, in_=in_), reads, writes, ch=ch)

    def barrier(self):
        deps = [o for o in self.last_compute.values() if o is not None]
        deps += list(self.ch_last.values())
        for e in ENGS:
            self.pending_barrier[e] = list(deps)

    def emit(self, nc, stack):
        for e in ENGS:
            c = 0
            for op in self.ops[e]:
                if op.ch is None and op.signal:
                    c += 1
                    op.count = c
        esem = {e: stack.enter_context(nc.semaphore("s_" + e)) for e in ENGS}
        chsem = {ch: stack.enter_context(nc.semaphore("c_" + str(ch))) for ch in self.ch_cnt}
        block = stack.enter_context(nc.Block())
        ops = self.ops
        ch_last = self.ch_last

        def run(e, eng):
            waited = {}
            for op in ops[e]:
                for d in op.waits:
                    if d.ch is not None:
                        key, sem, val = ("c", d.ch), chsem[d.ch], 16 * d.n
                    else:
                        key, sem, val = ("e", d.eng), esem[d.eng], d.count
                    if waited.get(key, 0) >= val:
                        continue
                    waited[key] = val
                    eng.wait_ge(sem, val)
                ins = op.fn(eng)
                if op.ch is not None:
                    ins.then_inc(chsem[op.ch], 16)
                elif op.signal:
                    ins.then_inc(esem[e], 1)
            for ch, last in ch_last.items():
                if last.eng == e:
                    eng.wait_ge(chsem[ch], 16 * last.n)

        @block.tensor
        def _(eng):
            run("pe", eng)

        @block.scalar
        def _(eng):
            run("act", eng)

        @block.vector
        def _(eng):
            run("dve", eng)

        @block.gpsimd
        def _(eng):
            run("pool", eng)

        @block.sync
        def _(eng):
            run("sp", eng)


class Arena:
    def __init__(self, nc, nbytes):
        self.t = nc.alloc_sbuf_tensor("arena", [128, nbytes], U8)
        self.nbytes = nbytes
        self.off = 0
        self.floor = 0

    def alloc(self, free_elems, dtype, parts=128):
        sz = 2 if dtype == BF16 else (1 if dtype == U8 else 4)
        nb = free_elems * sz
        nb_al = (nb + 63) // 64 * 64
        assert self.off + nb_al <= self.nbytes, f"SBUF arena overflow {self.off}+{nb_al}>{self.nbytes}"
        ap = self.t[0:parts, self.off:self.off + nb]
        self.off += nb_al
        if dtype != U8:
            ap = ap.bitcast(dtype)
        return ap

    def persist(self):
        self.floor = self.off

    def reset(self):
        self.off = self.floor


class Ctx:
    pass


def v3(ap, inner):
    return ap.rearrange("p (a b) -> p a b", b=inner)


def copy_op(C, eng, out, in_, reads, writes):
    S = C.S
    if eng == "act":
        S.op("act", lambda e: e.activation(out=out, in_=in_, func=AF.Copy), reads, writes)
    else:
        S.op(eng, lambda e: e.tensor_copy(out=out, in_=in_), reads, writes)


class Wt:
    def __init__(self, C, src, kcn, ncols, splits=None):
        self.ap = v3(C.A.alloc(kcn * ncols, BF16), ncols)
        self.src, self.kcn, self.ncols = src, kcn, ncols
        self.splits = splits or [(0, ncols)]
        self.bufs = [bufs(kcn) for _ in self.splits]

    def load(self, C, groups=None):
        for g in (range(len(self.splits)) if groups is None else groups):
            c0, c1 = self.splits[g]
            for kc in range(self.kcn):
                ch = "w%d" % (C.wctr % 4)
                C.wctr += 1
                C.S.dma("pool", ch, self.ap[:, kc, c0:c1], self.src[kc * 128:(kc + 1) * 128, c0:c1], writes=[self.bufs[g][kc]])

    def rd(self, col=0):
        for g, (c0, c1) in enumerate(self.splits):
            if c0 <= col < c1:
                return self.bufs[g]
        raise ValueError(col)

    def rd_all(self):
        return [b for g in self.bufs for b in g]


def norm_part(C, xsrc, row0, P):
    S = C.S
    for tt in range(4):
        slot = C.xctr % 2
        C.xctr += 1
        xs, bxs = P.xs[slot], P.bxs[slot]
        S.dma("sp", "x%d" % slot, xs, xsrc[row0 + tt * 128: row0 + (tt + 1) * 128, :], writes=[bxs])
        ss = P.ss[:, tt:tt + 1]
        rs = P.rs[:, tt:tt + 1]
        bss, brs = P.bss[tt], P.brs[tt]
        S.op("act", lambda e, xs=xs, ss=ss: e.activation(out=P.junk, in_=xs, func=AF.Square, accum_out=ss), [bxs], [P.bjunk, bss])
        S.op("act", lambda e, ss=ss, rs=rs: e.activation(out=rs, in_=ss, func=AF.Sqrt, scale=1.0 / D, bias=EPS), [bss], [brs])
        S.op("dve", lambda e, rs=rs: e.reciprocal(out=rs, in_=rs), [brs], [brs])
        hb = P.hb[tt]
        S.op("dve", lambda e, hb=hb, xs=xs, rs=rs: e.tensor_scalar(out=hb, in0=xs, scalar1=rs, scalar2=None, op0=ALU.mult), [bxs, brs], [P.bhb[tt]])


def transpose_part(C, P):
    S, ps, pb = C.S, C.ps, C.pb
    hT3 = P.hTs[P.hctr % 2]
    bhT_ = P.bhTs[P.hctr % 2]
    P.hctr += 1
    for tt in range(4):
        hb = P.hb[tt]
        tb = C.tbank[C.tctr % 2]
        C.tctr += 1
        pst = ps[:, tb * 512:(tb + 1) * 512].bitcast(BF16)

        def tr(e, hb=hb, pst=pst):
            for kc in range(8):
                last = e.transpose(pst[:, kc * 128:(kc + 1) * 128], hb[:, kc * 128:(kc + 1) * 128], C.ident)
            return last
        S.op("pe", tr, [P.bhb[tt], C.bconst], [pb[tb]])
        dstT = hT3[:, :, tt * 128:(tt + 1) * 128]
        S.op("dve", lambda e, dstT=dstT, pst=pst: e.tensor_tensor(out=dstT, in0=v3(pst, 128), in1=C.g_pre.unsqueeze(2).to_broadcast([128, 8, 128]), op=ALU.mult), [pb[tb], C.bconst], [bhT_])
    return hT3, bhT_


def rope_tables(C, pos_src, col0, P):
    S = C.S
    S.dma("sp", "pos", P.posi, pos_src[col0:col0 + 512].partition_broadcast(128), writes=[P.bposi])
    a, k, r, r2 = P.ang, P.kk, P.rr, P.r2
    S.op("dve", lambda e: e.tensor_copy(out=a, in_=P.posi), [P.bposi], [P.bang])
    S.op("dve", lambda e: e.tensor_scalar(out=a, in0=a, scalar1=C.invf, scalar2=None, op0=ALU.mult), [P.bang, C.bconst], [P.bang])
    S.op("dve", lambda e: e.tensor_scalar(out=k, in0=a, scalar1=float(1 / (2 * np.pi)), scalar2=MAGIC, op0=ALU.mult, op1=ALU.add), [P.bang], [P.bkk])
    S.op("dve", lambda e: e.tensor_single_scalar(out=k, in_=k, scalar=-MAGIC, op=ALU.add), [P.bkk], [P.bkk])
    S.op("dve", lambda e: e.scalar_tensor_tensor(out=r, in0=k, scalar=-CW1, in1=a, op0=ALU.mult, op1=ALU.add), [P.bkk, P.bang], [P.brr])
    S.op("dve", lambda e: e.scalar_tensor_tensor(out=r, in0=k, scalar=-CW2, in1=r, op0=ALU.mult, op1=ALU.add), [P.bkk, P.brr], [P.brr])
    S.op("dve", lambda e: e.scalar_tensor_tensor(out=r, in0=k, scalar=-CW3, in1=r, op0=ALU.mult, op1=ALU.add), [P.bkk, P.brr], [P.brr])
    S.op("dve", lambda e: e.tensor_scalar(out=r, in0=r, scalar1=-PI_LO, scalar2=PI_LO, op0=ALU.max, op1=ALU.min), [P.brr], [P.brr])
    S.op("dve", lambda e: e.scalar_tensor_tensor(out=r2, in0=r, scalar=-1.0, in1=r, op0=ALU.mult, op1=ALU.max), [P.brr], [P.br2])
    S.op("act", lambda e: e.activation(out=P.cs, in_=r2, func=AF.Sin, scale=-1.0, bias=float(np.pi / 2)), [P.br2], [P.bcs])
    S.op("act", lambda e: e.activation(out=P.sn, in_=r, func=AF.Sin, scale=C.sinsc), [P.brr, C.bconst], [P.bsn])


def mm_acc(C, bank, lhs_list, rhs_list, reads):
    n = len(lhs_list)

    def fn(e):
        for i in range(n):
            last = e.matmul(bank, lhsT=lhs_list[i], rhs=rhs_list[i], start=(i == 0), stop=(i == n - 1))
        return last
    return fn


def nextbank(C):
    b = C.mbanks[C.mctr % len(C.mbanks)]
    C.mctr += 1
    return b


def rope_apply(C, P, bx, bxs, out, bout, parts=128):
    S, ps, pb = C.S, C.ps, C.pb
    i = C.rctr % 2
    C.rctr += 1
    t1, t2 = P.rt1[i][0:parts], P.rt2[i][0:parts]
    b1, b2 = P.brt1[i], P.brt2[i]
    X = ps[0:parts, bx * 512:(bx + 1) * 512]
    Xs = ps[0:parts, bxs * 512:(bxs + 1) * 512]
    S.op("dve", lambda e: e.tensor_tensor(out=t1, in0=X, in1=P.cs[0:parts], op=ALU.mult), [pb[bx], P.bcs], [b1])
    S.op("dve", lambda e: e.tensor_tensor(out=t2, in0=Xs, in1=P.sn[0:parts], op=ALU.mult), [pb[bxs], P.bsn], [b2])
    S.op("pool", lambda e: e.tensor_tensor(out=out, in0=t1, in1=t2, op=ALU.add), [b1, b2], [bout])


class RopePipe:
    def __init__(self, C, P):
        self.C, self.P, self.pend = C, P, None
        A = C.A
        self.xb = [A.alloc(512, BF16) for _ in range(2)]
        self.bxb = bufs(2)
        self.k = 0

    def push(self, bx, out, bout, parts=128):
        C, S, ps, pb = self.C, self.C.S, self.C.ps, self.C.pb
        i = self.k % 2
        self.k += 1
        xb = self.xb[i][0:parts]
        X = ps[0:parts, bx * 512:(bx + 1) * 512]
        S.op("act", lambda e: e.activation(out=xb, in_=X, func=AF.Copy), [pb[bx]], [self.bxb[i]])
        prev = self.pend
        self.pend = (bx, xb, self.bxb[i], out, bout, parts)
        if prev is not None:
            self._finish(prev)

    def _finish(self, item):
        C, S, ps, pb = self.C, self.C.S, self.C.ps, self.C.pb
        bx, xb, bxb, out, bout, parts = item
        bxs = nextbank(C)
        Xs = ps[0:parts, bxs * 512:(bxs + 1) * 512]
        S.op("pe", lambda e: e.matmul(Xs, lhsT=C.perm[0:parts, 0:parts], rhs=xb, start=True, stop=True), [bxb, C.bconst], [pb[bxs]])
        rope_apply(C, self.P, bx, bxs, out, bout, parts)

    def flush(self):
        if self.pend is not None:
            self._finish(self.pend)
            self.pend = None


def feat_rmsnorm(C, P, src_cols, nch, wsb3, bw, hT3, bhT, outn3, boutn, n_feat, gain):
    S, ps, pb = C.S, C.ps, C.pb
    for c in range(nch):
        b = nextbank(C)
        bank = ps[:, b * 512:(b + 1) * 512]
        col = src_cols + c * 128
        S.op("pe", mm_acc(C, bank, [wsb3[:, kc, col:col + 128] for kc in range(8)], [hT3[:, kc, :] for kc in range(8)], None), bw + [bhT], [pb[b]])
        zf = P.zf3[:, c, :]
        sq = P.zsq3[:, c, :]
        S.op("act", lambda e, sq=sq, bank=bank: e.activation(out=sq, in_=bank, func=AF.Square), [pb[b]], [P.bzsq[c]])
        S.op("dve", lambda e, zf=zf, bank=bank: e.tensor_copy(out=zf, in_=bank), [pb[b]], [P.bzf[c]])
    b = nextbank(C)
    bank = ps[:, b * 512:(b + 1) * 512]
    S.op("pe", mm_acc(C, bank, [C.ones] * nch, [P.zsq3[:, c, :] for c in range(nch)], None), [C.bconst] + P.bzsq[:nch], [pb[b]])
    S.op("act", lambda e: e.activation(out=P.zrs, in_=bank, func=AF.Sqrt, scale=1.0 / n_feat, bias=EPS), [pb[b]], [P.bzrs])
    S.op("dve", lambda e: e.reciprocal(out=P.zrs, in_=P.zrs), [P.bzrs], [P.bzrs])
    for c in range(nch):
        zf = P.zf3[:, c, :]
        o = outn3[:, c, :]
        gc_ = gain[:, c:c + 1]
        S.op("dve", lambda e, zf=zf, o=o, gc_=gc_: e.scalar_tensor_tensor(out=o, in0=zf, scalar=gc_, in1=P.zrs, op0=ALU.mult, op1=ALU.mult), [P.bzf[c], P.bzrs, C.bconst], [boutn])


def alloc_norm_bufs(C, P):
    A = C.A
    P.xs = [A.alloc(D, F32) for _ in range(2)]
    P.bxs = bufs(2)
    P.hb = [A.alloc(D, BF16) for _ in range(4)]
    P.bhb = bufs(4)
    P.junk = A.alloc(D, BF16)
    P.bjunk = Buf()
    P.ss = A.alloc(4, F32)
    P.rs = A.alloc(4, F32)
    P.bss = bufs(4)
    P.brs = bufs(4)
    P.hTs = [v3(A.alloc(8 * 512, BF16), 512) for _ in range(2)]
    P.bhTs = bufs(2)
    P.hctr = 0


def alloc_rope_bufs(C, P):
    A = C.A
    P.posi = C.posi
    P.bposi = Buf()
    for n in ("ang", "kk", "rr", "r2", "cs", "sn"):
        setattr(P, n, A.alloc(512, F32))
        setattr(P, "b" + n, Buf())
    P.rt1 = [A.alloc(512, F32) for _ in range(2)]
    P.rt2 = [A.alloc(512, F32) for _ in range(2)]
    P.brt1 = bufs(2)
    P.brt2 = bufs(2)


def phase_A1(C, T):
    S, A, ps, pb = C.S, C.A, C.ps, C.pb
    NB = C.NB
    P = Ctx()
    NC1 = 2368
    W1 = Wt(C, T.w1, 8, NC1, [(0, 1024), (1024, 2048), (2048, 2368)])
    WK = Wt(C, T.wukv, 2, 2048)
    W1.load(C)
    WK.load(C)
    w13, wk3 = W1.ap, WK.ap
    bwk = WK.rd_all()
    alloc_norm_bufs(C, P)
    alloc_rope_bufs(C, P)
    Kst = A.alloc(8 * 512, BF16); bKst = Buf()
    KNst = A.alloc(8 * 512, BF16); bKNst = Buf()
    Vst = A.alloc(8 * 512, BF16); bVst = Buf()
    VBst = A.alloc(8 * 512, BF16); bVBst = Buf()
    KRst = A.alloc(512, BF16); bKRst = Buf()
    Kst3, KNst3 = v3(Kst, 512), v3(KNst, 512)
    Vst4 = Vst.rearrange("p (h t d) -> p h t d", h=8, t=4)
    VBst4 = VBst.rearrange("p (h t d) -> p h t d", h=8, t=4)
    P.zf3 = v3(A.alloc(2 * 512, F32), 512); P.bzf = bufs(2)
    P.zsq3 = v3(A.alloc(2 * 512, BF16), 512); P.bzsq = bufs(2)
    P.zrs = A.alloc(512, F32); P.bzrs = Buf()
    ckvn = A.alloc(2 * 512, BF16); bckvn = Buf()
    ckvn3 = v3(ckvn, 512)
    C.mbanks = [2, 3, 4, 5, 6, 7]
    C.tbank = [0, 1]
    RP = RopePipe(C, P)
    ev = 0
    import os
    stop = int(os.environ.get("A1STOP", "99"))
    if stop == 0:
        return
    norm_part(C, T.x_all, 0, P)
    nxt = transpose_part(C, P)
    rope_tables(C, T.pos_all, 0, P)
    for blk in range(NB):
        hT3, bhT = nxt
        hk = [hT3[:, kc, :] for kc in range(8)]
        for h in range(H):
            if h == 4 and blk + 1 < NB:
                norm_part(C, T.x_all, (blk + 1) * 512, P)
            bx = nextbank(C)
            S.op("pe", mm_acc(C, ps[:, bx * 512:(bx + 1) * 512], [w13[:, kc, h * 128:(h + 1) * 128] for kc in range(8)], hk, None), W1.rd(0) + [bhT], [pb[bx]])
            RP.push(bx, Kst3[:, h, :], bKst)
        bx = nextbank(C)
        S.op("pe", mm_acc(C, ps[0:64, bx * 512:(bx + 1) * 512], [w13[:, kc, 2304:2368] for kc in range(8)], hk, None), W1.rd(2048) + [bhT], [pb[bx]])
        RP.push(bx, KRst[0:64], bKRst, parts=64)
        RP.flush()
        S.dma("pool", "stK", T.KD.rearrange("h p s -> p h s")[:, :, blk * 512:(blk + 1) * 512], Kst3, reads=[bKst])
        S.dma("pool", "stKR", T.KR[:, blk * 512:(blk + 1) * 512], KRst[0:64], reads=[bKRst])
        if blk + 1 < NB:
            rope_tables(C, T.pos_all, (blk + 1) * 512, P)

        for tt in range(4):
            for half in range(2):
                b = nextbank(C)
                bank = ps[:, b * 512:(b + 1) * 512]
                S.op("pe", mm_acc(C, bank, [hT3[:, kc, tt * 128:(tt + 1) * 128] for kc in range(8)], [w13[:, kc, 1024 + half * 512:1024 + (half + 1) * 512] for kc in range(8)], None), W1.rd(1024) + [bhT], [pb[b]])
                ev += 1
                copy_op(C, "act", Vst4[:, half * 4:(half + 1) * 4, tt, :], v3(bank, 128), [pb[b]], [bVst])
        S.dma("pool", "stV", T.VD.rearrange("h p s -> p h s")[:, :, blk * 512:(blk + 1) * 512], v3(Vst, 512), reads=[bVst])
        if blk + 1 < NB:
            nxt = transpose_part(C, P)
        if stop == 4:
            continue
        feat_rmsnorm(C, P, 2048, 2, w13, W1.rd(2048), hT3, bhT, ckvn3, bckvn, 256, C.g_kva)
        if stop == 41:
            continue
        for h in range(H):
            b = nextbank(C)
            bank = ps[:, b * 512:(b + 1) * 512]
            S.op("pe", mm_acc(C, bank, [wk3[:, c, h * 128:(h + 1) * 128] for c in range(2)], [ckvn3[:, c, :] for c in range(2)], None), bwk + [bckvn], [pb[b]])
            ev += 1
            copy_op(C, "act", KNst3[:, h, :], bank, [pb[b]], [bKNst])
        S.dma("pool", "stKN", T.KN.rearrange("h p s -> p h s")[:, :, blk * 512:(blk + 1) * 512], KNst3, reads=[bKNst])
        if stop == 42:
            continue
        for tt in range(4):
            for half in range(2):
                b = nextbank(C)
                bank = ps[:, b * 512:(b + 1) * 512]
                S.op("pe", mm_acc(C, bank, [ckvn3[:, c, tt * 128:(tt + 1) * 128] for c in range(2)], [wk3[:, c, 1024 + half * 512:1024 + (half + 1) * 512] for c in range(2)], None), bwk + [bckvn], [pb[b]])
                ev += 1
                copy_op(C, "act", VBst4[:, half * 4:(half + 1) * 4, tt, :], v3(bank, 128), [pb[b]], [bVBst])
        S.dma("pool", "stVB", T.VB.rearrange("h p s -> p h s")[:, :, blk * 512:(blk + 1) * 512], v3(VBst, 512), reads=[bVBst])
        if stop == 5:
            continue


def phase_A2(C, T):
    S, A, ps, pb = C.S, C.A, C.ps, C.pb
    NQ = C.NQ
    P = Ctx()
    NC2 = 3456
    W2 = Wt(C, T.w2, 8, NC2, [(0, 1024), (1024, 1408), (1408, 3456)])
    WQ = Wt(C, T.wuq, 3, 1536)
    W2.load(C)
    WQ.load(C)
    w23, wq3 = W2.ap, WQ.ap
    bwq = WQ.rd_all()
    alloc_norm_bufs(C, P)
    alloc_rope_bufs(C, P)
    Qst = A.alloc(8 * 512, BF16); bQst = Buf()
    QNst = A.alloc(8 * 512, BF16); bQNst = Buf()
    QRst = A.alloc(4 * 512, BF16); bQRst = Buf()
    Gst = [A.alloc(8 * 512, BF16) for _ in range(2)]; bGst = bufs(2)
    Qst3, QNst3, QRst3 = v3(Qst, 512), v3(QNst, 512), v3(QRst, 512)
    P.zf3 = v3(A.alloc(3 * 512, F32), 512); P.bzf = bufs(3)
    P.zsq3 = v3(A.alloc(3 * 512, BF16), 512); P.bzsq = bufs(3)
    P.zrs = A.alloc(512, F32); P.bzrs = Buf()
    cqn = A.alloc(3 * 512, BF16); bcqn = Buf()
    cqn3 = v3(cqn, 512)
    C.mbanks = [2, 3, 4, 5, 6, 7]
    C.tbank = [0, 1]
    RP = RopePipe(C, P)
    ev = 0
    norm_part(C, T.x_own, 0, P)
    nxt = transpose_part(C, P)
    rope_tables(C, T.pos_own, 0, P)
    for j in range(NQ):
        hT3, bhT = nxt
        hk = [hT3[:, kc, :] for kc in range(8)]
        sl = slice(j * 512, (j + 1) * 512)
        for h in range(H):
            if h == 4 and j + 1 < NQ:
                norm_part(C, T.x_own, (j + 1) * 512, P)
            bx = nextbank(C)
            S.op("pe", mm_acc(C, ps[:, bx * 512:(bx + 1) * 512], [w23[:, kc, h * 128:(h + 1) * 128] for kc in range(8)], hk, None), W2.rd(0) + [bhT], [pb[bx]])
            RP.push(bx, Qst3[:, h, :], bQst)
        RP.flush()
        S.dma("pool", "stQ", T.QD.rearrange("h p s -> p h s")[:, :, sl], Qst3, reads=[bQst])
        feat_rmsnorm(C, P, 1024, 3, w23, W2.rd(1024), hT3, bhT, cqn3, bcqn, 384, C.g_qa)
        cq = [cqn3[:, c, :] for c in range(3)]
        for hp in range(4):
            bx = nextbank(C)
            S.op("pe", mm_acc(C, ps[:, bx * 512:(bx + 1) * 512], [wq3[:, c, 1024 + hp * 128:1024 + (hp + 1) * 128] for c in range(3)], cq, None), bwq + [bcqn], [pb[bx]])
            RP.push(bx, QRst3[:, hp, :], bQRst)
        RP.flush()
        S.dma("pool", "stQR", T.QR.rearrange("h r s -> (h r) s").rearrange("(a p) s -> p a s", p=128)[:, :, sl], QRst3, reads=[bQRst])
        if j + 1 < NQ:
            rope_tables(C, T.pos_own, (j + 1) * 512, P)
        for h in range(H):
            b = nextbank(C)
            bank = ps[:, b * 512:(b + 1) * 512]
            S.op("pe", mm_acc(C, bank, [wq3[:, c, h * 128:(h + 1) * 128] for c in range(3)], cq, None), bwq + [bcqn], [pb[b]])
            ev += 1
            copy_op(C, "act", QNst3[:, h, :], bank, [pb[b]], [bQNst])
        S.dma("pool", "stQN", T.QN.rearrange("h p s -> p h s")[:, :, sl], QNst3, reads=[bQNst])
        if j + 1 < NQ:
            nxt = transpose_part(C, P)
        for gi in range(2):
            g3 = v3(Gst[gi], 512)
            for c in range(8):
                b = nextbank(C)
                bank = ps[:, b * 512:(b + 1) * 512]
                col = 1408 + gi * 1024 + c * 128
                S.op("pe", mm_acc(C, bank, [w23[:, kc, col:col + 128] for kc in range(8)], hk, None), W2.rd(1408) + [bhT], [pb[b]])
                o = g3[:, c, :]
                S.op("act", lambda e, o=o, bank=bank: e.activation(out=o, in_=bank, func=AF.Sigmoid), [pb[b]], [bGst[gi]])
            S.dma("pool", "stG%d" % gi, (T.GA if gi == 0 else T.GB).rearrange("h p s -> p h s")[:, :, sl], g3, reads=[bGst[gi]])


def attention(C, T, mla):
    S, A, ps, pb = C.S, C.A, C.ps, C.pb
    NB, NQ, Sq, SQ = C.NB, C.NQ, C.S_, C.SQ
    scale = (192.0 if mla else 64.0) ** -0.5
    KT = A.alloc(Sq, BF16); bK = bufs(NB)
    V = A.alloc(Sq, BF16); bV = bufs(NB)
    QT = [A.alloc(SQ, BF16) for _ in range(2)]; bQ = bufs(2)
    if mla:
        KR = A.alloc(Sq, BF16); bKR = Buf()
        QR = [A.alloc(SQ, BF16) for _ in range(2)]; bQR = bufs(2)
        S.dma("sp", "ldKR", KR[0:64], T.KR, writes=[bKR])
        S.dma("sp", "ldKR", KR[64:128], T.KR, writes=[bKR])
    NP = 6
    Pb = [A.alloc(1024, BF16) for _ in range(NP)]; bP = bufs(NP)
    mA = A.alloc(4 * 512, BF16); mB = A.alloc(4 * 512, BF16); bM = Buf()
    S.dma("sp", "ldM", v3(mA, 512), T.maskA.rearrange("t p q -> p t q"), writes=[bM])
    S.dma("sp", "ldM", v3(mB, 512), T.maskB.rearrange("t p q -> p t q"), writes=[bM])
    mA3, mB3 = v3(mA, 512), v3(mB, 512)
    accs = [A.alloc(1024, F32) for _ in range(2)]; baccs = [bufs(2), bufs(2)]
    hi = [A.alloc(512, BF16) for _ in range(2)]; lo = [A.alloc(512, BF16) for _ in range(2)]
    bhl = bufs(2)
    r1 = A.alloc(512, F32); r2 = A.alloc(512, F32)
    br1, br2 = bufs(2)
    oe = [A.alloc(512, F32) for _ in range(3)]; boe = bufs(3)
    Ost = [A.alloc(512, BF16) for _ in range(2)]; bOst = bufs(2)
    Kdram = T.KN if mla else T.KD
    Vdram = T.VB if mla else T.VD
    Qdram = T.QN if mla else T.QD
    Odram = T.OB if mla else T.OA
    octr = 0
    pctr = 0
    sctr = 0
    bctr = 0
    SB = [(0, 1), (2, 3)]
    if mla:
        OB_ = [4, 5]
        L1 = 6
    else:
        O1, O2, L1, L2 = 4, 5, 6, 7

    def bk(i):
        return ps[:, i * 512:(i + 1) * 512]

    def load_head(h):
        for kb in range(NB):
            sl = slice(kb * 512, (kb + 1) * 512)
            S.dma("sp", "ldK%d" % (kb % 4), KT[:, sl], Kdram[h][:, sl], writes=[bK[kb]])
            S.dma("sp", "ldV%d" % (kb % 4), V[:, sl], Vdram[h][:, sl], writes=[bV[kb]])

    def load_q(h):
        s = h % 2
        S.dma("sp", "ldQ%d" % s, QT[s], Qdram[h], writes=[bQ[s]])
        if mla:
            S.dma("sp", "ldQR%d" % s, QR[s][0:64], T.QR[h], writes=[bQR[s]])
            S.dma("sp", "ldQR%d" % s, QR[s][64:128], T.QR[h], writes=[bQR[s]])

    load_q(0)
    pendq = []
    for h in range(H):
        if h + 1 < H:
            load_q(h + 1)
        load_head(h)
        qs = h % 2
        Q = QT[qs]
        units = []
        for j in range(NQ):
            nkb = 2 * j + 2
            tiles = [(kb, kt) for kb in range(nkb) for kt in range(4)]
            if mla:
                grp = [tiles[i:i + 2] for i in range(0, len(tiles), 2)]
            else:
                grp = [[t] for t in tiles]
            for gi, g in enumerate(grp):
                units.append((j, g, gi == 0, gi == len(grp) - 1))
        for (j, g, first, last) in units:
            if first:
                bctr += 1
            bsl = bctr % 2
            sb = SB[sctr % 2]
            sctr += 1
            pslot = pctr % NP
            pctr += 1
            qsl = slice(j * 512, (j + 1) * 512)
            rd = [bQ[qs], C.bconst, bM]
            if mla:
                rd += [bKR, bQR[qs]]
            for (kb, kt) in g:
                rd.append(bK[kb])

            def qk(e, g=g, sb=sb, j=j, qsl=qsl, Q=Q, qs=qs):
                last_i = None
                kss = [slice(kb * 512 + kt * 128, kb * 512 + (kt + 1) * 128) for (kb, kt) in g]
                masked = g[0][0] >= 2 * j
                m3 = mA3 if g[0][0] == 2 * j else mB3
                if mla:
                    e.matmul(bk(sb[0]), lhsT=KT[:, kss[0]], rhs=Q[:, qsl], start=True, stop=False)
                    e.matmul(bk(sb[1]), lhsT=KT[:, kss[1]], rhs=Q[:, qsl], start=True, stop=False)
                    e.matmul(bk(sb[0]), lhsT=KR[0:64, kss[0]], rhs=QR[qs][0:64, qsl], start=False, stop=not masked)
                    last_i = e.matmul(bk(sb[1]), lhsT=KR[64:128, kss[1]], rhs=QR[qs][64:128, qsl], start=False, stop=not masked)
                    if masked:
                        e.matmul(bk(sb[0]), lhsT=C.ident, rhs=m3[:, g[0][1], :], start=False, stop=True)
                        last_i = e.matmul(bk(sb[1]), lhsT=C.ident, rhs=m3[:, g[1][1], :], start=False, stop=True)
                else:
                    e.matmul(bk(sb[0]), lhsT=KT[0:64, kss[0]], rhs=Q[0:64, qsl], start=True, stop=not masked)
                    last_i = e.matmul(bk(sb[1]), lhsT=KT[64:128, kss[0]], rhs=Q[64:128, qsl], start=True, stop=not masked)
                    if masked:
                        e.matmul(bk(sb[0]), lhsT=C.ident, rhs=m3[:, g[0][1], :], start=False, stop=True)
                        last_i = e.matmul(bk(sb[1]), lhsT=C.ident, rhs=m3[:, g[0][1], :], start=False, stop=True)
                return last_i
            S.op("pe", qk, rd, [pb[sb[0]], pb[sb[1]]])
            Pt = Pb[pslot]
            src = ps[:, sb[0] * 512:(sb[0] + 2) * 512]
            S.op("act", lambda e, Pt=Pt, src=src: e.activation(out=Pt, in_=src, func=AF.Exp, scale=scale), [pb[sb[0]], pb[sb[1]]], [bP[pslot]])
            for half in ((0, 1) if mla else (0,)):
                ph = Pt[:, half * 512:(half + 1) * 512]
                ac = accs[bsl][:, half * 512:(half + 1) * 512]
                bb_ = baccs[bsl][half]
                if first:
                    S.op("dve", lambda e, ac=ac, ph=ph: e.tensor_copy(out=ac, in_=ph), [bP[pslot]], [bb_])
                else:
                    S.op("dve", lambda e, ac=ac, ph=ph: e.tensor_tensor(out=ac, in0=ac, in1=ph, op=ALU.add), [bP[pslot], bb_], [bb_])
            if len(pendq) >= 2:
                pendq.pop(0)()

            def pv_rec(g=g, first=first, last=last, Pt=Pt, pslot=pslot, j=j, bsl=bsl, h=h):
                nonlocal octr
                rdv = [bP[pslot], C.bconst] + [bV[kb] for (kb, kt) in g]
                kss = [slice(kb * 512 + kt * 128, kb * 512 + (kt + 1) * 128) for (kb, kt) in g]
                if mla:
                    Ob = OB_[bsl]

                    def pv(e):
                        e.matmul(bk(Ob), lhsT=V[:, kss[0]], rhs=Pt[:, 0:512], start=first, stop=False)
                        return e.matmul(bk(Ob), lhsT=V[:, kss[1]], rhs=Pt[:, 512:1024], start=False, stop=last)
                    wr = [pb[Ob]]
                else:
                    def pv(e):
                        e.matmul(bk(O1), lhsT=V[:, kss[0]], rhs=Pt[:, 0:512], start=first, stop=last)
                        e.matmul(bk(O2), lhsT=V[:, kss[0]], rhs=Pt[:, 512:1024], start=first, stop=last)
                        return e.matmul(bk(L2), lhsT=C.ones, rhs=Pt[:, 512:1024], start=first, stop=last)
                    wr = [pb[O1], pb[O2], pb[L2]]
                S.op("pe", pv, rdv, wr)
                if last:
                    os_ = octr % 2
                    octr += 1
                    ot = Ost[os_]
                    qsl2 = slice(j * 512, (j + 1) * 512)
                    acw = accs[bsl]
                    if not mla:
                        for k_, bnk in enumerate((O1, O2, L2)):
                            S.op("act", lambda e, k_=k_, bnk=bnk: e.activation(out=oe[k_], in_=bk(bnk), func=AF.Copy), [pb[bnk]], [boe[k_]])
                    for half in ((0, 1) if mla else (0,)):
                        ac, hh, ll = acw[:, half * 512:(half + 1) * 512], hi[half], lo[half]
                        S.op("pool", lambda e, ac=ac, hh=hh: e.tensor_copy(out=hh, in_=ac), [baccs[bsl][half]], [bhl[half]])
                        S.op("pool", lambda e, ac=ac, hh=hh, ll=ll: e.tensor_tensor(out=ll, in0=ac, in1=hh, op=ALU.subtract), [baccs[bsl][half], bhl[half]], [bhl[half]])
                    if mla:
                        def lsum(e):
                            e.matmul(bk(L1), lhsT=C.ones, rhs=hi[0], start=True, stop=False)
                            e.matmul(bk(L1), lhsT=C.ones, rhs=lo[0], start=False, stop=False)
                            e.matmul(bk(L1), lhsT=C.ones, rhs=hi[1], start=False, stop=False)
                            return e.matmul(bk(L1), lhsT=C.ones, rhs=lo[1], start=False, stop=True)
                        S.op("pe", lsum, [bhl[0], bhl[1], C.bconst], [pb[L1]])
                        S.op("act", lambda e: e.activation(out=r1, in_=bk(L1), func=AF.Ln), [pb[L1]], [br1])
                        S.op("act", lambda e: e.activation(out=r1, in_=r1, func=AF.Exp, scale=-1.0), [br1], [br1])
                        S.op("dve", lambda e: e.tensor_tensor(out=ot, in0=bk(Ob), in1=r1, op=ALU.mult), [pb[Ob], br1], [bOst[os_]])
                    else:
                        def lsum(e):
                            e.matmul(bk(L1), lhsT=C.ones, rhs=hi[0], start=True, stop=False)
                            return e.matmul(bk(L1), lhsT=C.ones, rhs=lo[0], start=False, stop=True)
                        S.op("pe", lsum, [bhl[0], C.bconst], [pb[L1]])
                        S.op("dve", lambda e: e.reciprocal(out=r1, in_=bk(L1)), [pb[L1]], [br1])
                        S.op("dve", lambda e: e.reciprocal(out=r2, in_=oe[2]), [boe[2]], [br2])
                        S.op("pool", lambda e: e.tensor_tensor(out=oe[0], in0=oe[0], in1=r1, op=ALU.mult), [boe[0], br1], [boe[0]])
                        S.op("pool", lambda e: e.tensor_tensor(out=oe[1], in0=oe[1], in1=r2, op=ALU.mult), [boe[1], br2], [boe[1]])
                        S.op("dve", lambda e: e.scalar_tensor_tensor(out=ot, in0=oe[1], scalar=C.neglam, in1=oe[0], op0=ALU.mult, op1=ALU.add), [boe[0], boe[1], C.bconst], [bOst[os_]])
                    S.dma("pool", "stO%d" % os_, Odram[h][:, qsl2], ot, reads=[bOst[os_]])
            pendq.append(pv_rec)
        while pendq:
            pendq.pop(0)()


def phase_C1(C, T):
    S, A, ps, pb = C.S, C.A, C.ps, C.pb
    NQ = C.NQ
    WPA = Wt(C, T.wpa, 8, 1024)
    WPB = Wt(C, T.wpb, 8, 1024)
    WO = Wt(C, T.wo, 8, 1024)
    WPA.load(C)
    WPB.load(C)
    WO.load(C)
    wpa, wpb, wo = WPA.ap, WPB.ap, WO.ap
    bwpa, bwpb, bwo = WPA.rd_all(), WPB.rd_all(), WO.rd_all()
    gpost = A.alloc(D, F32); bgpost = Buf()
    S.dma("sp", "ldg", gpost, T.g_post.partition_broadcast(128), writes=[bgpost])
    oa = [v3(A.alloc(8 * 512, BF16), 512) for _ in range(2)]; boa = bufs(2)
    ob = [v3(A.alloc(8 * 512, BF16), 512) for _ in range(2)]; bob = bufs(2)
    ga = [v3(A.alloc(8 * 512, BF16), 512) for _ in range(2)]; bga = bufs(2)
    gb = [v3(A.alloc(8 * 512, BF16), 512) for _ in range(2)]; bgb = bufs(2)
    oans = [v3(A.alloc(8 * 512, BF16), 512) for _ in range(2)]; boans = [bufs(8), bufs(8)]
    sq = [A.alloc(512, BF16) for _ in range(2)]; bsq = bufs(2)
    rsn = [A.alloc(512, F32) for _ in range(2)]; brsn = bufs(2)
    m1 = [A.alloc(512, F32) for _ in range(2)]; bm1 = bufs(2)
    m2 = [A.alloc(512, F32) for _ in range(2)]; bm2 = bufs(2)
    mT = v3(A.alloc(8 * 512, BF16), 512); bmT = bufs(8)
    xs = [A.alloc(D, F32) for _ in range(2)]; bxs = bufs(2)
    tt_ = [A.alloc(D, F32) for _ in range(2)]; btt = bufs(2)
    junk = A.alloc(D, BF16); bjunk = Buf()
    ss = A.alloc(8, F32); bss = bufs(8)
    C.mbanks = [0, 1, 2, 3]
    upairs = [(4, 5), (6, 7)]
    uc = 0
    xc = 0

    def loads(j):
        s = j % 2
        sl = slice(j * 512, (j + 1) * 512)
        S.dma("sp", "ldoa%d" % s, oa[s], T.OA.rearrange("h p s -> p h s")[:, :, sl], writes=[boa[s]])
        S.dma("sp", "ldob%d" % s, ob[s], T.OB.rearrange("h p s -> p h s")[:, :, sl], writes=[bob[s]])
        S.dma("sp", "ldga%d" % s, ga[s], T.GA.rearrange("h p s -> p h s")[:, :, sl], writes=[bga[s]])
        S.dma("sp", "ldgb%d" % s, gb[s], T.GB.rearrange("h p s -> p h s")[:, :, sl], writes=[bgb[s]])

    def subln(j):
        s = j % 2
        oan, boan = oans[s], boans[s]
        for h in range(H):
            i = h % 2
            src = oa[s][:, h, :]
            S.op("pool", lambda e, i=i, src=src: e.tensor_tensor(out=sq[i], in0=src, in1=src, op=ALU.mult), [boa[s]], [bsq[i]])
            b = nextbank(C)
            bank = ps[:, b * 512:(b + 1) * 512]
            S.op("pe", mm_acc(C, bank, [C.ones], [sq[i]], None), [C.bconst, bsq[i]], [pb[b]])
            S.op("act", lambda e, i=i, bank=bank: e.activation(out=rsn[i], in_=bank, func=AF.Ln, scale=1.0 / 128, bias=EPS), [pb[b]], [brsn[i]])
            S.op("act", lambda e, i=i: e.activation(out=rsn[i], in_=rsn[i], func=AF.Exp, scale=-0.5), [brsn[i]], [brsn[i]])
            dst = oan[:, h, :]
            S.op("dve", lambda e, i=i, src=src, dst=dst: e.scalar_tensor_tensor(out=dst, in0=src, scalar=C.gsub, in1=rsn[i], op0=ALU.mult, op1=ALU.mult), [boa[s], brsn[i], C.bconst], [boan[h]])

    loads(0)
    subln(0)
    for j in range(NQ):
        if j + 1 < NQ:
            loads(j + 1)
        s = j % 2
        oan, boan = oans[s], boans[s]
        for c in range(8):
            i = c % 2
            ba = nextbank(C)
            bka = ps[:, ba * 512:(ba + 1) * 512]
            S.op("pe", mm_acc(C, bka, [wpa[:, h, c * 128:(c + 1) * 128] for h in range(H)], [oan[:, h, :] for h in range(H)], None), bwpa + boan, [pb[ba]])
            bb = nextbank(C)
            bkb = ps[:, bb * 512:(bb + 1) * 512]
            S.op("pe", mm_acc(C, bkb, [wpb[:, h, c * 128:(c + 1) * 128] for h in range(H)], [ob[s][:, h, :] for h in range(H)], None), bwpb + [bob[s]], [pb[bb]])
            gac = ga[s][:, c, :]
            gbc = gb[s][:, c, :]
            S.op("dve", lambda e, i=i, bka=bka, gac=gac: e.tensor_tensor(out=m1[i], in0=bka, in1=gac, op=ALU.mult), [pb[ba], bga[s]], [bm1[i]])
            S.op("dve", lambda e, i=i, bkb=bkb, gbc=gbc: e.tensor_tensor(out=m2[i], in0=bkb, in1=gbc, op=ALU.mult), [pb[bb], bgb[s]], [bm2[i]])
            dst = mT[:, c, :]
            S.op("pool", lambda e, i=i, dst=dst: e.tensor_tensor(out=dst, in0=m1[i], in1=m2[i], op=ALU.add), [bm1[i], bm2[i]], [bmT[c]])
        if j + 1 < NQ:
            subln(j + 1)
        for tt in range(4):
            up = upairs[uc % 2]
            uc += 1
            xi = xc % 2
            xc += 1
            row0 = j * 512 + tt * 128
            S.dma("sp", "ldx%d" % xi, xs[xi], T.x_own[row0:row0 + 128, :], writes=[bxs[xi]])

            def mmu(e, up=up, tt=tt):
                for half in range(2):
                    bank = ps[:, up[half] * 512:(up[half] + 1) * 512]
                    for c in range(8):
                        last = e.matmul(bank, lhsT=mT[:, c, tt * 128:(tt + 1) * 128], rhs=wo[:, c, half * 512:(half + 1) * 512], start=(c == 0), stop=(c == 7))
                return last
            S.op("pe", mmu, bwo + bmT, [pb[up[0]], pb[up[1]]])
            u = ps[:, up[0] * 512:(up[0] + 2) * 512]
            si = (j * 4 + tt) % 8
            ssc = ss[:, si:si + 1]
            S.op("act", lambda e, u=u, ssc=ssc: e.activation(out=junk, in_=u, func=AF.Square, accum_out=ssc), [pb[up[0]], pb[up[1]]], [bjunk, bss[si]])
            S.op("act", lambda e, ssc=ssc: e.activation(out=ssc, in_=ssc, func=AF.Sqrt, scale=1.0 / D, bias=EPS), [bss[si]], [bss[si]])
            S.op("dve", lambda e, ssc=ssc: e.reciprocal(out=ssc, in_=ssc), [bss[si]], [bss[si]])
            t = tt_[xi]
            S.op("dve", lambda e, t=t, u=u, ssc=ssc: e.scalar_tensor_tensor(out=t, in0=u, scalar=ssc, in1=gpost, op0=ALU.mult, op1=ALU.mult), [pb[up[0]], pb[up[1]], bss[si], bgpost], [btt[xi]])
            x_ = xs[xi]
            S.op("pool", lambda e, t=t, x_=x_: e.tensor_tensor(out=t, in0=t, in1=x_, op=ALU.add), [btt[xi], bxs[xi]], [btt[xi]])
            S.dma("pool", "stx%d" % xi, T.out[row0:row0 + 128, :], t, reads=[btt[xi]])


def phase_C2(C, T):
    S, A, ps, pb = C.S, C.A, C.ps, C.pb
    NQ = C.NQ
    fsp = [(0, 768), (768, 1536), (1536, 2176), (2176, 2816)]
    WG = Wt(C, T.wg, 8, DFF, fsp)
    WU = Wt(C, T.wu, 8, DFF, fsp)
    WD = Wt(C, T.wd, NF, 1024)
    for g in range(4):
        WG.load(C, [g])
        WU.load(C, [g])
    WD.load(C)
    wg, wu, wd = WG.ap, WU.ap, WD.ap
    bwd = WD.rd_all()
    aT = v3(A.alloc(NF * 512, BF16), 512); baT = bufs(NF)
    gpost = A.alloc(D, F32); bgpost = Buf()
    S.dma("sp", "ldg", gpost, T.g_fpost.partition_broadcast(128), writes=[bgpost])
    x1 = [A.alloc(D, F32) for _ in range(4)]; bx1 = bufs(4)
    hb = [A.alloc(D, BF16) for _ in range(2)]; bhb = bufs(2)
    hT = A.alloc(8 * 512, BF16); bhT = Buf()
    hT3 = v3(hT, 512)
    junk = A.alloc(D, BF16); bjunk = Buf()
    ss = A.alloc(8, F32); bss = bufs(8)
    sg = [A.alloc(512, F32) for _ in range(2)]; bsg = bufs(2)
    ot = [A.alloc(D, F32) for _ in range(2)]; bot = bufs(2)
    tb = [0, 1]
    tc = 0
    gub = [(2, 3), (4, 5)]
    gc = 0
    dpairs = [(6, 7), (0, 1)]
    dc = 0
    oc = 0
    for j in range(NQ):
        for tt in range(4):
            row0 = j * 512 + tt * 128
            S.dma("sp", "ldx1_%d" % tt, x1[tt], T.out[row0:row0 + 128, :], writes=[bx1[tt]])
            si = tt
            ssc = ss[:, si:si + 1]
            xx = x1[tt]
            S.op("act", lambda e, xx=xx, ssc=ssc: e.activation(out=junk, in_=xx, func=AF.Square, accum_out=ssc), [bx1[tt]], [bjunk, bss[si]])
            S.op("act", lambda e, ssc=ssc: e.activation(out=ssc, in_=ssc, func=AF.Sqrt, scale=1.0 / D, bias=EPS), [bss[si]], [bss[si]])
            S.op("dve", lambda e, ssc=ssc: e.reciprocal(out=ssc, in_=ssc), [bss[si]], [bss[si]])
            hi = tt % 2
            hbi = hb[hi]
            S.op("dve", lambda e, hbi=hbi, xx=xx, ssc=ssc: e.tensor_scalar(out=hbi, in0=xx, scalar1=ssc, scalar2=None, op0=ALU.mult), [bx1[tt], bss[si]], [bhb[hi]])
            t_b = tb[tc % 2]
            tc += 1
            pst = ps[:, t_b * 512:(t_b + 1) * 512].bitcast(BF16)

            def tr(e, hbi=hbi, pst=pst):
                for kc in range(8):
                    last = e.transpose(pst[:, kc * 128:(kc + 1) * 128], hbi[:, kc * 128:(kc + 1) * 128], C.ident)
                return last
            S.op("pe", tr, [bhb[hi], C.bconst], [pb[t_b]])
            dstT = hT3[:, :, tt * 128:(tt + 1) * 128]
            S.op("dve", lambda e, dstT=dstT, pst=pst: e.tensor_tensor(out=dstT, in0=v3(pst, 128), in1=C.g_fpre.unsqueeze(2).to_broadcast([128, 8, 128]), op=ALU.mult), [pb[t_b], C.bconst], [bhT])
        hk = [hT3[:, kc, :] for kc in range(8)]
        for f in range(NF):
            gu = gub[gc % 2]
            gc += 1
            G = ps[:, gu[0] * 512:(gu[0] + 1) * 512]
            U = ps[:, gu[1] * 512:(gu[1] + 1) * 512]
            S.op("pe", mm_acc(C, G, [wg[:, kc, f * 128:(f + 1) * 128] for kc in range(8)], hk, None), WG.rd(f * 128) + [bhT], [pb[gu[0]]])
            S.op("pe", mm_acc(C, U, [wu[:, kc, f * 128:(f + 1) * 128] for kc in range(8)], hk, None), WU.rd(f * 128) + [bhT], [pb[gu[1]]])
            i = f % 2
            S.op("act", lambda e, i=i, G=G: e.activation(out=sg[i], in_=G, func=AF.Silu), [pb[gu[0]]], [bsg[i]])
            dst = aT[:, f, :]
            S.op("dve", lambda e, i=i, U=U, dst=dst: e.tensor_tensor(out=dst, in0=U, in1=sg[i], op=ALU.mult), [pb[gu[1]], bsg[i]], [baT[f]])
        for tt in range(4):
            dp = dpairs[dc % 2]
            dc += 1

            def mmd(e, dp=dp, tt=tt):
                for half in range(2):
                    bank = ps[:, dp[half] * 512:(dp[half] + 1) * 512]
                    for f in range(NF):
                        last = e.matmul(bank, lhsT=aT[:, f, tt * 128:(tt + 1) * 128], rhs=wd[:, f, half * 512:(half + 1) * 512], start=(f == 0), stop=(f == NF - 1))
                return last
            S.op("pe", mmd, bwd + baT, [pb[dp[0]], pb[dp[1]]])
            u = ps[:, dp[0] * 512:(dp[0] + 2) * 512] if dp[1] == dp[0] + 1 else None
            si = 4 + tt
            ssc = ss[:, si:si + 1]
            S.op("act", lambda e, u=u, ssc=ssc: e.activation(out=junk, in_=u, func=AF.Square, accum_out=ssc), [pb[dp[0]], pb[dp[1]]], [bjunk, bss[si]])
            S.op("act", lambda e, ssc=ssc: e.activation(out=ssc, in_=ssc, func=AF.Sqrt, scale=1.0 / D, bias=EPS), [bss[si]], [bss[si]])
            S.op("dve", lambda e, ssc=ssc: e.reciprocal(out=ssc, in_=ssc), [bss[si]], [bss[si]])
            oi = oc % 2
            oc += 1
            o_ = ot[oi]
            S.op("dve", lambda e, o_=o_, u=u, ssc=ssc: e.scalar_tensor_tensor(out=o_, in0=u, scalar=ssc, in1=gpost, op0=ALU.mult, op1=ALU.mult), [pb[dp[0]], pb[dp[1]], bss[si], bgpost], [bot[oi]])
            xx = x1[tt]
            S.op("pool", lambda e, o_=o_, xx=xx: e.tensor_tensor(out=o_, in0=o_, in1=xx, op=ALU.add), [bot[oi], bx1[tt]], [bot[oi]])
            row0 = j * 512 + tt * 128
            S.dma("pool", "sto%d" % oi, T.out[row0:row0 + 128, :], o_, reads=[bot[oi]])


def build(Sq, phases=("A1", "A2", "BD", "BM", "C1", "C2"), dbg=False):
    nc = bass.Bass("TRN2", target_bir_lowering=False)
    NB = Sq // 512
    NQ = NB // 2
    SQ = Sq // 2
    T = Ctx()

    def din(name, shape, dt=F32):
        return nc.dram_tensor(name, shape, dt, kind="ExternalInput").ap()

    def scr(name, shape, dt=BF16):
        return nc.dram_tensor(name, shape, dt, kind="ExternalOutput" if dbg else "Internal").ap()

    T.x_all = din("x_all", [Sq, D]); T.x_own = din("x_own", [SQ, D])
    T.pos_all = din("pos_all", [Sq], I32); T.pos_own = din("pos_own", [SQ], I32)
    T.w1 = din("w1", [D, 2368]); T.w2 = din("w2", [D, 3456])
    T.wuq = din("wuq", [384, 1536]); T.wukv = din("wukv", [256, 2048])
    T.wpa = din("wpa", [D, D]); T.wpb = din("wpb", [D, D]); T.wo = din("wo", [D, D])
    T.wg = din("wg", [D, DFF]); T.wu = din("wu", [D, DFF]); T.wd = din("wd", [DFF, D])
    T.cst = din("cst", [128, 32])
    T.permm = din("permm", [128, 128], BF16)
    T.lam4 = din("lam4", [4, 64])
    T.g_post = din("g_post", [D]); T.g_fpost = din("g_fpost", [D])
    T.maskA = din("maskA", [4, 128, 512], BF16); T.maskB = din("maskB", [4, 128, 512], BF16)
    T.out = nc.dram_tensor("out", [SQ, D], F32, kind="ExternalOutput").ap()
    T.KD = scr("KD", [H, 128, Sq]); T.VD = scr("VD", [H, 128, Sq])
    T.KN = scr("KN", [H, 128, Sq]); T.VB = scr("VB", [H, 128, Sq]); T.KR = scr("KR", [64, Sq])
    T.QD = scr("QD", [H, 128, SQ]); T.QN = scr("QN", [H, 128, SQ]); T.QR = scr("QR", [H, 64, SQ])
    T.GA = scr("GA", [H, 128, SQ]); T.GB = scr("GB", [H, 128, SQ])
    T.OA = scr("OA", [H, 128, SQ]); T.OB = scr("OB", [H, 128, SQ])

    C = Ctx()
    C.nc = nc
    C.S = Sched()
    C.A = Arena(nc, ARENA_BYTES)
    C.ps = nc.alloc_psum_tensor("ps", [128, 4096], F32)
    C.pb = [Buf(excl=True) for _ in range(8)]
    C.NB, C.NQ, C.S_, C.SQ = NB, NQ, Sq, SQ
    C.xctr = C.tctr = C.mctr = C.rctr = C.wctr = 0
    C.bout = Buf()
    C.posi = nc.alloc_sbuf_tensor("posi", [128, 512], I32)[:, :]
    S, A = C.S, C.A
    C.bconst = Buf()
    cst = A.alloc(32, F32)
    S.dma("sp", "cst", cst, T.cst, writes=[C.bconst])
    C.invf = cst[:, 0:1]
    C.sinsc = cst[:, 1:2]
    C.g_pre = cst[:, 4:12]
    C.g_qa = cst[:, 12:15]
    C.g_kva = cst[:, 15:17]
    C.g_fpre = cst[:, 17:25]
    small = A.alloc(8, F32)
    C.gsub = small[:, 0:1]
    C.neglam = small[:, 1:2]
    S.op("dve", lambda e: e.tensor_scalar(out=C.gsub, in0=cst[:, 2:3], scalar1=float(1.0 - LAMBDA_INIT), scalar2=None, op0=ALU.mult), [C.bconst], [C.bconst])
    lam = A.alloc(4 * 64, F32)
    S.dma("sp", "cst", lam, T.lam4.rearrange("a b -> (a b)").partition_broadcast(128), writes=[C.bconst])
    lp = A.alloc(128, F32)
    S.op("dve", lambda e: e.tensor_tensor(out=lp[:, 0:64], in0=lam[:, 0:64], in1=lam[:, 64:128], op=ALU.mult), [C.bconst], [C.bconst])
    S.op("dve", lambda e: e.tensor_tensor(out=lp[:, 64:128], in0=lam[:, 128:192], in1=lam[:, 192:256], op=ALU.mult), [C.bconst], [C.bconst])
    S.op("dve", lambda e: e.tensor_reduce(out=small[:, 2:3], in_=lp[:, 0:64], axis=mybir.AxisListType.X, op=ALU.add), [C.bconst], [C.bconst])
    S.op("dve", lambda e: e.tensor_reduce(out=small[:, 3:4], in_=lp[:, 64:128], axis=mybir.AxisListType.X, op=ALU.add), [C.bconst], [C.bconst])
    S.op("act", lambda e: e.activation(out=small[:, 4:6], in_=small[:, 2:4], func=AF.Exp), [C.bconst], [C.bconst])
    S.op("dve", lambda e: e.tensor_tensor(out=small[:, 6:7], in0=small[:, 5:6], in1=small[:, 4:5], op=ALU.subtract), [C.bconst], [C.bconst])
    S.op("dve", lambda e: e.tensor_single_scalar(out=C.neglam, in_=small[:, 6:7], scalar=-float(LAMBDA_INIT), op=ALU.add), [C.bconst], [C.bconst])
    C.perm = A.alloc(128, BF16)
    S.dma("sp", "cst", C.perm, T.permm, writes=[C.bconst])
    idf = A.alloc(128, F32)
    C.ident = A.alloc(128, BF16)
    C.ones = A.alloc(128, BF16)
    S.op("pool", lambda e: e.memset(idf, 1.0), [], [C.bconst])
    S.op("pool", lambda e: e.memset(C.ones, 1.0), [], [C.bconst])
    S.op("pool", lambda e: e.affine_select(out=idf, in_=idf, pattern=[[-1, 128]], compare_op=ALU.is_equal, fill=0.0, base=0, channel_multiplier=1), [C.bconst], [C.bconst])
    S.op("dve", lambda e: e.tensor_copy(out=C.ident, in_=idf), [C.bconst], [C.bconst])
    A.persist()
    S.barrier()
    for ph in phases:
        A.reset()
        if ph == "A1":
            phase_A1(C, T)
        elif ph == "A2":
            phase_A2(C, T)
        elif ph == "BD":
            attention(C, T, mla=False)
        elif ph == "BM":
            attention(C, T, mla=True)
        elif ph == "C1":
            phase_C1(C, T)
        elif ph == "C2":
            phase_C2(C, T)
        S.barrier()
    with ExitStack() as st:
        S.emit(nc, st)
    return nc


def _swap_idx(n_groups64):
    idx = []
    for g in range(n_groups64):
        b = g * 64
        idx += list(range(b + 32, b + 64)) + list(range(b, b + 32))
    return np.array(idx)


def host_prep(inputs, Sq):
    f32 = np.float32
    x = np.asarray(inputs["x"], f32)
    pos = np.asarray(inputs["positions"], np.int32)
    w_in = np.asarray(inputs["w_in"], f32)[0]
    qa, ka, va = w_in[:, 0:1024], w_in[:, 1024:2048], w_in[:, 2048:3072]
    cq, ckv, kr = w_in[:, 3072:3456], w_in[:, 3456:3712], w_in[:, 3712:3776]
    gA, gB = w_in[:, 3776:4800], w_in[:, 4800:5824]
    w1 = np.ascontiguousarray(np.concatenate([ka, va, ckv, kr], axis=1))
    w2 = np.ascontiguousarray(np.concatenate([qa, cq, gA, gB], axis=1))
    w_uq = np.asarray(inputs["w_uq"], f32)[0].reshape(384, 8, 192)
    uq_n = w_uq[:, :, 0:128].reshape(384, 1024)
    uq_r = w_uq[:, :, 128:192].reshape(384, 512)
    wuq = np.ascontiguousarray(np.concatenate([uq_n, uq_r], axis=1))
    w_ukv = np.asarray(inputs["w_ukv"], f32)[0].reshape(256, 8, 256)
    wukv = np.ascontiguousarray(np.concatenate([w_ukv[:, :, 0:128].reshape(256, 1024), w_ukv[:, :, 128:256].reshape(256, 1024)], axis=1))
    cst = np.zeros((128, 32), f32)
    invf = (1.0 / (f32(10000.0) ** (np.arange(32, dtype=f32) * f32(2.0 / 64)))).astype(f32)
    pidx = np.arange(128)
    cst[:, 0] = invf[pidx % 32]
    cst[:, 1] = np.where((pidx % 64) < 32, -1.0, 1.0)
    cst[:, 2] = np.asarray(inputs["da_subln"], f32)[0]
    cst[:, 4:12] = np.asarray(inputs["ln_mix_pre"], f32)[0].reshape(8, 128).T
    cst[:, 12:15] = np.asarray(inputs["q_a_norm"], f32)[0].reshape(3, 128).T
    cst[:, 15:17] = np.asarray(inputs["kv_a_norm"], f32)[0].reshape(2, 128).T
    cst[:, 17:25] = np.asarray(inputs["ln_ffn_pre"], f32)[0].reshape(8, 128).T
    lam4 = np.ascontiguousarray(np.stack([np.asarray(inputs[k], f32)[0] for k in ("lambda_q1", "lambda_k1", "lambda_q2", "lambda_k2")]))
    kk = np.arange(512)[:, None] // 64
    qq = np.arange(512)[None, :] // 64
    diag = np.where(kk <= qq, 0.0, NEG).astype(f32).reshape(4, 128, 512)
    zeros = np.zeros((4, 128, 512), f32)
    full = np.full((4, 128, 512), NEG, f32)
    bf = ml_dtypes.bfloat16
    bf = ml_dtypes.bfloat16
    permm = np.zeros((128, 128), f32)
    permm[_swap_idx(2), np.arange(128)] = 1.0
    shared = dict(
        permm=permm.astype(bf),
        w1=w1, w2=w2, wuq=wuq, wukv=wukv,
        wpa=np.ascontiguousarray(np.asarray(inputs["w_proj_a"], f32)[0]),
        wpb=np.ascontiguousarray(np.asarray(inputs["w_proj_b"], f32)[0]),
        wo=np.ascontiguousarray(np.asarray(inputs["w_o"], f32)[0]),
        wg=np.ascontiguousarray(np.asarray(inputs["w_ffn_gate"], f32)[0]),
        wu=np.ascontiguousarray(np.asarray(inputs["w_ffn_up"], f32)[0]),
        wd=np.ascontiguousarray(np.asarray(inputs["w_ffn_down"], f32)[0]),
        cst=cst, lam4=lam4,
        g_post=np.ascontiguousarray(np.asarray(inputs["ln_mix_post"], f32)[0]),
        g_fpost=np.ascontiguousarray(np.asarray(inputs["ln_ffn_post"], f32)[0]),
    )
    B = x.shape[0]
    NB = Sq // 512
    maps = []
    for c in range(2 * B):
        b, p = c // 2, c % 2
        m = dict(shared)
        m["x_all"] = np.ascontiguousarray(x[b])
        m["x_own"] = np.ascontiguousarray(x[b].reshape(NB, 512, D)[p::2].reshape(Sq // 2, D))
        m["pos_all"] = np.ascontiguousarray(pos[b])
        m["pos_own"] = np.ascontiguousarray(pos[b].reshape(NB, 512)[p::2].reshape(Sq // 2))
        m["maskA"] = (diag if p == 0 else zeros).astype(bf)
        m["maskB"] = (full if p == 0 else diag).astype(bf)
        maps.append(m)
    return maps


_NC_CACHE = {}


def kernel(**inputs):
    x = np.asarray(inputs["x"])
    B, Sq, _ = x.shape
    maps = host_prep(inputs, Sq)
    if Sq not in _NC_CACHE:
        _NC_CACHE[Sq] = build(Sq)
    nc = _NC_CACHE[Sq]
    res = run_bass_kernel_spmd(nc, maps, core_ids=list(range(2 * B)))
    NB = Sq // 512
    out = np.empty((B, NB, 512, D), np.float32)
    for c in range(2 * B):
        b, p = c // 2, c % 2
        out[b, p::2] = np.asarray(res.results[c]["out"], np.float32).reshape(NB // 2, 512, D)
    return out.reshape(B, Sq, D)
```
